# Optimizing a Trainium2 kernel written in Bass

```python
import math
import jax, jax.numpy as jnp
from jax import lax
import numpy as np

D_MODEL = 1024
BATCH = 16
SEQ = 2048
DEPTH = 1

CHUNK = 64
MIX_WIDTH = D_MODEL
POOL_WIDTH = MIX_WIDTH // 2
POOL_WINDOWS = (2, 4, 8, 16)
POOL_GROUPS = len(POOL_WINDOWS)
POOL_GROUP_DIM = POOL_WIDTH // POOL_GROUPS
DN_WIDTH = MIX_WIDTH - POOL_WIDTH
DN_HEADS = 4
DN_HEAD_DIM = DN_WIDTH // DN_HEADS
CONV_K = 4
IN_COLS = POOL_WIDTH + 4 * DN_WIDTH + 2 * DN_HEADS
MEM_LEN = 256
XA_HEADS = 4
XA_HEAD_DIM = D_MODEL // XA_HEADS
D_FF = 4 * D_MODEL
EPS = 1e-6

kernel_name = "hymba_pool_gdn_memxattn_layer"


def rmsnorm(x, g):
    xf = x.astype(jnp.float32)
    y = xf * lax.rsqrt(jnp.mean(xf * xf, axis=-1, keepdims=True) + EPS)
    return (y * g.astype(jnp.float32)).astype(x.dtype)


def l2norm(x):
    return x * lax.rsqrt(jnp.sum(x * x, axis=-1, keepdims=True) + EPS)


def pool_mixer(u, w_pool, pool_scale):
    B, S, _ = u.shape
    uf = u.astype(jnp.float32)
    csum = jnp.concatenate([jnp.zeros((B, 1, POOL_WIDTH), jnp.float32),
                            jnp.cumsum(uf, axis=1)], axis=1)
    t = jnp.arange(S)
    outs = []
    for gi, w in enumerate(POOL_WINDOWS):
        sl = slice(gi * POOL_GROUP_DIM, (gi + 1) * POOL_GROUP_DIM)
        c = csum[:, :, sl]
        lagged = jnp.concatenate([jnp.zeros((B, w - 1, POOL_GROUP_DIM), jnp.float32),
                                  c[:, :S - w + 1]], axis=1)
        cnt = jnp.minimum(t + 1, w).astype(jnp.float32)[None, :, None]
        outs.append((c[:, 1:] - lagged) / cnt - uf[:, :, sl])
    pooled = jnp.stack(outs, axis=2)
    mixed = jnp.einsum('bsgc,gcd->bsgd', pooled, w_pool.astype(jnp.float32))
    return (mixed.reshape(B, S, POOL_WIDTH) * pool_scale.astype(jnp.float32)).astype(u.dtype)


def causal_dwconv(x, w):
    K, C = w.shape
    return lax.conv_general_dilated(x, w[:, None, :], window_strides=(1,),
                                    padding=[(K - 1, 0)],
                                    dimension_numbers=('NWC', 'WIO', 'NWC'),
                                    feature_group_count=C)


def gated_delta_rule(q, k, v, g, beta):
    B, S, H, Dk = q.shape
    Dv = v.shape[-1]
    NC = S // CHUNK
    q = q * (Dk ** -0.5)

    def chunks(x):
        return x.reshape(B, NC, CHUNK, H, x.shape[-1]).transpose(0, 3, 1, 2, 4)

    qc, kc, vc = chunks(q), chunks(k), chunks(v)
    gc = jnp.cumsum(g.reshape(B, NC, CHUNK, H).transpose(0, 3, 1, 2), axis=-1)
    bc = beta.reshape(B, NC, CHUNK, H).transpose(0, 3, 1, 2)[..., None]

    tril = jnp.tril(jnp.ones((CHUNK, CHUNK), dtype=bool))
    strict = jnp.tril(jnp.ones((CHUNK, CHUNK), dtype=bool), k=-1)
    diff = gc[..., :, None] - gc[..., None, :]
    decay = jnp.where(tril, jnp.exp(jnp.where(tril, diff, 0.0)), 0.0)

    k_beta = kc * bc
    v_beta = vc * bc
    L = jnp.where(strict, jnp.einsum('bhnid,bhnjd->bhnij', k_beta, kc) * decay, 0.0)
    eye = jnp.eye(CHUNK, dtype=jnp.float32)
    T = lax.linalg.triangular_solve(eye + L, jnp.broadcast_to(eye, L.shape),
                                    left_side=True, lower=True)
    u = jnp.einsum('bhnij,bhnjd->bhnid', T, v_beta)
    w = jnp.einsum('bhnij,bhnjd->bhnid', T, k_beta * jnp.exp(gc)[..., None])
    attn = jnp.where(tril, jnp.einsum('bhnid,bhnjd->bhnij', qc, kc) * decay, 0.0)

    def step(state, inp):
        q_c, k_c, u_c, w_c, g_c, a_c = inp
        v_new = u_c - jnp.einsum('bhcd,bhde->bhce', w_c, state)
        o = (jnp.einsum('bhcd,bhde->bhce', q_c * jnp.exp(g_c)[..., None], state)
             + jnp.einsum('bhij,bhje->bhie', a_c, v_new))
        g_last = g_c[..., -1]
        k_dec = k_c * jnp.exp(g_last[..., None] - g_c)[..., None]
        state = state * jnp.exp(g_last)[..., None, None] + jnp.einsum('bhcd,bhce->bhde', k_dec, v_new)
        return state, o

    xs = (jnp.moveaxis(qc, 2, 0), jnp.moveaxis(kc, 2, 0), jnp.moveaxis(u, 2, 0),
          jnp.moveaxis(w, 2, 0), jnp.moveaxis(gc, 2, 0), jnp.moveaxis(attn, 2, 0))
    state0 = jnp.zeros((B, H, Dk, Dv), jnp.float32)
    _, o = lax.scan(step, state0, xs)
    return o.transpose(1, 0, 3, 2, 4).reshape(B, S, H, Dv)


def deltanet_mixer(q, k, v, z, b, a, conv_w, a_log, dt_bias, o_norm_g):
    B, S, _ = q.shape
    dtype = q.dtype
    qkv = jnp.concatenate([q, k, v], axis=-1).astype(jnp.float32)
    qkv = jax.nn.silu(causal_dwconv(qkv, conv_w.astype(jnp.float32)))
    qf, kf, vf = jnp.split(qkv, 3, axis=-1)
    qf = l2norm(qf.reshape(B, S, DN_HEADS, DN_HEAD_DIM))
    kf = l2norm(kf.reshape(B, S, DN_HEADS, DN_HEAD_DIM))
    vf = vf.reshape(B, S, DN_HEADS, DN_HEAD_DIM)
    beta = jax.nn.sigmoid(b.astype(jnp.float32))
    g = -jnp.exp(a_log.astype(jnp.float32)) * jax.nn.softplus(a.astype(jnp.float32) + dt_bias.astype(jnp.float32))
    o = gated_delta_rule(qf, kf, vf, g, beta)
    o = o * lax.rsqrt(jnp.mean(o * o, axis=-1, keepdims=True) + EPS) * o_norm_g.astype(jnp.float32)
    o = o * jax.nn.silu(z.astype(jnp.float32).reshape(B, S, DN_HEADS, DN_HEAD_DIM))
    return o.reshape(B, S, DN_WIDTH).astype(dtype)


def memory_cross_attention(h, m, w_q, w_k, w_v, w_o):
    B, S, _ = h.shape
    M = m.shape[1]
    q = (h @ w_q).reshape(B, S, XA_HEADS, XA_HEAD_DIM)
    k = (m @ w_k).reshape(B, M, XA_HEADS, XA_HEAD_DIM)
    v = (m @ w_v).reshape(B, M, XA_HEADS, XA_HEAD_DIM)
    s = jnp.einsum('bshd,bmhd->bhsm', q, k).astype(jnp.float32) * (XA_HEAD_DIM ** -0.5)
    p = jax.nn.softmax(s, axis=-1).astype(v.dtype)
    o = jnp.einsum('bhsm,bmhd->bshd', p, v).reshape(B, S, D_MODEL)
    return o @ w_o


def sqrelu_mlp(h, w_up, w_down):
    a = jax.nn.relu(h @ w_up)
    return (a * a) @ w_down


def setup_inputs(seed: int = 0) -> dict:
    key = jax.random.key(seed)
    ks = jax.random.split(key, 24)

    def nrm(k, shape, scale):
        return jax.random.normal(k, shape, jnp.float32) * scale

    def gain(k, shape):
        return 1.0 + 0.05 * jax.random.normal(k, shape, jnp.float32)

    L = DEPTH
    return {
        "x": nrm(ks[0], (BATCH, SEQ, D_MODEL), 1.0),
        "mem": nrm(ks[1], (BATCH, MEM_LEN, D_MODEL), 1.0),
        "norm_mix_g": gain(ks[2], (L, D_MODEL)),
        "w_in": nrm(ks[3], (L, D_MODEL, IN_COLS), D_MODEL ** -0.5),
        "w_pool": nrm(ks[4], (L, POOL_GROUPS, POOL_GROUP_DIM, POOL_GROUP_DIM), POOL_GROUP_DIM ** -0.5),
        "pool_scale": gain(ks[5], (L, POOL_WIDTH)),
        "conv_w": nrm(ks[6], (L, CONV_K, 3 * DN_WIDTH), CONV_K ** -0.5),
        "a_log": jnp.log(jax.random.uniform(ks[7], (L, DN_HEADS), jnp.float32, 1.0, 16.0)),
        "dt_bias": 0.5 + 0.1 * jax.random.normal(ks[8], (L, DN_HEADS), jnp.float32),
        "dn_out_norm_g": gain(ks[9], (L, DN_HEAD_DIM)),
        "w_out": nrm(ks[10], (L, MIX_WIDTH, D_MODEL), MIX_WIDTH ** -0.5),
        "norm_xattn_g": gain(ks[11], (L, D_MODEL)),
        "mem_norm_g": gain(ks[12], (L, D_MODEL)),
        "w_xq": nrm(ks[13], (L, D_MODEL, D_MODEL), D_MODEL ** -0.5),
        "w_xk": nrm(ks[14], (L, D_MODEL, D_MODEL), D_MODEL ** -0.5),
        "w_xv": nrm(ks[15], (L, D_MODEL, D_MODEL), D_MODEL ** -0.5),
        "w_xo": nrm(ks[16], (L, D_MODEL, D_MODEL), D_MODEL ** -0.5),
        "norm_mlp_g": gain(ks[17], (L, D_MODEL)),
        "w_up": nrm(ks[18], (L, D_MODEL, D_FF), D_MODEL ** -0.5),
        "w_down": nrm(ks[19], (L, D_FF, D_MODEL), D_FF ** -0.5),
        "final_norm_g": gain(ks[20], (D_MODEL,)),
    }


def reference(x, mem, norm_mix_g, w_in, w_pool, pool_scale, conv_w, a_log, dt_bias,
              dn_out_norm_g, w_out, norm_xattn_g, mem_norm_g, w_xq, w_xk, w_xv, w_xo,
              norm_mlp_g, w_up, w_down, final_norm_g):
    P, W, H = POOL_WIDTH, DN_WIDTH, DN_HEADS
    h = x
    for l in range(DEPTH):
        hn = rmsnorm(h, norm_mix_g[l])
        proj = hn @ w_in[l]
        u_pool = proj[..., :P]
        q = proj[..., P:P + W]
        k = proj[..., P + W:P + 2 * W]
        v = proj[..., P + 2 * W:P + 3 * W]
        z = proj[..., P + 3 * W:P + 4 * W]
        b = proj[..., P + 4 * W:P + 4 * W + H]
        a = proj[..., P + 4 * W + H:]
        y_pool = pool_mixer(u_pool, w_pool[l], pool_scale[l])
        y_dn = deltanet_mixer(q, k, v, z, b, a, conv_w[l], a_log[l], dt_bias[l], dn_out_norm_g[l])
        h = h + jnp.concatenate([y_pool, y_dn], axis=-1) @ w_out[l]
        m = rmsnorm(mem, mem_norm_g[l])
        h = h + memory_cross_attention(rmsnorm(h, norm_xattn_g[l]), m, w_xq[l], w_xk[l], w_xv[l], w_xo[l])
        h = h + sqrelu_mlp(rmsnorm(h, norm_mlp_g[l]), w_up[l], w_down[l])
    return rmsnorm(h, final_norm_g)
```

```python
import numpy as np
from contextlib import ExitStack
import concourse.bass as bass
import concourse.mybir as mybir
from concourse.bass_utils import run_bass_kernel_spmd

F32 = mybir.dt.float32
BF16 = mybir.dt.bfloat16
AF = mybir.ActivationFunctionType
ALU = mybir.AluOpType
AX = mybir.AxisListType

D = 1024
SEQ = 2048
NSEQ = 2
MEM = 256
INC = 2568
DFF = 4096
EPS = 1e-6
NHALF = 2
CPH = 16
TPH = CPH * 64
WINDOWS = (2, 4, 8, 16)

P_GMIX, P_GXAT, P_GMEM, P_GMLP, P_GFIN = 0, 8, 16, 24, 32
P_PSC = 40
P_CONV = 44
P_ALOG = 92
P_DTB = 96
P_GO = 100
PW = 228
C_ID, C_ONE, C_MTRI, C_NMTRI, C_MBLK, C_MSN, C_MI, C_RAT = 0, 128, 256, 384, 512, 640, 768, 896
CW = 960


class Prog:
    def __init__(self, nc, es):
        self.nc = nc
        self.es = es
        self.ops = []
        self.engs = {'pe': nc.tensor, 'act': nc.scalar, 'dve': nc.vector, 'pool': nc.gpsimd, 'sp': nc.sync}
        self.token = None

    def op(self, engine, fn, reads=(), writes=(), tok=True):
        r = tuple(reads)
        if tok and self.token is not None:
            r = r + (self.token,)
        self.ops.append(dict(e=engine, fn=fn, r=r, w=tuple(writes), dma=None))

    def dma(self, queue, fn, reads=(), writes=(), group=None, tok=True):
        r = tuple(reads)
        if tok and self.token is not None:
            r = r + (self.token,)
        self.ops.append(dict(e=queue, fn=fn, r=r, w=tuple(writes), dma=group))

    def barrier(self, fn):
        self.ops.append(dict(e='dve', fn=fn, r=(), w=(self.token,), dma=None))

    def finalize(self, final_groups=()):
        nc = self.nc
        ops = self.ops
        last_writer = {}
        readers = {}
        for i, o in enumerate(ops):
            deps = set()
            for r in o['r']:
                if r in last_writer:
                    deps.add(last_writer[r])
            for w in o['w']:
                if w in last_writer:
                    deps.add(last_writer[w])
                deps.update(readers.get(w, ()))
            deps.discard(i)
            best = {}
            for d in deps:
                od = ops[d]
                key = ('g', od['dma']) if od['dma'] is not None else ('e', od['e'])
                if key not in best or best[key] < d:
                    best[key] = d
            o['deps'] = set(best.values())
            for r in o['r']:
                readers.setdefault(r, []).append(i)
            for w in o['w']:
                last_writer[w] = i
                readers[w] = []

        def skip(od, o):
            return od['dma'] is None and od['e'] == 'pe' and o['e'] == 'pe' and o['dma'] is None

        need = [False] * len(ops)
        for i, o in enumerate(ops):
            for d in o['deps']:
                if not skip(ops[d], o):
                    need[d] = True
        sems = {e: self.es.enter_context(nc.semaphore('sem_' + e)) for e in self.engs}
        gsem = {}
        gcount = {}
        cnt = {e: 0 for e in self.engs}
        for i, o in enumerate(ops):
            if o['dma'] is not None:
                g = o['dma']
                if g not in gsem:
                    gsem[g] = self.es.enter_context(nc.semaphore('dg_' + g))
                    gcount[g] = 0
                gcount[g] += 16
                o['sig'] = (gsem[g], gcount[g])
            elif need[i]:
                cnt[o['e']] += 1
                o['sig'] = (sems[o['e']], cnt[o['e']])
            else:
                o['sig'] = None
            o['gsnap'] = None
        running = {}
        snaps = []
        for i, o in enumerate(ops):
            snaps.append(dict(running))
            if o['dma'] is not None:
                running[o['dma']] = o['sig'][1]
        waited = {e: {} for e in self.engs}
        nwaits = 0
        for i, o in enumerate(ops):
            e = o['e']
            eng = self.engs[e]
            wl = {}
            for d in o['deps']:
                od = ops[d]
                if skip(od, o):
                    continue
                s, v = od['sig']
                if od['dma'] is not None:
                    v = snaps[i][od['dma']]
                key = id(s)
                if key not in wl or wl[key][1] < v:
                    wl[key] = (s, v)
            for key, (s, v) in wl.items():
                if waited[e].get(key, 0) >= v:
                    continue
                waited[e][key] = v
                eng.wait_ge(s, v)
                nwaits += 1
            inst = o['fn'](eng)
            if o['dma'] is not None:
                inst.then_inc(o['sig'][0], 16)
            elif o['sig'] is not None:
                inst.then_inc(o['sig'][0], 1)
        for g in final_groups:
            nc.sync.wait_ge(gsem[g], gcount[g])
        self.stats = dict(n_ops=len(ops), n_waits=nwaits, sig=cnt, ngroups=len(gsem))
        return self.stats


def bcast(ap, pos, n):
    lst = [list(x) for x in ap.ap]
    lst.insert(pos, [0, n])
    return bass.AP(ap.tensor, ap.offset, lst)


def build_nc(stop_after=None, dbg=None, max_ops=None):
    nc = bass.Bass("TRN2", target_bir_lowering=False)

    def dr(name, shape, kind="ExternalInput", dt=F32):
        return nc.dram_tensor(name, list(shape), dt, kind=kind).ap()

    x_d = dr("x", [NSEQ, SEQ, D])
    mem_d = dr("mem", [NSEQ, MEM, D])
    win_d = dr("w_in", [D, INC])
    wpool_d = dr("w_pool", [4, 128, 128])
    wout_d = dr("w_out", [D, D])
    wxq_d = dr("w_xq", [D, D])
    wxk_d = dr("w_xk", [D, D])
    wxv_d = dr("w_xv", [D, D])
    wxo_d = dr("w_xo", [D, D])
    wup_d = dr("w_up", [D, DFF])
    wdn_d = dr("w_down", [DFF, D])
    prm_d = dr("prm", [128, PW])
    cst_d = dr("cst", [128, CW])
    out_d = dr("out", [NSEQ, SEQ, D], kind="ExternalOutput")
    dbg_d = {}
    if dbg:
        for k, (shp, dtn) in dbg.items():
            dbg_d[k] = dr("dbg_" + k, shp, kind="ExternalOutput", dt=(BF16 if dtn == 'bf16' else F32))

    es = ExitStack()
    with es:
        P = Prog(nc, es)

        uid = [0]

        def sbuf(stack, name, shape, dt):
            uid[0] += 1
            return stack.enter_context(nc.sbuf_tensor(f"sb{uid[0]}_{name}", list(shape), dt))

        hT = sbuf(es, "hT", [128, 8, NSEQ, TPH], F32)
        cst = sbuf(es, "cst", [128, CW], F32)
        prm = sbuf(es, "prm", [128, PW], F32)
        identb = sbuf(es, "identb", [128, 128], BF16)
        onesb = sbuf(es, "onesb", [128, 128], BF16)
        convd = sbuf(es, "convd", [128, 12, 4, 128], BF16)
        WA = sbuf(es, "warena", [128, 29312], BF16)
        S32 = sbuf(es, "S32", [128, 4, 256], F32)
        Sbf = sbuf(es, "Sbf", [128, 4, 256], BF16)
        qkvh = sbuf(es, "qkvh", [128, 12, 2, 3], BF16)
        uph = sbuf(es, "uph", [128, 4, 2, 16], F32)
        negA = sbuf(es, "negA", [128, 4], F32)
        dummy = sbuf(es, "dummy", [128, 2], F32)
        PS = es.enter_context(nc.psum_tensor("PS", [128, 8, 512], F32))

        ident = cst[:, C_ID:C_ID + 128]
        ones = cst[:, C_ONE:C_ONE + 128]
        mtri = cst[:, C_MTRI:C_MTRI + 128]
        nmtri = cst[:, C_NMTRI:C_NMTRI + 128]
        mblk = cst[:, C_MBLK:C_MBLK + 128]
        msn = cst[:, C_MSN:C_MSN + 128]
        mi = cst[:, C_MI:C_MI + 128]
        rat = cst[:, C_RAT:C_RAT + 64]

        def bank(b):
            return PS[:, b, :]

        def bank4(b):
            return PS[:, b, :].rearrange("p (j n) -> p j n", j=4)

        def bank8(b):
            return PS[:, b:b + 2, :].rearrange("p b (j n) -> p (b j) n", j=4)

        def bankb(b):
            return PS[:, b, :].bitcast(BF16).rearrange("p (j n) -> p j n", j=8)

        def psn(*bs):
            return [f"ps{b}" for b in bs]

        def wa_regs(lo, hi):
            return [f"wa{b}" for b in range(lo // 2048, (hi - 1) // 2048 + 1)]

        def hreg(s, t0, t1):
            return [f"h{s}_{c}" for c in range(t0 // 64, (t1 - 1) // 64 + 1)]

        dbg_n = [0]

        def dump(name, ap, reads):
            if name in dbg_d:
                dbg_n[0] += 1
                P.dma('sp', lambda e: e.dma_start(out=dbg_d[name], in_=ap), reads=reads, writes=['dbg_' + name],
                      group=f'dbg{dbg_n[0]}')

        P.dma('sp', lambda e: e.dma_start(out=cst[:], in_=cst_d[:, :]), writes=['cst'], group='c0')
        P.dma('sp', lambda e: e.dma_start(out=prm[:], in_=prm_d[:, :]), writes=['prm'], group='c1')
        P.op('dve', lambda e: e.tensor_copy(identb[:], ident), reads=['cst'], writes=['identb'])
        P.op('dve', lambda e: e.tensor_copy(onesb[:], ones), reads=['cst'], writes=['onesb'])
        for b in range(12):
            for k in range(4):
                col = P_CONV + b * 4 + k
                P.op('dve', lambda e, b=b, k=k, col=col: e.tensor_scalar_mul(convd[:, b, k, :], ident, prm[:, col:col + 1]),
                     reads=['cst', 'prm'], writes=['convd'])
        P.op('act', lambda e: e.activation(negA[:], prm[:, P_ALOG:P_ALOG + 4], AF.Exp), reads=['prm'], writes=['negA'])
        P.op('dve', lambda e: e.tensor_scalar_mul(negA[:], negA[:], -1.0), reads=['negA'], writes=['negA'])
        P.op('dve', lambda e: e.memset(S32[:], 0.0), writes=['S32'])
        P.op('dve', lambda e: e.memset(Sbf[:], 0.0), writes=['Sbf'])
        P.op('dve', lambda e: e.memset(qkvh[:], 0.0), writes=['qkvh'])
        P.op('dve', lambda e: e.memset(uph[:], 0.0), writes=['uph'])

        WIN0 = 0
        WOUT0 = 8 * INC
        WPOOL0 = WOUT0 + 8 * D
        winv = win_d.rearrange("(k p) n -> p k n", p=128)

        def load_w1024(dram, base, gname):
            v = dram.rearrange("(k p) n -> p k n", p=128)
            for k in range(8):
                lo = base + k * 1024
                P.dma('pool', lambda e, k=k, lo=lo: e.dma_start(out=WA[:, lo:lo + 1024], in_=v[:, k, :]),
                      writes=wa_regs(lo, lo + 1024), group=gname, tok=False)

        def load_p1_weights():
            for k in range(8):
                lo = WIN0 + k * INC
                P.dma('pool', lambda e, k=k, lo=lo: e.dma_start(out=WA[:, lo:lo + INC], in_=winv[:, k, :],
                                                                 max_dma_last_dim=4096),
                      writes=wa_regs(lo, lo + INC), group='g_win', tok=False)
            load_w1024(wout_d, WOUT0, 'g_wout')
            P.dma('pool', lambda e: e.dma_start(out=WA[:, WPOOL0:WPOOL0 + 512].rearrange("p (g d) -> p g d", g=4),
                                                in_=wpool_d.rearrange("g c d -> c g d")),
                  writes=wa_regs(WPOOL0, WPOOL0 + 512), group='g_wpool', tok=False)

        def Win(k, lo, hi):
            return WA[:, WIN0 + k * INC + lo:WIN0 + k * INC + hi]

        def Win_regs(k, lo, hi):
            return wa_regs(WIN0 + k * INC + lo, WIN0 + k * INC + hi)

        def W1024(base, k, lo, hi):
            return WA[:, base + k * 1024 + lo:base + k * 1024 + hi]

        def W1024_regs(base, k, lo, hi):
            return wa_regs(base + k * 1024 + lo, base + k * 1024 + hi)

        def phase_barrier(name):
            P.barrier(lambda e: e.memset(dummy[:], 0.0))

        P.token = 'tok'

        def rms_rows(stack_bufs, src_ps_or_sb, n, gcol, out_hn, reads, ps_stat_bank, sq, rstd, src_k):
            raise NotImplementedError

        def phase1(half):
            load_p1_weights()
            with ExitStack() as p1:
                xt = sbuf(p1, "xt", [128, D], F32)
                sq = sbuf(p1, "sq", [128, 8, 128], F32)
                rstd = sbuf(p1, "rstd", [128, 128], F32)
                hn = sbuf(p1, "hn", [128, 8, 128], BF16)
                qkvb = sbuf(p1, "qkvb", [128, 12, 2, 67], BF16)
                upb = sbuf(p1, "upb", [128, 4, 2, 80], F32)
                wsb = sbuf(p1, "wsb", [128, 4, 2, 80], F32)
                wsc = sbuf(p1, "wsc", [128, 4, 2, 80], F32)
                pooled = sbuf(p1, "pooled", [128, 4, 2, 64], BF16)
                catT = sbuf(p1, "catT", [128, 8, 128], BF16)
                qks = sbuf(p1, "qks", [128, 8, 128], F32)
                vT = sbuf(p1, "vT", [128, 4, 128], BF16)
                kq = sbuf(p1, "kq", [128, 4, 2, 128], BF16)
                kT = sbuf(p1, "kT", [128, 4, 128], BF16)
                sm = sbuf(p1, "sm", [128, 64], F32)
                gB = sbuf(p1, "gB", [128, 4, 128], F32)
                dexp = sbuf(p1, "dexp", [128, 4, 128], F32)
                E2 = xt[:].rearrange("p (h a n) -> p h a n", h=4, a=2)
                egrow = sbuf(p1, "egrow", [128, 4, 128], F32)
                qg = sbuf(p1, "qg", [128, 4, 128], BF16)
                LA = sbuf(p1, "LA", [128, 4, 2, 128], BF16)
                N0 = sbuf(p1, "N0", [128, 4, 128], BF16)
                NAa = sbuf(p1, "NAa", [128, 4, 2, 128], BF16)
                NAb = sbuf(p1, "NAb", [128, 4, 2, 128], BF16)
                PTa = sbuf(p1, "PTa", [128, 4, 128], BF16)
                PTb = sbuf(p1, "PTb", [128, 4, 128], BF16)
                negwT = sbuf(p1, "negwT", [128, 4, 128], BF16)
                vbblk = sbuf(p1, "vbblk", [128, 4, 256], BF16)
                kbg = sbuf(p1, "kbg", [128, 4, 128], BF16)
                kdec = sbuf(p1, "kdec", [128, 4, 128], BF16)
                vnblk = sbuf(p1, "vnblk", [128, 4, 256], BF16)
                zt = sbuf(p1, "zt", [128, 4, 128], BF16)
                ydn = sbuf(p1, "ydn", [128, 4, 128], BF16)
                otok = qks[:, 0:4, :]
                osq = dexp

                S_BA, S_BETA, S_G, S_T1, S_T2, S_GC, S_GL, S_SC1, S_SC2, S_OSS, S_ORS = 0, 8, 12, 16, 20, 24, 28, 32, 36, 40, 44

                P.op('dve', lambda e: e.memset(vbblk[:], 0.0), writes=['vbblk'])
                P.op('dve', lambda e: e.memset(sm[:], 0.0), writes=['sm', 'sm_o', 'sm_ba'])
                P.op('dve', lambda e: e.memset(vnblk[:], 0.0), writes=['vnblk'])

                for c in range(CPH):
                    cg = half * CPH + c
                    t0 = c * 64
                    hregs = [f"h0_{c}", f"h1_{c}"]
                    hview = hT[:, :, :, t0:t0 + 64]
                    for s in range(NSEQ):
                        P.dma('sp', lambda e, s=s, cg=cg: e.dma_start(out=xt[s * 64:(s + 1) * 64, :],
                                                                      in_=x_d[s, cg * 64:(cg + 1) * 64, :]),
                              writes=['xt'], group='g_x')
                    pT = bank8(0)
                    for k in range(8):
                        P.op('pe', lambda e, k=k: e.transpose(pT[:, k, :], xt[:, k * 128:(k + 1) * 128], ident),
                             reads=['xt', 'cst'], writes=psn(0, 1))
                    P.op('act', lambda e, hview=hview: e.copy(hview, pT.rearrange("p k (s t) -> p k s t", s=2)),
                         reads=psn(0, 1), writes=hregs)
                    P.op('act', lambda e: e.activation(sq[:], pT, AF.Square), reads=psn(0, 1), writes=['sq'])
                    pst = bank(2)[:, 0:128]
                    for k in range(8):
                        P.op('pe', lambda e, k=k: e.matmul(pst, ones, sq[:, k, :], start=(k == 0), stop=(k == 7)),
                             reads=['sq', 'cst'], writes=psn(2))
                    P.op('act', lambda e: e.activation(rstd[:], pst, AF.Ln, bias=EPS, scale=1.0 / D),
                         reads=psn(2), writes=['rstd'])
                    P.op('act', lambda e: e.activation(rstd[:], rstd[:], AF.Exp, scale=-0.5), reads=['rstd'], writes=['rstd'])
                    for k in range(8):
                        P.op('dve', lambda e, k=k: e.scalar_tensor_tensor(hn[:, k, :], pT[:, k, :],
                                                                          prm[:, P_GMIX + k:P_GMIX + k + 1], rstd[:],
                                                                          ALU.mult, ALU.mult),
                             reads=psn(0, 1) + ['prm', 'rstd'], writes=['hn'])
                    for grp in range(4):
                        pb = bank4(3 + grp)
                        for j in range(4):
                            col = (grp * 4 + j) * 128
                            for k in range(8):
                                P.op('pe', lambda e, pb=pb, j=j, k=k, col=col: e.matmul(
                                    pb[:, j, :], Win(k, col, col + 128), hn[:, k, :], start=(k == 0), stop=(k == 7)),
                                     reads=['hn'] + Win_regs(k, col, col + 128), writes=psn(3 + grp))
                    for k in range(8):
                        P.op('pe', lambda e, k=k: e.matmul(bank(7), hn[:, k, :], Win(k, 2048, 2560),
                                                           start=(k == 0), stop=(k == 7)),
                             reads=['hn'] + Win_regs(k, 2048, 2560), writes=psn(7))
                    pba = bank(2)[:, 128:136]
                    for k in range(8):
                        P.op('pe', lambda e, k=k: e.matmul(pba, hn[:, k, :], Win(k, 2560, 2568),
                                                           start=(k == 0), stop=(k == 7)),
                             reads=['hn'] + Win_regs(k, 2560, 2568), writes=psn(2))
                    P.op('pool', lambda e: e.tensor_copy(upb[:, :, :, 0:16], uph[:]), reads=['uph'], writes=['upb_h'])
                    P.op('pool', lambda e: e.tensor_copy(qkvb[:, :, :, 0:3], qkvh[:]), reads=['qkvh'], writes=['qkvb_h'])
                    P.op('act', lambda e: e.copy(upb[:, :, :, 16:80], bank4(3).rearrange("p g (s t) -> p g s t", s=2)),
                         reads=psn(3), writes=['upb_c'])
                    for j in range(3):
                        eng = 'dve' if j != 1 else 'act'
                        if eng == 'dve':
                            P.op('dve', lambda e, j=j: e.tensor_copy(qkvb[:, 4 * j:4 * j + 4, :, 3:67],
                                                                     bank4(4 + j).rearrange("p g (s t) -> p g s t", s=2)),
                                 reads=psn(4 + j), writes=[f'qkvb_c{j}'])
                        else:
                            P.op('act', lambda e, j=j: e.copy(qkvb[:, 4 * j:4 * j + 4, :, 3:67],
                                                              bank4(4 + j).rearrange("p g (s t) -> p g s t", s=2)),
                                 reads=psn(4 + j), writes=[f'qkvb_c{j}'])
                    P.op('act', lambda e: e.activation(zt[:], bank4(7), AF.Silu), reads=psn(7), writes=['zt'])
                    P.op('dve', lambda e: e.tensor_copy(sm[:, S_BA:S_BA + 8], pba), reads=psn(2), writes=['sm_ba'])
                    P.op('pool', lambda e: e.tensor_copy(uph[:], upb[:, :, :, 64:80]), reads=['upb_c'], writes=['uph'])
                    P.op('pool', lambda e: e.tensor_copy(qkvh[:], qkvb[:, :, :, 64:67]),
                         reads=['qkvb_c0', 'qkvb_c1', 'qkvb_c2'], writes=['qkvh'])

                    U = ['upb_h', 'upb_c']
                    P.op('pool', lambda e: e.tensor_tensor(wsb[:, :, :, 1:80], upb[:, :, :, 1:80], upb[:, :, :, 0:79], ALU.add),
                         reads=U, writes=['wsb'])
                    P.op('pool', lambda e: e.tensor_tensor(wsc[:, 1:4, :, 3:80], wsb[:, 1:4, :, 3:80], wsb[:, 1:4, :, 1:78], ALU.add),
                         reads=['wsb'], writes=['wsc'])
                    P.op('pool', lambda e: e.tensor_tensor(wsb[:, 2:4, :, 7:80], wsc[:, 2:4, :, 7:80], wsc[:, 2:4, :, 3:76], ALU.add),
                         reads=['wsc'], writes=['wsb'])
                    P.op('pool', lambda e: e.tensor_tensor(wsc[:, 3:4, :, 15:80], wsb[:, 3:4, :, 15:80], wsb[:, 3:4, :, 7:72], ALU.add),
                         reads=['wsb'], writes=['wsc'])
                    fin = [wsb, wsc, wsb, wsc]
                    for g in range(4):
                        P.op('pool', lambda e, g=g: e.tensor_scalar_mul(fin[g][:, g, :, 16:80], fin[g][:, g, :, 16:80],
                                                                        1.0 / WINDOWS[g]),
                             reads=['wsb', 'wsc'], writes=['wsb', 'wsc'])
                        if cg == 0:
                            P.op('pool', lambda e, g=g: e.tensor_tensor(fin[g][:, g, :, 16:32], fin[g][:, g, :, 16:32],
                                                                        bcast(rat[:, g * 16:(g + 1) * 16], 1, 2), ALU.mult),
                                 reads=['wsb', 'wsc', 'cst'], writes=['wsb', 'wsc'])
                        P.op('pool', lambda e, g=g: e.tensor_tensor(pooled[:, g, :, :], fin[g][:, g, :, 16:80],
                                                                    upb[:, g, :, 16:80], ALU.subtract),
                             reads=['wsb', 'wsc'] + U, writes=['pooled'])

                    for b in range(12):
                        pc = bank4(4 + b // 4)[:, b % 4, :].rearrange("p (s t) -> p s t", s=2)
                        for k in range(4):
                            P.op('pe', lambda e, b=b, k=k, pc=pc: e.matmul(pc, convd[:, b, k, :], qkvb[:, b, :, k:k + 64],
                                                                           start=(k == 0), stop=(k == 3)),
                                 reads=['convd', 'qkvb_h', f'qkvb_c{b // 4}'], writes=psn(4 + b // 4))
                    P.op('act', lambda e: e.activation(qks[:], bank8(4), AF.Silu), reads=psn(4, 5), writes=['qks'])
                    P.op('act', lambda e: e.activation(vT[:], bank4(6), AF.Silu), reads=psn(6), writes=['vT'])
                    P.op('dve', lambda e: e.tensor_tensor(hn[:], qks[:], qks[:], ALU.mult), reads=['qks'], writes=['hn'])
                    pl = bank8(0)
                    for j in range(8):
                        P.op('pe', lambda e, j=j: e.matmul(pl[:, j, :], onesb[:], hn[:, j, :], start=True, stop=True),
                             reads=['hn', 'onesb'], writes=psn(0, 1))
                    P.op('act', lambda e: e.activation(sq[:], pl, AF.Ln, bias=EPS), reads=psn(0, 1), writes=['sq'])
                    P.op('act', lambda e: e.activation(sq[:], sq[:], AF.Exp, scale=-0.5), reads=['sq'], writes=['sq'])
                    P.op('dve', lambda e: e.scalar_tensor_tensor(kq[:, :, 1, :], qks[:, 0:4, :], 128.0 ** -0.5, sq[:, 0:4, :],
                                                                 ALU.mult, ALU.mult), reads=['qks', 'sq'], writes=['kq_q'])
                    P.op('dve', lambda e: e.tensor_tensor(kT[:], qks[:, 4:8, :], sq[:, 4:8, :], ALU.mult),
                         reads=['qks', 'sq'], writes=['kT'])

                    b_ = sm[:, S_BA:S_BA + 4]
                    a_ = sm[:, S_BA + 4:S_BA + 8]
                    beta = sm[:, S_BETA:S_BETA + 4]
                    g_ = sm[:, S_G:S_G + 4]
                    t1 = sm[:, S_T1:S_T1 + 4]
                    t2 = sm[:, S_T2:S_T2 + 4]
                    SM = ['sm']
                    P.op('act', lambda e: e.activation(t1, b_, AF.Exp, scale=-1.0), reads=['sm_ba'], writes=SM)
                    P.op('dve', lambda e: e.tensor_scalar_add(t1, t1, 1.0), reads=SM, writes=SM)
                    P.op('dve', lambda e: e.reciprocal(beta, t1), reads=SM, writes=SM)
                    P.op('dve', lambda e: e.tensor_tensor(t1, a_, prm[:, P_DTB:P_DTB + 4], ALU.add), reads=['sm_ba', 'prm'] + SM, writes=SM)
                    P.op('dve', lambda e: e.tensor_scalar_mul(t2, t1, -1.0), reads=SM, writes=SM)
                    P.op('dve', lambda e: e.tensor_tensor(t2, t2, t1, ALU.max), reads=SM, writes=SM)
                    P.op('act', lambda e: e.activation(t2, t2, AF.Exp, scale=-1.0), reads=SM, writes=SM)
                    P.op('act', lambda e: e.activation(t2, t2, AF.Ln, bias=1.0), reads=SM, writes=SM)
                    P.op('dve', lambda e: e.tensor_scalar_max(t1, t1, 0.0), reads=SM, writes=SM)
                    P.op('dve', lambda e: e.tensor_tensor(t1, t1, t2, ALU.add), reads=SM, writes=SM)
                    P.op('dve', lambda e: e.tensor_tensor(g_, t1, negA[:], ALU.mult), reads=SM + ['negA'], writes=SM)
                    pg = bank(2)[:, 136:144]
                    P.op('pe', lambda e: e.matmul(pg[:, 0:4], mtri, g_, start=True, stop=True), reads=SM + ['cst'], writes=psn(2))
                    P.op('pe', lambda e: e.matmul(pg[:, 4:8], mblk, g_, start=True, stop=True), reads=SM + ['cst'], writes=psn(2))
                    gc = sm[:, S_GC:S_GC + 4]
                    gl = sm[:, S_GL:S_GL + 4]
                    sc1 = sm[:, S_SC1:S_SC1 + 4]
                    sc2 = sm[:, S_SC2:S_SC2 + 4]
                    P.op('dve', lambda e: e.tensor_copy(sm[:, S_GC:S_GC + 8], pg), reads=psn(2), writes=SM)
                    P.op('act', lambda e: e.activation(sc1, gc, AF.Exp), reads=SM, writes=SM)
                    P.op('dve', lambda e: e.tensor_tensor(sc1, sc1, beta, ALU.mult), reads=SM, writes=SM)
                    P.op('dve', lambda e: e.tensor_tensor(sc2, gl, gc, ALU.subtract), reads=SM, writes=SM)
                    P.op('act', lambda e: e.activation(sc2, sc2, AF.Exp), reads=SM, writes=SM)
                    for h in range(4):
                        P.op('dve', lambda e, h=h: e.tensor_scalar_mul(gB[:, h, :], ones, sm[:, S_G + h:S_G + h + 1]),
                             reads=SM + ['cst'], writes=['gB'])
                    pdf = bank4(3)
                    pgr = bank4(7)
                    for h in range(4):
                        P.op('pe', lambda e, h=h: e.matmul(pdf[:, h, :], gB[:, h, :], mtri, start=True, stop=False),
                             reads=['gB', 'cst'], writes=psn(3))
                        P.op('pe', lambda e, h=h: e.matmul(pdf[:, h, :], nmtri, gB[:, h, :], start=False, stop=True),
                             reads=['gB', 'cst'], writes=psn(3))
                    for h in range(4):
                        P.op('pe', lambda e, h=h: e.matmul(pgr[:, h, :], gB[:, h, :], mtri, start=True, stop=True),
                             reads=['gB', 'cst'], writes=psn(7))
                    P.op('dve', lambda e: e.tensor_scalar_min(dexp[:], pdf, 0.0), reads=psn(3), writes=['dexp'])
                    P.op('act', lambda e: e.activation(dexp[:], dexp[:], AF.Exp), reads=['dexp'], writes=['dexp'])
                    P.op('act', lambda e: e.activation(egrow[:], pgr, AF.Exp), reads=psn(7), writes=['egrow'])
                    P.op('pool', lambda e: e.tensor_tensor(E2[:, :, 0, :], dexp[:], bcast(msn, 1, 4), ALU.mult),
                         reads=['dexp', 'cst'], writes=['xt'])
                    P.op('pool', lambda e: e.tensor_tensor(E2[:, :, 1, :], dexp[:], bcast(mi, 1, 4), ALU.mult),
                         reads=['dexp', 'cst'], writes=['xt'])
                    for h in range(4):
                        P.op('dve', lambda e, h=h: e.tensor_scalar_mul(gB[:, h, :], ones, sm[:, S_BETA + h:S_BETA + h + 1]),
                             reads=SM + ['cst'], writes=['gB'])
                    pbr = bank4(3)
                    for h in range(4):
                        P.op('pe', lambda e, h=h: e.matmul(pbr[:, h, :], gB[:, h, :], ident, start=True, stop=True),
                             reads=['gB', 'cst'], writes=psn(3))
                    P.op('dve', lambda e: e.tensor_tensor(kq[:, :, 0, :], pbr, kT[:], ALU.mult), reads=psn(3) + ['kT'], writes=['kq_k'])
                    P.op('pool', lambda e: e.tensor_tensor(qg[:], kq[:, :, 1, :], egrow[:], ALU.mult), reads=['kq_q', 'egrow'], writes=['qg'])
                    ptb = bankb(0)
                    for h in range(4):
                        P.op('pe', lambda e, h=h: e.transpose(ptb[:, h, :], kT[:, h, :], identb[:]), reads=['kT', 'identb'], writes=psn(0))
                    for h in range(4):
                        P.op('pe', lambda e, h=h: e.transpose(ptb[:, 4 + h, :], vT[:, h, :], identb[:]), reads=['vT', 'identb'], writes=psn(0))
                    for h in range(4):
                        P.op('dve', lambda e, h=h: e.tensor_scalar_mul(kbg[:, h, :], ptb[:, h, :], sm[:, S_SC1 + h:S_SC1 + h + 1]),
                             reads=psn(0) + SM, writes=['kbg'])
                        P.op('dve', lambda e, h=h: e.tensor_scalar_mul(kdec[:, h, :], ptb[:, h, :], sm[:, S_SC2 + h:S_SC2 + h + 1]),
                             reads=psn(0) + SM, writes=['kdec'])
                        for s in range(2):
                            rows = slice(s * 64, (s + 1) * 64)
                            P.op('dve', lambda e, h=h, s=s, rows=rows: e.tensor_scalar_mul(
                                vbblk[rows, h, s * 128:(s + 1) * 128], ptb[rows, 4 + h, :], sm[rows, S_BETA + h:S_BETA + h + 1]),
                                 reads=psn(0) + SM, writes=['vbblk'])
                    pla = PS[:, 4:6, :].rearrange("p b (h n) -> p (b h) n", h=2)
                    for h in range(4):
                        P.op('pe', lambda e, h=h: e.matmul(pla[:, h, :], kT[:, h, :], kq[:, h, :, :].rearrange("p a n -> p (a n)"),
                                                           start=True, stop=True),
                             reads=['kT', 'kq_k', 'kq_q'], writes=psn(4, 5))
                    P.op('dve', lambda e: e.tensor_tensor(LA[:].rearrange("p h a n -> p h (a n)"), pla,
                                                          E2[:].rearrange("p h a n -> p h (a n)"), ALU.mult),
                         reads=psn(4, 5) + ['xt'], writes=['LA'])
                    ptn = bankb(1)
                    for h in range(4):
                        P.op('pe', lambda e, h=h: e.transpose(ptn[:, h, :], LA[:, h, 0, :], identb[:]), reads=['LA', 'identb'], writes=psn(1))
                    P.op('act', lambda e: e.copy(N0[:], ptn[:, 0:4, :]), reads=psn(1), writes=['N0'])
                    P.op('dve', lambda e: e.tensor_tensor(PTa[:], LA[:, :, 0, :], bcast(identb[:], 1, 4), ALU.add),
                         reads=['LA', 'identb'], writes=['PTa'])
                    Nprev = lambda h: N0[:, h, :]
                    NTprev = lambda h: LA[:, h, 0, :]
                    prevreg = ['N0', 'LA']
                    PTcur, PTnext = PTa, PTb
                    PTr = {id(PTa): 'PTa', id(PTb): 'PTb'}
                    NAs = [NAa, NAb]
                    NAr = ['NAa', 'NAb']
                    for l in range(1, 6):
                        psq = PS[:, 6:8, :].rearrange("p b (h n) -> p (b h) n", h=2)
                        for h in range(4):
                            P.op('pe', lambda e, h=h, Np=Nprev, NTp=NTprev: e.matmul(psq[:, h, 0:128], NTp(h), Np(h), start=True, stop=True),
                                 reads=prevreg, writes=psn(6, 7))
                            if l < 5:
                                P.op('pe', lambda e, h=h, Np=Nprev, NTp=NTprev: e.matmul(psq[:, h, 128:256], Np(h), NTp(h), start=True, stop=True),
                                     reads=prevreg, writes=psn(6, 7))
                        NAn = NAs[l % 2]
                        NAnr = NAr[l % 2]
                        if l < 5:
                            P.op('act', lambda e, NAn=NAn: e.copy(NAn[:].rearrange("p h a n -> p h (a n)"), psq), reads=psn(6, 7), writes=[NAnr])
                        else:
                            P.op('act', lambda e, NAn=NAn: e.copy(NAn[:, :, 0, :], psq[:, :, 0:128]), reads=psn(6, 7), writes=[NAnr])
                        Nprev = (lambda NAn: (lambda h: NAn[:, h, 0, :]))(NAn)
                        NTprev = (lambda NAn: (lambda h: NAn[:, h, 1, :]))(NAn)
                        prevreg = [NAnr]
                        ppt = bank4(3)
                        for h in range(4):
                            P.op('pe', lambda e, h=h, PTc=PTcur: e.matmul(ppt[:, h, :], identb[:], PTc[:, h, :], start=True, stop=False),
                                 reads=[PTr[id(PTcur)], 'identb'], writes=psn(3))
                            P.op('pe', lambda e, h=h, PTc=PTcur, Np=Nprev: e.matmul(ppt[:, h, :], Np(h), PTc[:, h, :], start=False, stop=True),
                                 reads=[PTr[id(PTcur)], NAnr], writes=psn(3))
                        P.op('dve', lambda e, PTn=PTnext: e.tensor_copy(PTn[:], ppt), reads=psn(3), writes=[PTr[id(PTnext)]])
                        PTcur, PTnext = PTnext, PTcur
                    TT = PTcur
                    TTr = PTr[id(TT)]
                    pw = bank4(0)
                    for h in range(4):
                        P.op('pe', lambda e, h=h, TT=TT: e.matmul(pw[:, h, :], kbg[:, h, :], TT[:, h, :], start=True, stop=True),
                             reads=['kbg', TTr], writes=psn(0))
                    P.op('act', lambda e: e.mul(negwT[:], pw, -1.0), reads=psn(0), writes=['negwT'])
                    pvn = PS[:, 4:6, :].rearrange("p b (h n) -> p (b h) n", h=2)
                    for h in range(4):
                        P.op('pe', lambda e, h=h, TT=TT: e.matmul(pvn[:, h, :], TT[:, h, :], vbblk[:, h, :], start=True, stop=False),
                             reads=['vbblk', TTr], writes=psn(4, 5))
                        P.op('pe', lambda e, h=h: e.matmul(pvn[:, h, :], negwT[:, h, :], Sbf[:, h, :], start=False, stop=True),
                             reads=['negwT', 'Sbf'], writes=psn(4, 5))
                    P.op('dve', lambda e: e.tensor_copy(vnblk[0:64, :, 0:128], pvn[0:64, :, 0:128]), reads=psn(4, 5), writes=['vnblk'])
                    P.op('act', lambda e: e.copy(vnblk[64:128, :, 128:256], pvn[64:128, :, 128:256]), reads=psn(4, 5), writes=['vnblk'])
                    po = PS[:, 6:8, :].rearrange("p b (h n) -> p (b h) n", h=2)
                    for h in range(4):
                        P.op('pe', lambda e, h=h: e.matmul(po[:, h, :], qg[:, h, :], Sbf[:, h, :], start=True, stop=False),
                             reads=['qg', 'Sbf'], writes=psn(6, 7))
                        P.op('pe', lambda e, h=h: e.matmul(po[:, h, :], LA[:, h, 1, :], vnblk[:, h, :], start=False, stop=True),
                             reads=['LA', 'vnblk'], writes=psn(6, 7))
                    psu = PS[:, 0:2, :].rearrange("p b (h n) -> p (b h) n", h=2)
                    for h in range(4):
                        P.op('pe', lambda e, h=h: e.matmul(psu[:, h, :], kdec[:, h, :], vnblk[:, h, :], start=True, stop=True),
                             reads=['kdec', 'vnblk'], writes=psn(0, 1))
                    for h in range(4):
                        for s in range(2):
                            col = s * 64 + 63
                            P.op('dve', lambda e, h=h, s=s, col=col: e.scalar_tensor_tensor(
                                S32[:, h, s * 128:(s + 1) * 128], S32[:, h, s * 128:(s + 1) * 128], egrow[:, h, col:col + 1],
                                psu[:, h, s * 128:(s + 1) * 128], ALU.mult, ALU.add),
                                 reads=['S32', 'egrow'] + psn(0, 1), writes=['S32'])
                    P.op('act', lambda e: e.copy(Sbf[:], S32[:]), reads=['S32'], writes=['Sbf'])
                    P.op('dve', lambda e: e.tensor_copy(otok[0:64, :, :], po[0:64, :, 0:128]), reads=psn(6, 7), writes=['qks'])
                    P.op('act', lambda e: e.copy(otok[64:128, :, :], po[64:128, :, 128:256]), reads=psn(6, 7), writes=['qks'])
                    oss = sm[:, S_OSS:S_OSS + 4]
                    ors = sm[:, S_ORS:S_ORS + 4]
                    P.op('pool', lambda e: e.tensor_tensor(osq[:], otok[:], otok[:], ALU.mult), reads=['qks'], writes=['dexp'])
                    P.op('dve', lambda e: e.tensor_reduce(oss, osq[:], AX.X, ALU.add), reads=['dexp'], writes=['sm_o'])
                    P.op('act', lambda e: e.activation(ors, oss, AF.Ln, bias=EPS, scale=1.0 / 128), reads=['sm_o'], writes=['sm_o'])
                    P.op('act', lambda e: e.activation(ors, ors, AF.Exp, scale=-0.5), reads=['sm_o'], writes=['sm_o'])
                    P.op('pool', lambda e: e.tensor_tensor(osq[:], zt[:], bcast(prm[:, P_GO:P_GO + 128], 1, 4), ALU.mult),
                         reads=['zt', 'prm', 'sm_o'], writes=['dexp'])
                    for h in range(4):
                        P.op('dve', lambda e, h=h: e.scalar_tensor_tensor(ydn[:, h, :], otok[:, h, :], sm[:, S_ORS + h:S_ORS + h + 1],
                                                                          osq[:, h, :], ALU.mult, ALU.mult),
                             reads=['qks', 'sm_o', 'dexp'], writes=['ydn'])
                    pyt = bankb(2)
                    for h in range(4):
                        P.op('pe', lambda e, h=h: e.transpose(pyt[:, h, :], ydn[:, h, :], identb[:]), reads=['ydn', 'identb'], writes=psn(2))
                    P.op('act', lambda e: e.copy(catT[:, 4:8, :], pyt[:, 0:4, :]), reads=psn(2), writes=['catT_d'])
                    pp = bank4(3)
                    for g in range(4):
                        P.op('pe', lambda e, g=g: e.matmul(pp[:, g, :], WA[:, WPOOL0 + g * 128:WPOOL0 + (g + 1) * 128],
                                                           pooled[:, g, :, :].rearrange("p s t -> p (s t)"), start=True, stop=True),
                             reads=['pooled'] + wa_regs(WPOOL0, WPOOL0 + 512), writes=psn(3))
                    for g in range(4):
                        P.op('dve', lambda e, g=g: e.tensor_scalar_mul(catT[:, g, :], pp[:, g, :], prm[:, P_PSC + g:P_PSC + g + 1]),
                             reads=psn(3) + ['prm'], writes=['catT_p'])
                    pout = bank8(4)
                    for j in range(8):
                        for k in range(8):
                            P.op('pe', lambda e, j=j, k=k: e.matmul(pout[:, j, :], W1024(WOUT0, k, j * 128, (j + 1) * 128), catT[:, k, :],
                                                                    start=(k == 0), stop=(k == 7)),
                                 reads=['catT_d', 'catT_p'] + W1024_regs(WOUT0, k, j * 128, (j + 1) * 128), writes=psn(4, 5))
                    P.op('dve', lambda e, hview=hview: e.tensor_tensor(hview, pout.rearrange("p k (s t) -> p k s t", s=2), hview, ALU.add),
                         reads=psn(4, 5) + hregs, writes=hregs)
                    if stop_after == ('p1tile', cg):
                        break
                if stop_after is not None and stop_after[0] == 'p1tile':
                    dump('hT', hT[:, :, :, 0:(stop_after[1] % CPH + 1) * 64], [f"h{s}_{c}" for s in range(2) for c in range(CPH)])
                    dump('S32', S32[:], ['S32'])
                    dump('catT', catT[:], ['catT_d', 'catT_p'])
                    dump('otok', otok, ['qks'])
                    dump('kq', kq[:], ['kq_k', 'kq_q'])
                    dump('LA', LA[:], ['LA'])
                    dump('TT', TT[:], [TTr])
                    dump('sm', sm[:], ['sm', 'sm_o', 'sm_ba'])
                    dump('kT', kT[:], ['kT'])
                    dump('vT', vT[:], ['vT'])
                    dump('vnblk', vnblk[:], ['vnblk'])
                    dump('pooled', pooled[:], ['pooled'])
                phase_barrier('p1')
        def phase1_dump(half):
            if stop_after[0] == 'p1':
                dump('hT', hT[:], [f"h{s}_{c}" for s in range(2) for c in range(CPH)])

        def phase2(half):
            WXA, WXB = 0, 8192
            load_w1024(wxk_d, WXA, 'g_wxa')
            load_w1024(wxv_d, WXB, 'g_wxb')
            with ExitStack() as p2:
                KT = sbuf(p2, "KT", [128, 2, 8, 256], BF16)
                Vm = sbuf(p2, "Vm", [128, 2, 2, 1024], BF16)
                with ExitStack() as p2a:
                    memt = sbuf(p2a, "memt", [128, D], F32)
                    mjunk = sbuf(p2a, "mjunk", [128, D], F32)
                    mn = sbuf(p2a, "mn", [128, D], BF16)
                    mT = sbuf(p2a, "mT", [128, 8, 2, 256], BF16)
                    msm = sbuf(p2a, "msm", [128, 4], F32)
                    for s in range(2):
                        for mt in range(2):
                            P.dma('sp', lambda e, s=s, mt=mt: e.dma_start(out=memt[:], in_=mem_d[s, mt * 128:(mt + 1) * 128, :]),
                                  writes=['memt'], group='g_mem')
                            P.op('act', lambda e: e.activation(mjunk[:], memt[:], AF.Square), reads=['memt'], writes=['mjunk'])
                            P.op('dve', lambda e: e.tensor_reduce(msm[:, 0:1], mjunk[:], AX.X, ALU.add), reads=['mjunk'], writes=['msm'])
                            P.op('act', lambda e: e.activation(msm[:, 1:2], msm[:, 0:1], AF.Ln, bias=EPS, scale=1.0 / D), reads=['msm'], writes=['msm'])
                            P.op('act', lambda e: e.activation(msm[:, 1:2], msm[:, 1:2], AF.Exp, scale=-0.5), reads=['msm'], writes=['msm'])
                            P.op('dve', lambda e: e.tensor_scalar_mul(mn[:], memt[:], msm[:, 1:2]), reads=['memt', 'msm'], writes=['mn'])
                            pmt = bankb(0)
                            for k in range(8):
                                P.op('pe', lambda e, k=k: e.transpose(pmt[:, k, :], mn[:, k * 128:(k + 1) * 128], identb[:]),
                                     reads=['mn', 'identb'], writes=psn(0))
                            for k in range(8):
                                P.op('dve', lambda e, k=k, s=s, mt=mt: e.tensor_scalar_mul(mT[:, k, s, mt * 128:(mt + 1) * 128], pmt[:, k, :],
                                                                                           prm[:, P_GMEM + k:P_GMEM + k + 1]),
                                     reads=psn(0) + ['prm'], writes=['mT'])
                    nb = 0
                    for s in range(2):
                        for dj in range(8):
                            b = 1 + (nb % 4)
                            nb += 1
                            pk = bank(b)[:, 0:256]
                            for k in range(8):
                                P.op('pe', lambda e, k=k, s=s, dj=dj, pk=pk: e.matmul(pk, W1024(WXA, k, dj * 128, (dj + 1) * 128), mT[:, k, s, :],
                                                                                      start=(k == 0), stop=(k == 7)),
                                     reads=['mT'] + W1024_regs(WXA, k, dj * 128, (dj + 1) * 128), writes=psn(b))
                            if nb % 2:
                                P.op('act', lambda e, s=s, dj=dj, pk=pk: e.copy(KT[:, s, dj, :], pk), reads=psn(b), writes=['KT'])
                            else:
                                P.op('dve', lambda e, s=s, dj=dj, pk=pk: e.tensor_copy(KT[:, s, dj, :], pk), reads=psn(b), writes=['KT'])
                    for s in range(2):
                        for mt in range(2):
                            for hf in range(2):
                                b = 1 + (nb % 4)
                                nb += 1
                                pv = bank(b)
                                for k in range(8):
                                    P.op('pe', lambda e, k=k, s=s, mt=mt, hf=hf, pv=pv: e.matmul(
                                        pv, mT[:, k, s, mt * 128:(mt + 1) * 128], W1024(WXB, k, hf * 512, (hf + 1) * 512),
                                        start=(k == 0), stop=(k == 7)),
                                         reads=['mT'] + W1024_regs(WXB, k, hf * 512, (hf + 1) * 512), writes=psn(b))
                                if nb % 2:
                                    P.op('act', lambda e, s=s, mt=mt, hf=hf, pv=pv: e.copy(Vm[:, s, mt, hf * 512:(hf + 1) * 512], pv),
                                         reads=psn(b), writes=['Vm'])
                                else:
                                    P.op('dve', lambda e, s=s, mt=mt, hf=hf, pv=pv: e.tensor_copy(Vm[:, s, mt, hf * 512:(hf + 1) * 512], pv),
                                         reads=psn(b), writes=['Vm'])
                    phase_barrier('p2a')
                load_w1024(wxq_d, WXA, 'g_wxa')
                load_w1024(wxo_d, WXB, 'g_wxb')
                TQ = 256
                sq2 = sbuf(p2, "sq2", [128, 8, TQ], F32)
                rstd2 = sbuf(p2, "rstd2", [128, TQ], F32)
                hn2 = sbuf(p2, "hn2", [128, 8, TQ], BF16)
                qT2 = sbuf(p2, "qT2", [128, 8, TQ], BF16)
                oT2 = sbuf(p2, "oT2", [128, 8, TQ], BF16)
                Eb = sbuf(p2, "Eb", [128, 4, 256], F32)
                Pm = sbuf(p2, "Pm", [128, 4, 256], BF16)
                PTt = sbuf(p2, "PTt", [128, 8, 128], BF16)
                ssm = sbuf(p2, "ssm", [128, 16], F32)
                for s in range(2):
                    for qi in range(TPH // TQ):
                        q0 = qi * TQ
                        hr = hreg(s, q0, q0 + TQ)
                        hv = hT[:, :, s, q0:q0 + TQ]
                        P.op('act', lambda e, hv=hv: e.activation(sq2[:], hv, AF.Square), reads=hr, writes=['sq2'])
                        pst = bank(0)[:, 0:TQ]
                        for k in range(8):
                            P.op('pe', lambda e, k=k, pst=pst: e.matmul(pst, ones, sq2[:, k, :], start=(k == 0), stop=(k == 7)),
                                 reads=['sq2', 'cst'], writes=psn(0))
                        P.op('act', lambda e, pst=pst: e.activation(rstd2[:], pst, AF.Ln, bias=EPS, scale=1.0 / D), reads=psn(0), writes=['rstd2'])
                        P.op('act', lambda e: e.activation(rstd2[:], rstd2[:], AF.Exp, scale=-0.5), reads=['rstd2'], writes=['rstd2'])
                        for k in range(8):
                            P.op('dve', lambda e, k=k, hv=hv: e.scalar_tensor_tensor(hn2[:, k, :], hv[:, k, :], prm[:, P_GXAT + k:P_GXAT + k + 1],
                                                                                     rstd2[:], ALU.mult, ALU.mult),
                                 reads=hr + ['prm', 'rstd2'], writes=['hn2'])
                        pq = PS[:, 0:4, :].rearrange("p b (j n) -> p (b j) n", j=2)
                        for dj in range(8):
                            for k in range(8):
                                P.op('pe', lambda e, dj=dj, k=k: e.matmul(pq[:, dj, :], W1024(WXA, k, dj * 128, (dj + 1) * 128), hn2[:, k, :],
                                                                          start=(k == 0), stop=(k == 7)),
                                     reads=['hn2'] + W1024_regs(WXA, k, dj * 128, (dj + 1) * 128), writes=psn(dj // 2))
                        P.op('act', lambda e: e.mul(qT2[:, 0:4, :], pq[:, 0:4, :], 1.0 / 16), reads=psn(0, 1), writes=['qT2a'])
                        P.op('dve', lambda e: e.tensor_scalar_mul(qT2[:, 4:8, :], pq[:, 4:8, :], 1.0 / 16), reads=psn(2, 3), writes=['qT2b'])
                        for sub in range(TQ // 128):
                            tsl = slice(sub * 128, (sub + 1) * 128)
                            psc = PS[:, 4:6, :].rearrange("p b (h n) -> p (b h) n", h=2)
                            for h in range(4):
                                for j in range(2):
                                    P.op('pe', lambda e, h=h, j=j, tsl=tsl, s=s: e.matmul(psc[:, h, :], qT2[:, 2 * h + j, tsl], KT[:, s, 2 * h + j, :],
                                                                                     start=(j == 0), stop=(j == 1)),
                                         reads=['qT2a', 'qT2b', 'KT'], writes=psn(4, 5))
                            P.op('dve', lambda e: e.tensor_reduce(ssm[:, 0:4], psc, AX.X, ALU.max, negate=True), reads=psn(4, 5), writes=['ssm'])
                            P.op('dve', lambda e: e.tensor_tensor(Eb[:], psc, bcast(ssm[:, 0:4], 2, 256), ALU.add), reads=psn(4, 5) + ['ssm'], writes=['Eb'])
                            P.op('act', lambda e: e.activation(Eb[:], Eb[:], AF.Exp), reads=['Eb'], writes=['Eb'])
                            P.op('dve', lambda e: e.tensor_reduce(ssm[:, 4:8], Eb[:], AX.X, ALU.add), reads=['Eb'], writes=['ssm2'])
                            P.op('dve', lambda e: e.reciprocal(ssm[:, 8:12], ssm[:, 4:8]), reads=['ssm2'], writes=['ssm3'])
                            P.op('dve', lambda e: e.tensor_tensor(Pm[:], Eb[:], bcast(ssm[:, 8:12], 2, 256), ALU.mult), reads=['Eb', 'ssm3'], writes=['Pm'])
                            ptp = bankb(6)
                            for h in range(4):
                                for mt in range(2):
                                    P.op('pe', lambda e, h=h, mt=mt: e.transpose(ptp[:, h * 2 + mt, :], Pm[:, h, mt * 128:(mt + 1) * 128], identb[:]),
                                         reads=['Pm', 'identb'], writes=psn(6))
                            P.op('act', lambda e: e.copy(PTt[:], ptp), reads=psn(6), writes=['PTt'])
                            pov = bank8(0) if sub == 0 else bank8(2)
                            pr = psn(0, 1) if sub == 0 else psn(2, 3)
                            for h in range(4):
                                for j in range(2):
                                    for mt in range(2):
                                        c0 = h * 256 + j * 128
                                        P.op('pe', lambda e, h=h, j=j, mt=mt, c0=c0, pov=pov, s=s: e.matmul(
                                            pov[:, 2 * h + j, :], Vm[:, s, mt, c0:c0 + 128], PTt[:, h * 2 + mt, :], start=(mt == 0), stop=(mt == 1)),
                                             reads=['Vm', 'PTt'], writes=pr)
                            P.op('dve', lambda e, tsl=tsl, pov=pov: e.tensor_copy(oT2[:, :, tsl], pov), reads=pr, writes=[f'oT2_{sub}'])
                        pxo = PS[:, 4:8, :].rearrange("p b (j n) -> p (b j) n", j=2)
                        for dj in range(8):
                            for k in range(8):
                                P.op('pe', lambda e, dj=dj, k=k: e.matmul(pxo[:, dj, :], W1024(WXB, k, dj * 128, (dj + 1) * 128), oT2[:, k, :],
                                                                          start=(k == 0), stop=(k == 7)),
                                     reads=['oT2_0', 'oT2_1'] + W1024_regs(WXB, k, dj * 128, (dj + 1) * 128), writes=psn(4 + dj // 2))
                        P.op('dve', lambda e, hv=hv: e.tensor_tensor(hv, pxo, hv, ALU.add), reads=psn(4, 5, 6, 7) + hr, writes=hr)
                phase_barrier('p2')
        def phase2_dump(half):
            dump('hT', hT[:], [f"h{s}_{c}" for s in range(2) for c in range(CPH)])

        def phase3(half):
            with ExitStack() as p3:
                hn3 = sbuf(p3, "hn3", [128, 8, 2, TPH], BF16)
                sq3 = sbuf(p3, "sq3", [128, 8, 128], F32)
                rstd3 = sbuf(p3, "rstd3", [128, 128], F32)
                rl = sbuf(p3, "rl", [128, 512], F32)
                aT = [sbuf(p3, f"aT{i}", [128, 4, 512], BF16) for i in range(2)]
                of = sbuf(p3, "of", [128, 8, 128], F32)
                osb = [sbuf(p3, f"osb{i}", [128, D], F32) for i in range(2)]

                def norm3(s, q0, n, gbase, outk, outregs, sqb, rsb):
                    hr = hreg(s, q0, q0 + n)
                    hv = hT[:, :, s, q0:q0 + n]
                    P.op('act', lambda e: e.activation(sqb[:, :, 0:n], hv, AF.Square), reads=hr, writes=['sq3'])
                    pst = bank(0)[:, 0:n]
                    for k in range(8):
                        P.op('pe', lambda e, k=k: e.matmul(pst, ones, sqb[:, k, 0:n], start=(k == 0), stop=(k == 7)),
                             reads=['sq3', 'cst'], writes=psn(0))
                    P.op('act', lambda e: e.activation(rsb[:, 0:n], pst, AF.Ln, bias=EPS, scale=1.0 / D), reads=psn(0), writes=['rstd3'])
                    P.op('act', lambda e: e.activation(rsb[:, 0:n], rsb[:, 0:n], AF.Exp, scale=-0.5), reads=['rstd3'], writes=['rstd3'])
                    for k in range(8):
                        P.op('dve', lambda e, k=k: e.scalar_tensor_tensor(outk(k), hv[:, k, :], prm[:, gbase + k:gbase + k + 1], rsb[:, 0:n],
                                                                          ALU.mult, ALU.mult),
                             reads=hr + ['prm', 'rstd3'], writes=outregs)

                for s in range(2):
                    for qi in range(TPH // 128):
                        q0 = qi * 128
                        norm3(s, q0, 128, P_GMLP, lambda k, s=s, q0=q0: hn3[:, k, s, q0:q0 + 128], [f'hn3_{s}_{qi // 4}'], sq3, rstd3)
                NFC = DFF // 512
                SL_UP = [0, 8192]
                SL_DN = [4096, 12288]
                wupv = wup_d.rearrange("(k p) n -> p k n", p=128)
                wdnv = wdn_d.rearrange("(j p) n -> p j n", p=128)

                def load_chunk(fc):
                    sl = fc % 2
                    for k in range(8):
                        lo = SL_UP[sl] + k * 512
                        P.dma('pool', lambda e, k=k, lo=lo, fc=fc: e.dma_start(out=WA[:, lo:lo + 512], in_=wupv[:, k, fc * 512:(fc + 1) * 512]),
                              writes=wa_regs(lo, lo + 512), group=f'g_up{sl}', tok=False)
                    for j in range(4):
                        lo = SL_DN[sl] + j * 1024
                        P.dma('pool', lambda e, j=j, lo=lo, fc=fc: e.dma_start(out=WA[:, lo:lo + 1024], in_=wdnv[:, fc * 4 + j, :]),
                              writes=wa_regs(lo, lo + 1024), group=f'g_dn{sl}', tok=False)

                load_chunk(0)
                nt = 0
                for fc in range(NFC):
                    if fc + 1 < NFC:
                        load_chunk(fc + 1)
                    sl = fc % 2
                    for s in range(2):
                        for qi in range(TPH // 512):
                            q0 = qi * 512
                            hr = hreg(s, q0, q0 + 512)
                            a = aT[nt % 2]
                            ar = f'aT{nt % 2}'
                            nt += 1
                            for fb in range(4):
                                for k in range(8):
                                    lo = SL_UP[sl] + k * 512 + fb * 128
                                    P.op('pe', lambda e, fb=fb, k=k, lo=lo, s=s, q0=q0: e.matmul(bank(fb), WA[:, lo:lo + 128], hn3[:, k, s, q0:q0 + 512],
                                                                                                 start=(k == 0), stop=(k == 7)),
                                         reads=[f'hn3_{s}_{qi}'] + wa_regs(lo, lo + 128), writes=psn(fb))
                                P.op('act', lambda e, fb=fb: e.activation(rl[:], bank(fb), AF.Relu), reads=psn(fb), writes=['rl'])
                                P.op('act', lambda e, fb=fb, a=a: e.activation(a[:, fb, :], rl[:], AF.Square), reads=['rl'], writes=[ar])
                            for dj in range(8):
                                b = 4 + dj % 4
                                for fb in range(4):
                                    lo = SL_DN[sl] + fb * 1024 + dj * 128
                                    P.op('pe', lambda e, fb=fb, dj=dj, lo=lo, a=a, b=b: e.matmul(bank(b), WA[:, lo:lo + 128], a[:, fb, :],
                                                                                                 start=(fb == 0), stop=(fb == 3)),
                                         reads=[ar] + wa_regs(lo, lo + 128), writes=psn(b))
                                hv = hT[:, dj, s, q0:q0 + 512]
                                P.op('dve', lambda e, hv=hv, b=b: e.tensor_tensor(hv, bank(b), hv, ALU.add), reads=psn(b) + hr, writes=hr)
                no = 0
                for s in range(2):
                    for ti in range(TPH // 128):
                        q0 = ti * 128
                        norm3(s, q0, 128, P_GFIN, lambda k: of[:, k, :], ['of'], sq3, rstd3)
                        pto = bank8(2)
                        for k in range(8):
                            P.op('pe', lambda e, k=k: e.transpose(pto[:, k, :], of[:, k, :], ident), reads=['of', 'cst'], writes=psn(2, 3))
                        ob = osb[no % 2]
                        obr = f'osb{no % 2}'
                        no += 1
                        P.op('act', lambda e, ob=ob: e.copy(ob[:].rearrange("p (k n) -> p k n", k=8), pto), reads=psn(2, 3), writes=[obr])
                        tg = half * TPH + q0
                        P.dma('sp', lambda e, ob=ob, s=s, tg=tg: e.dma_start(out=out_d[s, tg:tg + 128, :], in_=ob[:]),
                              reads=[obr], writes=['out'], group=f'g_{obr}')
                phase_barrier('p3')

        for half in range(NHALF):
            phase1(half)
            if stop_after is not None and stop_after[0] in ('p1tile', 'p1'):
                phase1_dump(half)
                break
            phase2(half)
            if stop_after is not None and stop_after[0] == 'p2':
                phase2_dump(half)
                break
            phase3(half)

        fg = [g for g in ('g_osb0', 'g_osb1') if stop_after is None]
        fg += [f'dbg{i + 1}' for i in range(dbg_n[0])]
        if max_ops is not None:
            P.ops = P.ops[:max_ops]
            fg = []
        st = P.finalize(final_groups=fg)
        build_nc.stats = st
    return nc


def make_consts():
    c = np.zeros((128, CW), np.float32)
    idx = np.arange(128)
    blk = idx // 64
    same = blk[:, None] == blk[None, :]
    c[:, C_ID:C_ID + 128] = np.eye(128)
    c[:, C_ONE:C_ONE + 128] = 1.0
    tri = same & (idx[:, None] <= idx[None, :])
    c[:, C_MTRI:C_MTRI + 128] = tri
    c[:, C_NMTRI:C_NMTRI + 128] = -tri.astype(np.float32)
    c[:, C_MBLK:C_MBLK + 128] = same
    c[:, C_MSN:C_MSN + 128] = -(same & (idx[None, :] > idx[:, None])).astype(np.float32)
    c[:, C_MI:C_MI + 128] = (same & (idx[None, :] >= idx[:, None]))
    for g, w in enumerate(WINDOWS):
        t = np.arange(16)
        c[:, C_RAT + g * 16:C_RAT + (g + 1) * 16] = (w / np.minimum(t + 1, w))[None, :]
    return c


def make_prm(inp):
    p = np.zeros((128, PW), np.float32)

    def colvec(v):
        return np.asarray(v, np.float32).reshape(8, 128).T

    p[:, P_GMIX:P_GMIX + 8] = colvec(inp['norm_mix_g'][0])
    p[:, P_GXAT:P_GXAT + 8] = colvec(inp['norm_xattn_g'][0])
    p[:, P_GMEM:P_GMEM + 8] = colvec(inp['mem_norm_g'][0])
    p[:, P_GMLP:P_GMLP + 8] = colvec(inp['norm_mlp_g'][0])
    p[:, P_GFIN:P_GFIN + 8] = colvec(inp['final_norm_g'])
    p[:, P_PSC:P_PSC + 4] = np.asarray(inp['pool_scale'][0], np.float32).reshape(4, 128).T
    cw = np.asarray(inp['conv_w'][0], np.float32)
    p[:, P_CONV:P_CONV + 48] = cw.reshape(4, 12, 128).transpose(2, 1, 0).reshape(128, 48)
    p[:, P_ALOG:P_ALOG + 4] = np.broadcast_to(np.asarray(inp['a_log'][0], np.float32)[None, :], (128, 4))
    p[:, P_DTB:P_DTB + 4] = np.broadcast_to(np.asarray(inp['dt_bias'][0], np.float32)[None, :], (128, 4))
    p[:, P_GO:P_GO + 128] = np.broadcast_to(np.asarray(inp['dn_out_norm_g'][0], np.float32)[None, :], (128, 128))
    return p


def make_in_maps(inp, n_cores=8):
    prm = make_prm(inp)
    cst = make_consts()
    shared = dict(
        w_in=np.ascontiguousarray(inp['w_in'][0], dtype=np.float32),
        w_pool=np.ascontiguousarray(inp['w_pool'][0], dtype=np.float32),
        w_out=np.ascontiguousarray(inp['w_out'][0], dtype=np.float32),
        w_xq=np.ascontiguousarray(inp['w_xq'][0], dtype=np.float32),
        w_xk=np.ascontiguousarray(inp['w_xk'][0], dtype=np.float32),
        w_xv=np.ascontiguousarray(inp['w_xv'][0], dtype=np.float32),
        w_xo=np.ascontiguousarray(inp['w_xo'][0], dtype=np.float32),
        w_up=np.ascontiguousarray(inp['w_up'][0], dtype=np.float32),
        w_down=np.ascontiguousarray(inp['w_down'][0], dtype=np.float32),
        prm=prm, cst=cst)
    x = np.asarray(inp['x'], np.float32)
    mem = np.asarray(inp['mem'], np.float32)
    maps = []
    for i in range(n_cores):
        m = dict(shared)
        m['x'] = np.ascontiguousarray(x[NSEQ * i:NSEQ * (i + 1)])
        m['mem'] = np.ascontiguousarray(mem[NSEQ * i:NSEQ * (i + 1)])
        maps.append(m)
    return maps


def kernel(**inputs):
    nc = build_nc()
    maps = make_in_maps(inputs, 8)
    res = run_bass_kernel_spmd(nc, maps, core_ids=list(range(8)))
    outs = [np.asarray(r['out'], np.float32) for r in res.results]
    return np.concatenate(outs, axis=0)
```

```python
import numpy as np
from contextlib import ExitStack
import concourse.bass as bass
import concourse.mybir as mybir
from concourse.bass_utils import run_bass_kernel_spmd

F32 = mybir.dt.float32
BF16 = mybir.dt.bfloat16
AF = mybir.ActivationFunctionType
ALU = mybir.AluOpType
AX = mybir.AxisListType

D = 1024
SEQ = 2048
NSEQ = 2
MEM = 256
INC = 2568
DFF = 4096
EPS = 1e-6
NHALF = 2
CPH = 16
TPH = CPH * 64
WINDOWS = (2, 4, 8, 16)

P_GMIX, P_GXAT, P_GMEM, P_GMLP, P_GFIN = 0, 8, 16, 24, 32
P_PSC = 40
P_CONV = 44
P_ALOG = 92
P_DTB = 96
P_GO = 100
PW = 228
C_ID, C_ONE, C_MTRI, C_NMTRI, C_MBLK, C_MSN, C_MI, C_RAT = 0, 128, 256, 384, 512, 640, 768, 896
CW = 960


class Prog:
    def __init__(self, nc, es):
        self.nc = nc
        self.es = es
        self.ops = []
        self.engs = {'pe': nc.tensor, 'act': nc.scalar, 'dve': nc.vector, 'pool': nc.gpsimd, 'sp': nc.sync}
        self.token = None

    def op(self, engine, fn, reads=(), writes=(), tok=True):
        r = tuple(reads)
        if tok and self.token is not None:
            r = r + (self.token,)
        self.ops.append(dict(e=engine, fn=fn, r=r, w=tuple(writes), dma=None))

    def dma(self, queue, fn, reads=(), writes=(), group=None, tok=True):
        r = tuple(reads)
        if tok and self.token is not None:
            r = r + (self.token,)
        self.ops.append(dict(e=queue, fn=fn, r=r, w=tuple(writes), dma=group))

    def barrier(self, fn):
        self.ops.append(dict(e='dve', fn=fn, r=(), w=(self.token,), dma=None))

    def finalize(self, final_groups=()):
        nc = self.nc
        ops = self.ops
        last_writer = {}
        readers = {}
        for i, o in enumerate(ops):
            deps = set()
            for r in o['r']:
                if r in last_writer:
                    deps.add(last_writer[r])
            for w in o['w']:
                if w in last_writer:
                    deps.add(last_writer[w])
                deps.update(readers.get(w, ()))
            deps.discard(i)
            best = {}
            for d in deps:
                od = ops[d]
                key = ('g', od['dma']) if od['dma'] is not None else ('e', od['e'])
                if key not in best or best[key] < d:
                    best[key] = d
            o['deps'] = set(best.values())
            for r in o['r']:
                readers.setdefault(r, []).append(i)
            for w in o['w']:
                last_writer[w] = i
                readers[w] = []

        def skip(od, o):
            return od['dma'] is None and od['e'] == 'pe' and o['e'] == 'pe' and o['dma'] is None

        need = [False] * len(ops)
        for i, o in enumerate(ops):
            for d in o['deps']:
                if not skip(ops[d], o):
                    need[d] = True
        sems = {e: self.es.enter_context(nc.semaphore('sem_' + e)) for e in self.engs}
        gsem = {}
        gcount = {}
        cnt = {e: 0 for e in self.engs}
        for i, o in enumerate(ops):
            if o['dma'] is not None:
                g = o['dma']
                if g not in gsem:
                    gsem[g] = self.es.enter_context(nc.semaphore('dg_' + g))
                    gcount[g] = 0
                gcount[g] += 16
                o['sig'] = (gsem[g], gcount[g])
            elif need[i]:
                cnt[o['e']] += 1
                o['sig'] = (sems[o['e']], cnt[o['e']])
            else:
                o['sig'] = None
            o['gsnap'] = None
        running = {}
        snaps = []
        for i, o in enumerate(ops):
            snaps.append(dict(running))
            if o['dma'] is not None:
                running[o['dma']] = o['sig'][1]
        waited = {e: {} for e in self.engs}
        nwaits = 0
        for i, o in enumerate(ops):
            e = o['e']
            eng = self.engs[e]
            wl = {}
            for d in o['deps']:
                od = ops[d]
                if skip(od, o):
                    continue
                s, v = od['sig']
                if od['dma'] is not None:
                    v = snaps[i][od['dma']]
                key = id(s)
                if key not in wl or wl[key][1] < v:
                    wl[key] = (s, v)
            for key, (s, v) in wl.items():
                if waited[e].get(key, 0) >= v:
                    continue
                waited[e][key] = v
                eng.wait_ge(s, v)
                nwaits += 1
            inst = o['fn'](eng)
            if o['dma'] is not None:
                inst.then_inc(o['sig'][0], 16)
            elif o['sig'] is not None:
                inst.then_inc(o['sig'][0], 1)
        for g in final_groups:
            nc.sync.wait_ge(gsem[g], gcount[g])
        self.stats = dict(n_ops=len(ops), n_waits=nwaits, sig=cnt, ngroups=len(gsem))
        return self.stats


def bcast(ap, pos, n):
    lst = [list(x) for x in ap.ap]
    lst.insert(pos, [0, n])
    return bass.AP(ap.tensor, ap.offset, lst)


def build_nc(stop_after=None, dbg=None, max_ops=None):
    nc = bass.Bass("TRN2", target_bir_lowering=False)

    def dr(name, shape, kind="ExternalInput", dt=F32):
        return nc.dram_tensor(name, list(shape), dt, kind=kind).ap()

    x_d = dr("x", [NSEQ, SEQ, D])
    mem_d = dr("mem", [NSEQ, MEM, D])
    win_d = dr("w_in", [D, INC])
    wpool_d = dr("w_pool", [4, 128, 128])
    wout_d = dr("w_out", [D, D])
    wxq_d = dr("w_xq", [D, D])
    wxk_d = dr("w_xk", [D, D])
    wxv_d = dr("w_xv", [D, D])
    wxo_d = dr("w_xo", [D, D])
    wup_d = dr("w_up", [D, DFF])
    wdn_d = dr("w_down", [DFF, D])
    prm_d = dr("prm", [128, PW])
    cst_d = dr("cst", [128, CW])
    out_d = dr("out", [NSEQ, SEQ, D], kind="ExternalOutput")
    dbg_d = {}
    if dbg:
        for k, (shp, dtn) in dbg.items():
            dbg_d[k] = dr("dbg_" + k, shp, kind="ExternalOutput", dt=(BF16 if dtn == 'bf16' else F32))

    es = ExitStack()
    with es:
        P = Prog(nc, es)

        uid = [0]

        def sbuf(stack, name, shape, dt):
            uid[0] += 1
            return stack.enter_context(nc.sbuf_tensor(f"sb{uid[0]}_{name}", list(shape), dt))

        hT = sbuf(es, "hT", [128, 8, NSEQ, TPH], F32)
        cst = sbuf(es, "cst", [128, CW], F32)
        prm = sbuf(es, "prm", [128, PW], F32)
        identb = sbuf(es, "identb", [128, 128], BF16)
        onesb = sbuf(es, "onesb", [128, 128], BF16)
        convd = sbuf(es, "convd", [128, 12, 4, 128], BF16)
        WA = sbuf(es, "warena", [128, 29312], BF16)
        S32 = sbuf(es, "S32", [128, 4, 256], F32)
        Sbf = sbuf(es, "Sbf", [128, 4, 256], BF16)
        qkvh = sbuf(es, "qkvh", [128, 12, 2, 3], BF16)
        uph = sbuf(es, "uph", [128, 4, 2, 16], F32)
        negA = sbuf(es, "negA", [128, 4], F32)
        dummy = sbuf(es, "dummy", [128, 2], F32)
        PS = es.enter_context(nc.psum_tensor("PS", [128, 8, 512], F32))

        ident = cst[:, C_ID:C_ID + 128]
        ones = cst[:, C_ONE:C_ONE + 128]
        mtri = cst[:, C_MTRI:C_MTRI + 128]
        nmtri = cst[:, C_NMTRI:C_NMTRI + 128]
        mblk = cst[:, C_MBLK:C_MBLK + 128]
        msn = cst[:, C_MSN:C_MSN + 128]
        mi = cst[:, C_MI:C_MI + 128]
        rat = cst[:, C_RAT:C_RAT + 64]

        def bank(b):
            return PS[:, b, :]

        def bank4(b):
            return PS[:, b, :].rearrange("p (j n) -> p j n", j=4)

        def bank8(b):
            return PS[:, b:b + 2, :].rearrange("p b (j n) -> p (b j) n", j=4)

        def bankb(b):
            return PS[:, b, :].bitcast(BF16).rearrange("p (j n) -> p j n", j=8)

        def psn(*bs):
            return [f"ps{b}" for b in bs]

        def wa_regs(lo, hi):
            return [f"wa{b}" for b in range(lo // 2048, (hi - 1) // 2048 + 1)]

        def hreg(s, t0, t1):
            return [f"h{s}_{c}" for c in range(t0 // 64, (t1 - 1) // 64 + 1)]

        dbg_n = [0]

        def dump(name, ap, reads):
            if name in dbg_d:
                dbg_n[0] += 1
                P.dma('sp', lambda e: e.dma_start(out=dbg_d[name], in_=ap), reads=reads, writes=['dbg_' + name],
                      group=f'dbg{dbg_n[0]}')

        P.dma('sp', lambda e: e.dma_start(out=cst[:], in_=cst_d[:, :]), writes=['cst'], group='c0')
        P.dma('sp', lambda e: e.dma_start(out=prm[:], in_=prm_d[:, :]), writes=['prm'], group='c1')
        P.op('dve', lambda e: e.tensor_copy(identb[:], ident), reads=['cst'], writes=['identb'])
        P.op('dve', lambda e: e.tensor_copy(onesb[:], ones), reads=['cst'], writes=['onesb'])
        for b in range(12):
            for k in range(4):
                col = P_CONV + b * 4 + k
                P.op('dve', lambda e, b=b, k=k, col=col: e.tensor_scalar_mul(convd[:, b, k, :], ident, prm[:, col:col + 1]),
                     reads=['cst', 'prm'], writes=['convd'])
        P.op('act', lambda e: e.activation(negA[:], prm[:, P_ALOG:P_ALOG + 4], AF.Exp), reads=['prm'], writes=['negA'])
        P.op('dve', lambda e: e.tensor_scalar_mul(negA[:], negA[:], -1.0), reads=['negA'], writes=['negA'])
        P.op('dve', lambda e: e.memset(S32[:], 0.0), writes=['S32'])
        P.op('dve', lambda e: e.memset(Sbf[:], 0.0), writes=['Sbf'])
        P.op('dve', lambda e: e.memset(qkvh[:], 0.0), writes=['qkvh'])
        P.op('dve', lambda e: e.memset(uph[:], 0.0), writes=['uph'])

        WIN0 = 0
        WOUT0 = 8 * INC
        WPOOL0 = WOUT0 + 8 * D
        winv = win_d.rearrange("(k p) n -> p k n", p=128)

        def load_w1024(dram, base, gname):
            v = dram.rearrange("(k p) n -> p k n", p=128)
            for k in range(8):
                lo = base + k * 1024
                P.dma('pool', lambda e, k=k, lo=lo: e.dma_start(out=WA[:, lo:lo + 1024], in_=v[:, k, :]),
                      writes=wa_regs(lo, lo + 1024), group=gname, tok=False)

        def load_p1_weights():
            for k in range(8):
                lo = WIN0 + k * INC
                P.dma('pool', lambda e, k=k, lo=lo: e.dma_start(out=WA[:, lo:lo + INC], in_=winv[:, k, :],
                                                                 max_dma_last_dim=4096),
                      writes=wa_regs(lo, lo + INC), group='g_win', tok=False)
            load_w1024(wout_d, WOUT0, 'g_wout')
            P.dma('pool', lambda e: e.dma_start(out=WA[:, WPOOL0:WPOOL0 + 512].rearrange("p (g d) -> p g d", g=4),
                                                in_=wpool_d.rearrange("g c d -> c g d")),
                  writes=wa_regs(WPOOL0, WPOOL0 + 512), group='g_wpool', tok=False)

        def Win(k, lo, hi):
            return WA[:, WIN0 + k * INC + lo:WIN0 + k * INC + hi]

        def Win_regs(k, lo, hi):
            return wa_regs(WIN0 + k * INC + lo, WIN0 + k * INC + hi)

        def W1024(base, k, lo, hi):
            return WA[:, base + k * 1024 + lo:base + k * 1024 + hi]

        def W1024_regs(base, k, lo, hi):
            return wa_regs(base + k * 1024 + lo, base + k * 1024 + hi)

        def phase_barrier(name):
            P.barrier(lambda e: e.memset(dummy[:], 0.0))

        P.token = 'tok'

        def rms_rows(stack_bufs, src_ps_or_sb, n, gcol, out_hn, reads, ps_stat_bank, sq, rstd, src_k):
            raise NotImplementedError

        def phase1(half):
            load_p1_weights()
            with ExitStack() as p1:
                xt = sbuf(p1, "xt", [128, D], F32)
                sq = sbuf(p1, "sq", [128, 8, 128], F32)
                rstd = sbuf(p1, "rstd", [128, 128], F32)
                hn = sbuf(p1, "hn", [128, 8, 128], BF16)
                qkvb = sbuf(p1, "qkvb", [128, 12, 2, 67], BF16)
                upb = sbuf(p1, "upb", [128, 4, 2, 80], F32)
                wsb = sbuf(p1, "wsb", [128, 4, 2, 80], F32)
                wsc = sbuf(p1, "wsc", [128, 4, 2, 80], F32)
                pooled = sbuf(p1, "pooled", [128, 4, 2, 64], BF16)
                catT = sbuf(p1, "catT", [128, 8, 128], BF16)
                qks = sbuf(p1, "qks", [128, 8, 128], F32)
                vT = sbuf(p1, "vT", [128, 4, 128], BF16)
                kq = sbuf(p1, "kq", [128, 4, 2, 128], BF16)
                kT = sbuf(p1, "kT", [128, 4, 128], BF16)
                sm = sbuf(p1, "sm", [128, 64], F32)
                gB = sbuf(p1, "gB", [128, 4, 128], F32)
                dexp = sbuf(p1, "dexp", [128, 4, 128], F32)
                E2 = xt[:].rearrange("p (h a n) -> p h a n", h=4, a=2)
                egrow = sbuf(p1, "egrow", [128, 4, 128], F32)
                qg = sbuf(p1, "qg", [128, 4, 128], BF16)
                LA = sbuf(p1, "LA", [128, 4, 2, 128], BF16)
                N0 = sbuf(p1, "N0", [128, 4, 128], BF16)
                NAa = sbuf(p1, "NAa", [128, 4, 2, 128], BF16)
                NAb = sbuf(p1, "NAb", [128, 4, 2, 128], BF16)
                PTa = sbuf(p1, "PTa", [128, 4, 128], BF16)
                PTb = sbuf(p1, "PTb", [128, 4, 128], BF16)
                negwT = sbuf(p1, "negwT", [128, 4, 128], BF16)
                vbblk = sbuf(p1, "vbblk", [128, 4, 256], BF16)
                kbg = sbuf(p1, "kbg", [128, 4, 128], BF16)
                kdec = sbuf(p1, "kdec", [128, 4, 128], BF16)
                vnblk = sbuf(p1, "vnblk", [128, 4, 256], BF16)
                zt = sbuf(p1, "zt", [128, 4, 128], BF16)
                ydn = sbuf(p1, "ydn", [128, 4, 128], BF16)
                otok = qks[:, 0:4, :]
                osq = dexp

                S_BA, S_BETA, S_G, S_T1, S_T2, S_GC, S_GL, S_SC1, S_SC2, S_OSS, S_ORS = 0, 8, 12, 16, 20, 24, 28, 32, 36, 40, 44

                P.op('dve', lambda e: e.memset(vbblk[:], 0.0), writes=['vbblk'])
                P.op('dve', lambda e: e.memset(sm[:], 0.0), writes=['sm', 'sm_o', 'sm_ba'])
                P.op('dve', lambda e: e.memset(vnblk[:], 0.0), writes=['vnblk'])

                tiles = []
                for c in range(CPH):
                    saved_ops = P.ops
                    P.ops = []
                    marks = {}
                    cg = half * CPH + c
                    t0 = c * 64
                    hregs = [f"h0_{c}", f"h1_{c}"]
                    hview = hT[:, :, :, t0:t0 + 64]
                    for s in range(NSEQ):
                        P.dma('sp', lambda e, s=s, cg=cg: e.dma_start(out=xt[s * 64:(s + 1) * 64, :],
                                                                      in_=x_d[s, cg * 64:(cg + 1) * 64, :]),
                              writes=['xt'], group='g_x')
                    pT = bank8(0)
                    for k in range(8):
                        P.op('pe', lambda e, k=k: e.transpose(pT[:, k, :], xt[:, k * 128:(k + 1) * 128], ident),
                             reads=['xt', 'cst'], writes=psn(0, 1))
                    P.op('act', lambda e, hview=hview: e.copy(hview, pT.rearrange("p k (s t) -> p k s t", s=2)),
                         reads=psn(0, 1), writes=hregs)
                    P.op('act', lambda e: e.activation(sq[:], pT, AF.Square), reads=psn(0, 1), writes=['sq'])
                    pst = bank(2)[:, 0:128]
                    for k in range(8):
                        P.op('pe', lambda e, k=k: e.matmul(pst, ones, sq[:, k, :], start=(k == 0), stop=(k == 7)),
                             reads=['sq', 'cst'], writes=psn(2))
                    P.op('act', lambda e: e.activation(rstd[:], pst, AF.Ln, bias=EPS, scale=1.0 / D),
                         reads=psn(2), writes=['rstd'])
                    P.op('act', lambda e: e.activation(rstd[:], rstd[:], AF.Exp, scale=-0.5), reads=['rstd'], writes=['rstd'])
                    for k in range(8):
                        P.op('dve', lambda e, k=k: e.scalar_tensor_tensor(hn[:, k, :], pT[:, k, :],
                                                                          prm[:, P_GMIX + k:P_GMIX + k + 1], rstd[:],
                                                                          ALU.mult, ALU.mult),
                             reads=psn(0, 1) + ['prm', 'rstd'], writes=['hn'])
                    marks['head_end'] = len(P.ops)
                    for grp in range(4):
                        pb = bank4(3 + grp)
                        for j in range(4):
                            col = (grp * 4 + j) * 128
                            for k in range(8):
                                P.op('pe', lambda e, pb=pb, j=j, k=k, col=col: e.matmul(
                                    pb[:, j, :], Win(k, col, col + 128), hn[:, k, :], start=(k == 0), stop=(k == 7)),
                                     reads=['hn'] + Win_regs(k, col, col + 128), writes=psn(3 + grp))
                    for k in range(8):
                        P.op('pe', lambda e, k=k: e.matmul(bank(7), hn[:, k, :], Win(k, 2048, 2560),
                                                           start=(k == 0), stop=(k == 7)),
                             reads=['hn'] + Win_regs(k, 2048, 2560), writes=psn(7))
                    pba = bank(2)[:, 128:136]
                    for k in range(8):
                        P.op('pe', lambda e, k=k: e.matmul(pba, hn[:, k, :], Win(k, 2560, 2568),
                                                           start=(k == 0), stop=(k == 7)),
                             reads=['hn'] + Win_regs(k, 2560, 2568), writes=psn(2))
                    P.op('pool', lambda e: e.tensor_copy(upb[:, :, :, 0:16], uph[:]), reads=['uph'], writes=['upb_h'])
                    P.op('pool', lambda e: e.tensor_copy(qkvb[:, :, :, 0:3], qkvh[:]), reads=['qkvh'], writes=['qkvb_h'])
                    P.op('act', lambda e: e.copy(upb[:, :, :, 16:80], bank4(3).rearrange("p g (s t) -> p g s t", s=2)),
                         reads=psn(3), writes=['upb_c'])
                    for j in range(3):
                        eng = 'dve' if j != 1 else 'act'
                        if eng == 'dve':
                            P.op('dve', lambda e, j=j: e.tensor_copy(qkvb[:, 4 * j:4 * j + 4, :, 3:67],
                                                                     bank4(4 + j).rearrange("p g (s t) -> p g s t", s=2)),
                                 reads=psn(4 + j), writes=[f'qkvb_c{j}'])
                        else:
                            P.op('act', lambda e, j=j: e.copy(qkvb[:, 4 * j:4 * j + 4, :, 3:67],
                                                              bank4(4 + j).rearrange("p g (s t) -> p g s t", s=2)),
                                 reads=psn(4 + j), writes=[f'qkvb_c{j}'])
                    P.op('act', lambda e: e.activation(zt[:], bank4(7), AF.Silu), reads=psn(7), writes=['zt'])
                    P.op('dve', lambda e: e.tensor_copy(sm[:, S_BA:S_BA + 8], pba), reads=psn(2), writes=['sm_ba'])
                    P.op('pool', lambda e: e.tensor_copy(uph[:], upb[:, :, :, 64:80]), reads=['upb_c'], writes=['uph'])
                    P.op('pool', lambda e: e.tensor_copy(qkvh[:], qkvb[:, :, :, 64:67]),
                         reads=['qkvb_c0', 'qkvb_c1', 'qkvb_c2'], writes=['qkvh'])

                    U = ['upb_h', 'upb_c']
                    P.op('pool', lambda e: e.tensor_tensor(wsb[:, :, :, 1:80], upb[:, :, :, 1:80], upb[:, :, :, 0:79], ALU.add),
                         reads=U, writes=['wsb'])
                    P.op('pool', lambda e: e.tensor_tensor(wsc[:, 1:4, :, 3:80], wsb[:, 1:4, :, 3:80], wsb[:, 1:4, :, 1:78], ALU.add),
                         reads=['wsb'], writes=['wsc'])
                    P.op('pool', lambda e: e.tensor_tensor(wsb[:, 2:4, :, 7:80], wsc[:, 2:4, :, 7:80], wsc[:, 2:4, :, 3:76], ALU.add),
                         reads=['wsc'], writes=['wsb'])
                    P.op('pool', lambda e: e.tensor_tensor(wsc[:, 3:4, :, 15:80], wsb[:, 3:4, :, 15:80], wsb[:, 3:4, :, 7:72], ALU.add),
                         reads=['wsb'], writes=['wsc'])
                    fin = [wsb, wsc, wsb, wsc]
                    for g in range(4):
                        P.op('pool', lambda e, g=g: e.tensor_scalar_mul(fin[g][:, g, :, 16:80], fin[g][:, g, :, 16:80],
                                                                        1.0 / WINDOWS[g]),
                             reads=['wsb', 'wsc'], writes=['wsb', 'wsc'])
                        if cg == 0:
                            P.op('pool', lambda e, g=g: e.tensor_tensor(fin[g][:, g, :, 16:32], fin[g][:, g, :, 16:32],
                                                                        bcast(rat[:, g * 16:(g + 1) * 16], 1, 2), ALU.mult),
                                 reads=['wsb', 'wsc', 'cst'], writes=['wsb', 'wsc'])
                        P.op('pool', lambda e, g=g: e.tensor_tensor(pooled[:, g, :, :], fin[g][:, g, :, 16:80],
                                                                    upb[:, g, :, 16:80], ALU.subtract),
                             reads=['wsb', 'wsc'] + U, writes=['pooled'])

                    for b in range(12):
                        pc = bank4(4 + b // 4)[:, b % 4, :].rearrange("p (s t) -> p s t", s=2)
                        for k in range(4):
                            P.op('pe', lambda e, b=b, k=k, pc=pc: e.matmul(pc, convd[:, b, k, :], qkvb[:, b, :, k:k + 64],
                                                                           start=(k == 0), stop=(k == 3)),
                                 reads=['convd', 'qkvb_h', f'qkvb_c{b // 4}'], writes=psn(4 + b // 4))
                    marks['silu'] = len(P.ops)
                    P.op('act', lambda e: e.activation(qks[:], bank8(4), AF.Silu), reads=psn(4, 5), writes=['qks'])
                    P.op('act', lambda e: e.activation(vT[:], bank4(6), AF.Silu), reads=psn(6), writes=['vT'])
                    P.op('dve', lambda e: e.tensor_tensor(hn[:], qks[:], qks[:], ALU.mult), reads=['qks'], writes=['hn'])
                    pl = bank8(0)
                    for j in range(8):
                        P.op('pe', lambda e, j=j: e.matmul(pl[:, j, :], onesb[:], hn[:, j, :], start=True, stop=True),
                             reads=['hn', 'onesb'], writes=psn(0, 1))
                    P.op('act', lambda e: e.activation(sq[:], pl, AF.Ln, bias=EPS), reads=psn(0, 1), writes=['sq'])
                    P.op('act', lambda e: e.activation(sq[:], sq[:], AF.Exp, scale=-0.5), reads=['sq'], writes=['sq'])
                    P.op('dve', lambda e: e.scalar_tensor_tensor(kq[:, :, 1, :], qks[:, 0:4, :], 128.0 ** -0.5, sq[:, 0:4, :],
                                                                 ALU.mult, ALU.mult), reads=['qks', 'sq'], writes=['kq_q'])
                    P.op('dve', lambda e: e.tensor_tensor(kT[:], qks[:, 4:8, :], sq[:, 4:8, :], ALU.mult),
                         reads=['qks', 'sq'], writes=['kT'])

                    marks['sa0'] = len(P.ops)
                    b_ = sm[:, S_BA:S_BA + 4]
                    a_ = sm[:, S_BA + 4:S_BA + 8]
                    beta = sm[:, S_BETA:S_BETA + 4]
                    g_ = sm[:, S_G:S_G + 4]
                    t1 = sm[:, S_T1:S_T1 + 4]
                    t2 = sm[:, S_T2:S_T2 + 4]
                    SM = ['sm']
                    P.op('act', lambda e: e.activation(t1, b_, AF.Exp, scale=-1.0), reads=['sm_ba'], writes=SM)
                    P.op('dve', lambda e: e.tensor_scalar_add(t1, t1, 1.0), reads=SM, writes=SM)
                    P.op('dve', lambda e: e.reciprocal(beta, t1), reads=SM, writes=SM)
                    P.op('dve', lambda e: e.tensor_tensor(t1, a_, prm[:, P_DTB:P_DTB + 4], ALU.add), reads=['sm_ba', 'prm'] + SM, writes=SM)
                    P.op('dve', lambda e: e.tensor_scalar_mul(t2, t1, -1.0), reads=SM, writes=SM)
                    P.op('dve', lambda e: e.tensor_tensor(t2, t2, t1, ALU.max), reads=SM, writes=SM)
                    P.op('act', lambda e: e.activation(t2, t2, AF.Exp, scale=-1.0), reads=SM, writes=SM)
                    P.op('act', lambda e: e.activation(t2, t2, AF.Ln, bias=1.0), reads=SM, writes=SM)
                    P.op('dve', lambda e: e.tensor_scalar_max(t1, t1, 0.0), reads=SM, writes=SM)
                    P.op('dve', lambda e: e.tensor_tensor(t1, t1, t2, ALU.add), reads=SM, writes=SM)
                    P.op('dve', lambda e: e.tensor_tensor(g_, t1, negA[:], ALU.mult), reads=SM + ['negA'], writes=SM)
                    marks['sa1'] = len(P.ops)
                    pg = bank(2)[:, 136:144]
                    P.op('pe', lambda e: e.matmul(pg[:, 0:4], mtri, g_, start=True, stop=True), reads=SM + ['cst'], writes=psn(2))
                    P.op('pe', lambda e: e.matmul(pg[:, 4:8], mblk, g_, start=True, stop=True), reads=SM + ['cst'], writes=psn(2))
                    gc = sm[:, S_GC:S_GC + 4]
                    gl = sm[:, S_GL:S_GL + 4]
                    sc1 = sm[:, S_SC1:S_SC1 + 4]
                    sc2 = sm[:, S_SC2:S_SC2 + 4]
                    P.op('dve', lambda e: e.tensor_copy(sm[:, S_GC:S_GC + 8], pg), reads=psn(2), writes=SM)
                    P.op('act', lambda e: e.activation(sc1, gc, AF.Exp), reads=SM, writes=SM)
                    P.op('dve', lambda e: e.tensor_tensor(sc1, sc1, beta, ALU.mult), reads=SM, writes=SM)
                    P.op('dve', lambda e: e.tensor_tensor(sc2, gl, gc, ALU.subtract), reads=SM, writes=SM)
                    P.op('act', lambda e: e.activation(sc2, sc2, AF.Exp), reads=SM, writes=SM)
                    for h in range(4):
                        P.op('dve', lambda e, h=h: e.tensor_scalar_mul(gB[:, h, :], ones, sm[:, S_G + h:S_G + h + 1]),
                             reads=SM + ['cst'], writes=['gB'])
                    pdf = bank4(3)
                    pgr = bank4(7)
                    for h in range(4):
                        P.op('pe', lambda e, h=h: e.matmul(pdf[:, h, :], gB[:, h, :], mtri, start=True, stop=False),
                             reads=['gB', 'cst'], writes=psn(3))
                        P.op('pe', lambda e, h=h: e.matmul(pdf[:, h, :], nmtri, gB[:, h, :], start=False, stop=True),
                             reads=['gB', 'cst'], writes=psn(3))
                    for h in range(4):
                        P.op('pe', lambda e, h=h: e.matmul(pgr[:, h, :], gB[:, h, :], mtri, start=True, stop=True),
                             reads=['gB', 'cst'], writes=psn(7))
                    P.op('dve', lambda e: e.tensor_scalar_min(dexp[:], pdf, 0.0), reads=psn(3), writes=['dexp'])
                    P.op('act', lambda e: e.activation(dexp[:], dexp[:], AF.Exp), reads=['dexp'], writes=['dexp'])
                    P.op('act', lambda e: e.activation(egrow[:], pgr, AF.Exp), reads=psn(7), writes=['egrow'])
                    P.op('pool', lambda e: e.tensor_tensor(E2[:, :, 0, :], dexp[:], bcast(msn, 1, 4), ALU.mult),
                         reads=['dexp', 'cst'], writes=['xt'])
                    P.op('pool', lambda e: e.tensor_tensor(E2[:, :, 1, :], dexp[:], bcast(mi, 1, 4), ALU.mult),
                         reads=['dexp', 'cst'], writes=['xt'])
                    marks['brow'] = len(P.ops)
                    for h in range(4):
                        P.op('dve', lambda e, h=h: e.tensor_scalar_mul(gB[:, h, :], ones, sm[:, S_BETA + h:S_BETA + h + 1]),
                             reads=SM + ['cst'], writes=['gB'])
                    pbr = bank4(3)
                    for h in range(4):
                        P.op('pe', lambda e, h=h: e.matmul(pbr[:, h, :], gB[:, h, :], ident, start=True, stop=True),
                             reads=['gB', 'cst'], writes=psn(3))
                    P.op('dve', lambda e: e.tensor_tensor(kq[:, :, 0, :], pbr, kT[:], ALU.mult), reads=psn(3) + ['kT'], writes=['kq_k'])
                    P.op('pool', lambda e: e.tensor_tensor(qg[:], kq[:, :, 1, :], egrow[:], ALU.mult), reads=['kq_q', 'egrow'], writes=['qg'])
                    ptb = bankb(0)
                    for h in range(4):
                        P.op('pe', lambda e, h=h: e.transpose(ptb[:, h, :], kT[:, h, :], identb[:]), reads=['kT', 'identb'], writes=psn(0))
                    for h in range(4):
                        P.op('pe', lambda e, h=h: e.transpose(ptb[:, 4 + h, :], vT[:, h, :], identb[:]), reads=['vT', 'identb'], writes=psn(0))
                    for h in range(4):
                        P.op('dve', lambda e, h=h: e.tensor_scalar_mul(kbg[:, h, :], ptb[:, h, :], sm[:, S_SC1 + h:S_SC1 + h + 1]),
                             reads=psn(0) + SM, writes=['kbg'])
                        P.op('dve', lambda e, h=h: e.tensor_scalar_mul(kdec[:, h, :], ptb[:, h, :], sm[:, S_SC2 + h:S_SC2 + h + 1]),
                             reads=psn(0) + SM, writes=['kdec'])
                        for s in range(2):
                            rows = slice(s * 64, (s + 1) * 64)
                            P.op('dve', lambda e, h=h, s=s, rows=rows: e.tensor_scalar_mul(
                                vbblk[rows, h, s * 128:(s + 1) * 128], ptb[rows, 4 + h, :], sm[rows, S_BETA + h:S_BETA + h + 1]),
                                 reads=psn(0) + SM, writes=['vbblk'])
                    pla = PS[:, 4:6, :].rearrange("p b (h n) -> p (b h) n", h=2)
                    for h in range(4):
                        P.op('pe', lambda e, h=h: e.matmul(pla[:, h, :], kT[:, h, :], kq[:, h, :, :].rearrange("p a n -> p (a n)"),
                                                           start=True, stop=True),
                             reads=['kT', 'kq_k', 'kq_q'], writes=psn(4, 5))
                    P.op('dve', lambda e: e.tensor_tensor(LA[:].rearrange("p h a n -> p h (a n)"), pla,
                                                          E2[:].rearrange("p h a n -> p h (a n)"), ALU.mult),
                         reads=psn(4, 5) + ['xt'], writes=['LA'])
                    ptn = bankb(1)
                    for h in range(4):
                        P.op('pe', lambda e, h=h: e.transpose(ptn[:, h, :], LA[:, h, 0, :], identb[:]), reads=['LA', 'identb'], writes=psn(1))
                    P.op('act', lambda e: e.copy(N0[:], ptn[:, 0:4, :]), reads=psn(1), writes=['N0'])
                    P.op('dve', lambda e: e.tensor_tensor(PTa[:], LA[:, :, 0, :], bcast(identb[:], 1, 4), ALU.add),
                         reads=['LA', 'identb'], writes=['PTa'])
                    Nprev = lambda h: N0[:, h, :]
                    NTprev = lambda h: LA[:, h, 0, :]
                    prevreg = ['N0', 'LA']
                    PTcur, PTnext = PTa, PTb
                    PTr = {id(PTa): 'PTa', id(PTb): 'PTb'}
                    NAs = [NAa, NAb]
                    NAr = ['NAa', 'NAb']
                    for l in range(1, 6):
                        psq = PS[:, 6:8, :].rearrange("p b (h n) -> p (b h) n", h=2)
                        for h in range(4):
                            P.op('pe', lambda e, h=h, Np=Nprev, NTp=NTprev: e.matmul(psq[:, h, 0:128], NTp(h), Np(h), start=True, stop=True),
                                 reads=prevreg, writes=psn(6, 7))
                            if l < 5:
                                P.op('pe', lambda e, h=h, Np=Nprev, NTp=NTprev: e.matmul(psq[:, h, 128:256], Np(h), NTp(h), start=True, stop=True),
                                     reads=prevreg, writes=psn(6, 7))
                        NAn = NAs[l % 2]
                        NAnr = NAr[l % 2]
                        if l < 5:
                            P.op('act', lambda e, NAn=NAn: e.copy(NAn[:].rearrange("p h a n -> p h (a n)"), psq), reads=psn(6, 7), writes=[NAnr])
                        else:
                            P.op('act', lambda e, NAn=NAn: e.copy(NAn[:, :, 0, :], psq[:, :, 0:128]), reads=psn(6, 7), writes=[NAnr])
                        Nprev = (lambda NAn: (lambda h: NAn[:, h, 0, :]))(NAn)
                        NTprev = (lambda NAn: (lambda h: NAn[:, h, 1, :]))(NAn)
                        prevreg = [NAnr]
                        ppt = bank4(3)
                        for h in range(4):
                            P.op('pe', lambda e, h=h, PTc=PTcur: e.matmul(ppt[:, h, :], identb[:], PTc[:, h, :], start=True, stop=False),
                                 reads=[PTr[id(PTcur)], 'identb'], writes=psn(3))
                            P.op('pe', lambda e, h=h, PTc=PTcur, Np=Nprev: e.matmul(ppt[:, h, :], Np(h), PTc[:, h, :], start=False, stop=True),
                                 reads=[PTr[id(PTcur)], NAnr], writes=psn(3))
                        P.op('dve', lambda e, PTn=PTnext: e.tensor_copy(PTn[:], ppt), reads=psn(3), writes=[PTr[id(PTnext)]])
                        PTcur, PTnext = PTnext, PTcur
                    TT = PTcur
                    TTr = PTr[id(TT)]
                    pw = bank4(0)
                    for h in range(4):
                        P.op('pe', lambda e, h=h, TT=TT: e.matmul(pw[:, h, :], kbg[:, h, :], TT[:, h, :], start=True, stop=True),
                             reads=['kbg', TTr], writes=psn(0))
                    P.op('act', lambda e: e.mul(negwT[:], pw, -1.0), reads=psn(0), writes=['negwT'])
                    pvn = PS[:, 4:6, :].rearrange("p b (h n) -> p (b h) n", h=2)
                    for h in range(4):
                        P.op('pe', lambda e, h=h, TT=TT: e.matmul(pvn[:, h, :], TT[:, h, :], vbblk[:, h, :], start=True, stop=False),
                             reads=['vbblk', TTr], writes=psn(4, 5))
                        P.op('pe', lambda e, h=h: e.matmul(pvn[:, h, :], negwT[:, h, :], Sbf[:, h, :], start=False, stop=True),
                             reads=['negwT', 'Sbf'], writes=psn(4, 5))
                    P.op('dve', lambda e: e.tensor_copy(vnblk[0:64, :, 0:128], pvn[0:64, :, 0:128]), reads=psn(4, 5), writes=['vnblk'])
                    P.op('act', lambda e: e.copy(vnblk[64:128, :, 128:256], pvn[64:128, :, 128:256]), reads=psn(4, 5), writes=['vnblk'])
                    po = PS[:, 6:8, :].rearrange("p b (h n) -> p (b h) n", h=2)
                    for h in range(4):
                        P.op('pe', lambda e, h=h: e.matmul(po[:, h, :], qg[:, h, :], Sbf[:, h, :], start=True, stop=False),
                             reads=['qg', 'Sbf'], writes=psn(6, 7))
                        P.op('pe', lambda e, h=h: e.matmul(po[:, h, :], LA[:, h, 1, :], vnblk[:, h, :], start=False, stop=True),
                             reads=['LA', 'vnblk'], writes=psn(6, 7))
                    psu = PS[:, 0:2, :].rearrange("p b (h n) -> p (b h) n", h=2)
                    for h in range(4):
                        P.op('pe', lambda e, h=h: e.matmul(psu[:, h, :], kdec[:, h, :], vnblk[:, h, :], start=True, stop=True),
                             reads=['kdec', 'vnblk'], writes=psn(0, 1))
                    for h in range(4):
                        for s in range(2):
                            col = s * 64 + 63
                            P.op('dve', lambda e, h=h, s=s, col=col: e.scalar_tensor_tensor(
                                S32[:, h, s * 128:(s + 1) * 128], S32[:, h, s * 128:(s + 1) * 128], egrow[:, h, col:col + 1],
                                psu[:, h, s * 128:(s + 1) * 128], ALU.mult, ALU.add),
                                 reads=['S32', 'egrow'] + psn(0, 1), writes=['S32'])
                    P.op('act', lambda e: e.copy(Sbf[:], S32[:]), reads=['S32'], writes=['Sbf'])
                    P.op('dve', lambda e: e.tensor_copy(otok[0:64, :, :], po[0:64, :, 0:128]), reads=psn(6, 7), writes=['qks'])
                    P.op('act', lambda e: e.copy(otok[64:128, :, :], po[64:128, :, 128:256]), reads=psn(6, 7), writes=['qks'])
                    marks['tail'] = len(P.ops)
                    oss = sm[:, S_OSS:S_OSS + 4]
                    ors = sm[:, S_ORS:S_ORS + 4]
                    P.op('pool', lambda e: e.tensor_tensor(osq[:], otok[:], otok[:], ALU.mult), reads=['qks'], writes=['dexp'])
                    P.op('dve', lambda e: e.tensor_reduce(oss, osq[:], AX.X, ALU.add), reads=['dexp'], writes=['sm_o'])
                    P.op('act', lambda e: e.activation(ors, oss, AF.Ln, bias=EPS, scale=1.0 / 128), reads=['sm_o'], writes=['sm_o'])
                    P.op('act', lambda e: e.activation(ors, ors, AF.Exp, scale=-0.5), reads=['sm_o'], writes=['sm_o'])
                    P.op('pool', lambda e: e.tensor_tensor(osq[:], zt[:], bcast(prm[:, P_GO:P_GO + 128], 1, 4), ALU.mult),
                         reads=['zt', 'prm', 'sm_o'], writes=['dexp'])
                    for h in range(4):
                        P.op('dve', lambda e, h=h: e.scalar_tensor_tensor(ydn[:, h, :], otok[:, h, :], sm[:, S_ORS + h:S_ORS + h + 1],
                                                                          osq[:, h, :], ALU.mult, ALU.mult),
                             reads=['qks', 'sm_o', 'dexp'], writes=['ydn'])
                    pyt = bankb(2)
                    for h in range(4):
                        P.op('pe', lambda e, h=h: e.transpose(pyt[:, h, :], ydn[:, h, :], identb[:]), reads=['ydn', 'identb'], writes=psn(2))
                    P.op('act', lambda e: e.copy(catT[:, 4:8, :], pyt[:, 0:4, :]), reads=psn(2), writes=['catT_d'])
                    pp = bank4(3)
                    for g in range(4):
                        P.op('pe', lambda e, g=g: e.matmul(pp[:, g, :], WA[:, WPOOL0 + g * 128:WPOOL0 + (g + 1) * 128],
                                                           pooled[:, g, :, :].rearrange("p s t -> p (s t)"), start=True, stop=True),
                             reads=['pooled'] + wa_regs(WPOOL0, WPOOL0 + 512), writes=psn(3))
                    for g in range(4):
                        P.op('dve', lambda e, g=g: e.tensor_scalar_mul(catT[:, g, :], pp[:, g, :], prm[:, P_PSC + g:P_PSC + g + 1]),
                             reads=psn(3) + ['prm'], writes=['catT_p'])
                    pout = bank8(4)
                    for j in range(8):
                        for k in range(8):
                            P.op('pe', lambda e, j=j, k=k: e.matmul(pout[:, j, :], W1024(WOUT0, k, j * 128, (j + 1) * 128), catT[:, k, :],
                                                                    start=(k == 0), stop=(k == 7)),
                                 reads=['catT_d', 'catT_p'] + W1024_regs(WOUT0, k, j * 128, (j + 1) * 128), writes=psn(4, 5))
                    P.op('dve', lambda e, hview=hview: e.tensor_tensor(hview, pout.rearrange("p k (s t) -> p k s t", s=2), hview, ALU.add),
                         reads=psn(4, 5) + hregs, writes=hregs)
                    tiles.append((P.ops, marks))
                    P.ops = saved_ops
                    if stop_after == ('p1tile', cg):
                        break
                for ti, (L, mk) in enumerate(tiles):
                    A = L[mk['sa0']:mk['sa1']]
                    if ti == 0:
                        P.ops.extend(L[:mk['head_end']])
                    P.ops.extend(L[mk['head_end']:mk['silu']])
                    X = L[mk['silu']:mk['sa0']]
                    Y = A + L[mk['sa1']:mk['brow']]
                    nx, ny = len(X), len(Y)
                    i = j = 0
                    while i < nx or j < ny:
                        if j < ny and (i >= nx or j * nx <= i * ny):
                            P.ops.append(Y[j])
                            j += 1
                        else:
                            P.ops.append(X[i])
                            i += 1
                    P.ops.extend(L[mk['brow']:mk['tail']])
                    if ti + 1 < len(tiles):
                        L2, mk2 = tiles[ti + 1]
                        P.ops.extend(L2[:mk2['head_end']])
                    P.ops.extend(L[mk['tail']:])
                if stop_after is not None and stop_after[0] == 'p1tile':
                    dump('hT', hT[:, :, :, 0:(stop_after[1] % CPH + 1) * 64], [f"h{s}_{c}" for s in range(2) for c in range(CPH)])
                    dump('S32', S32[:], ['S32'])
                    dump('catT', catT[:], ['catT_d', 'catT_p'])
                    dump('otok', otok, ['qks'])
                    dump('kq', kq[:], ['kq_k', 'kq_q'])
                    dump('LA', LA[:], ['LA'])
                    dump('TT', TT[:], [TTr])
                    dump('sm', sm[:], ['sm', 'sm_o', 'sm_ba'])
                    dump('kT', kT[:], ['kT'])
                    dump('vT', vT[:], ['vT'])
                    dump('vnblk', vnblk[:], ['vnblk'])
                    dump('pooled', pooled[:], ['pooled'])
                phase_barrier('p1')
        def phase1_dump(half):
            if stop_after[0] == 'p1':
                dump('hT', hT[:], [f"h{s}_{c}" for s in range(2) for c in range(CPH)])

        def phase2(half):
            WXA, WXB = 0, 8192
            load_w1024(wxk_d, WXA, 'g_wxa')
            load_w1024(wxv_d, WXB, 'g_wxb')
            with ExitStack() as p2:
                KT = sbuf(p2, "KT", [128, 2, 8, 256], BF16)
                Vm = sbuf(p2, "Vm", [128, 2, 2, 1024], BF16)
                with ExitStack() as p2a:
                    memt = sbuf(p2a, "memt", [128, D], F32)
                    mjunk = sbuf(p2a, "mjunk", [128, D], F32)
                    mn = sbuf(p2a, "mn", [128, D], BF16)
                    mT = sbuf(p2a, "mT", [128, 8, 2, 256], BF16)
                    msm = sbuf(p2a, "msm", [128, 4], F32)
                    for s in range(2):
                        for mt in range(2):
                            P.dma('sp', lambda e, s=s, mt=mt: e.dma_start(out=memt[:], in_=mem_d[s, mt * 128:(mt + 1) * 128, :]),
                                  writes=['memt'], group='g_mem')
                            P.op('act', lambda e: e.activation(mjunk[:], memt[:], AF.Square), reads=['memt'], writes=['mjunk'])
                            P.op('dve', lambda e: e.tensor_reduce(msm[:, 0:1], mjunk[:], AX.X, ALU.add), reads=['mjunk'], writes=['msm'])
                            P.op('act', lambda e: e.activation(msm[:, 1:2], msm[:, 0:1], AF.Ln, bias=EPS, scale=1.0 / D), reads=['msm'], writes=['msm'])
                            P.op('act', lambda e: e.activation(msm[:, 1:2], msm[:, 1:2], AF.Exp, scale=-0.5), reads=['msm'], writes=['msm'])
                            P.op('dve', lambda e: e.tensor_scalar_mul(mn[:], memt[:], msm[:, 1:2]), reads=['memt', 'msm'], writes=['mn'])
                            pmt = bankb(0)
                            for k in range(8):
                                P.op('pe', lambda e, k=k: e.transpose(pmt[:, k, :], mn[:, k * 128:(k + 1) * 128], identb[:]),
                                     reads=['mn', 'identb'], writes=psn(0))
                            for k in range(8):
                                P.op('dve', lambda e, k=k, s=s, mt=mt: e.tensor_scalar_mul(mT[:, k, s, mt * 128:(mt + 1) * 128], pmt[:, k, :],
                                                                                           prm[:, P_GMEM + k:P_GMEM + k + 1]),
                                     reads=psn(0) + ['prm'], writes=['mT'])
                    nb = 0
                    for s in range(2):
                        for dj in range(8):
                            b = 1 + (nb % 4)
                            nb += 1
                            pk = bank(b)[:, 0:256]
                            for k in range(8):
                                P.op('pe', lambda e, k=k, s=s, dj=dj, pk=pk: e.matmul(pk, W1024(WXA, k, dj * 128, (dj + 1) * 128), mT[:, k, s, :],
                                                                                      start=(k == 0), stop=(k == 7)),
                                     reads=['mT'] + W1024_regs(WXA, k, dj * 128, (dj + 1) * 128), writes=psn(b))
                            if nb % 2:
                                P.op('act', lambda e, s=s, dj=dj, pk=pk: e.copy(KT[:, s, dj, :], pk), reads=psn(b), writes=['KT'])
                            else:
                                P.op('dve', lambda e, s=s, dj=dj, pk=pk: e.tensor_copy(KT[:, s, dj, :], pk), reads=psn(b), writes=['KT'])
                    for s in range(2):
                        for mt in range(2):
                            for hf in range(2):
                                b = 1 + (nb % 4)
                                nb += 1
                                pv = bank(b)
                                for k in range(8):
                                    P.op('pe', lambda e, k=k, s=s, mt=mt, hf=hf, pv=pv: e.matmul(
                                        pv, mT[:, k, s, mt * 128:(mt + 1) * 128], W1024(WXB, k, hf * 512, (hf + 1) * 512),
                                        start=(k == 0), stop=(k == 7)),
                                         reads=['mT'] + W1024_regs(WXB, k, hf * 512, (hf + 1) * 512), writes=psn(b))
                                if nb % 2:
                                    P.op('act', lambda e, s=s, mt=mt, hf=hf, pv=pv: e.copy(Vm[:, s, mt, hf * 512:(hf + 1) * 512], pv),
                                         reads=psn(b), writes=['Vm'])
                                else:
                                    P.op('dve', lambda e, s=s, mt=mt, hf=hf, pv=pv: e.tensor_copy(Vm[:, s, mt, hf * 512:(hf + 1) * 512], pv),
                                         reads=psn(b), writes=['Vm'])
                    phase_barrier('p2a')
                load_w1024(wxq_d, WXA, 'g_wxa')
                load_w1024(wxo_d, WXB, 'g_wxb')
                TQ = 256
                sq2 = sbuf(p2, "sq2", [128, 8, TQ], F32)
                rstd2 = sbuf(p2, "rstd2", [128, TQ], F32)
                hn2 = sbuf(p2, "hn2", [128, 8, TQ], BF16)
                qT2 = sbuf(p2, "qT2", [128, 8, TQ], BF16)
                oT2 = sbuf(p2, "oT2", [128, 8, TQ], BF16)
                Eb = sbuf(p2, "Eb", [128, 4, 256], F32)
                Pm = sbuf(p2, "Pm", [128, 4, 256], BF16)
                PTt = sbuf(p2, "PTt", [128, 8, 128], BF16)
                ssm = sbuf(p2, "ssm", [128, 16], F32)
                for s in range(2):
                    for qi in range(TPH // TQ):
                        q0 = qi * TQ
                        hr = hreg(s, q0, q0 + TQ)
                        hv = hT[:, :, s, q0:q0 + TQ]
                        P.op('act', lambda e, hv=hv: e.activation(sq2[:], hv, AF.Square), reads=hr, writes=['sq2'])
                        pst = bank(0)[:, 0:TQ]
                        for k in range(8):
                            P.op('pe', lambda e, k=k, pst=pst: e.matmul(pst, ones, sq2[:, k, :], start=(k == 0), stop=(k == 7)),
                                 reads=['sq2', 'cst'], writes=psn(0))
                        P.op('act', lambda e, pst=pst: e.activation(rstd2[:], pst, AF.Ln, bias=EPS, scale=1.0 / D), reads=psn(0), writes=['rstd2'])
                        P.op('act', lambda e: e.activation(rstd2[:], rstd2[:], AF.Exp, scale=-0.5), reads=['rstd2'], writes=['rstd2'])
                        for k in range(8):
                            P.op('dve', lambda e, k=k, hv=hv: e.scalar_tensor_tensor(hn2[:, k, :], hv[:, k, :], prm[:, P_GXAT + k:P_GXAT + k + 1],
                                                                                     rstd2[:], ALU.mult, ALU.mult),
                                 reads=hr + ['prm', 'rstd2'], writes=['hn2'])
                        pq = PS[:, 0:4, :].rearrange("p b (j n) -> p (b j) n", j=2)
                        for dj in range(8):
                            for k in range(8):
                                P.op('pe', lambda e, dj=dj, k=k: e.matmul(pq[:, dj, :], W1024(WXA, k, dj * 128, (dj + 1) * 128), hn2[:, k, :],
                                                                          start=(k == 0), stop=(k == 7)),
                                     reads=['hn2'] + W1024_regs(WXA, k, dj * 128, (dj + 1) * 128), writes=psn(dj // 2))
                        P.op('act', lambda e: e.mul(qT2[:, 0:4, :], pq[:, 0:4, :], 1.0 / 16), reads=psn(0, 1), writes=['qT2a'])
                        P.op('dve', lambda e: e.tensor_scalar_mul(qT2[:, 4:8, :], pq[:, 4:8, :], 1.0 / 16), reads=psn(2, 3), writes=['qT2b'])
                        for sub in range(TQ // 128):
                            tsl = slice(sub * 128, (sub + 1) * 128)
                            psc = PS[:, 4:6, :].rearrange("p b (h n) -> p (b h) n", h=2)
                            for h in range(4):
                                for j in range(2):
                                    P.op('pe', lambda e, h=h, j=j, tsl=tsl, s=s: e.matmul(psc[:, h, :], qT2[:, 2 * h + j, tsl], KT[:, s, 2 * h + j, :],
                                                                                     start=(j == 0), stop=(j == 1)),
                                         reads=['qT2a', 'qT2b', 'KT'], writes=psn(4, 5))
                            P.op('dve', lambda e: e.tensor_reduce(ssm[:, 0:4], psc, AX.X, ALU.max, negate=True), reads=psn(4, 5), writes=['ssm'])
                            P.op('dve', lambda e: e.tensor_tensor(Eb[:], psc, bcast(ssm[:, 0:4], 2, 256), ALU.add), reads=psn(4, 5) + ['ssm'], writes=['Eb'])
                            P.op('act', lambda e: e.activation(Eb[:], Eb[:], AF.Exp), reads=['Eb'], writes=['Eb'])
                            P.op('dve', lambda e: e.tensor_reduce(ssm[:, 4:8], Eb[:], AX.X, ALU.add), reads=['Eb'], writes=['ssm2'])
                            P.op('dve', lambda e: e.reciprocal(ssm[:, 8:12], ssm[:, 4:8]), reads=['ssm2'], writes=['ssm3'])
                            P.op('dve', lambda e: e.tensor_tensor(Pm[:], Eb[:], bcast(ssm[:, 8:12], 2, 256), ALU.mult), reads=['Eb', 'ssm3'], writes=['Pm'])
                            ptp = bankb(6)
                            for h in range(4):
                                for mt in range(2):
                                    P.op('pe', lambda e, h=h, mt=mt: e.transpose(ptp[:, h * 2 + mt, :], Pm[:, h, mt * 128:(mt + 1) * 128], identb[:]),
                                         reads=['Pm', 'identb'], writes=psn(6))
                            P.op('act', lambda e: e.copy(PTt[:], ptp), reads=psn(6), writes=['PTt'])
                            pov = bank8(0) if sub == 0 else bank8(2)
                            pr = psn(0, 1) if sub == 0 else psn(2, 3)
                            for h in range(4):
                                for j in range(2):
                                    for mt in range(2):
                                        c0 = h * 256 + j * 128
                                        P.op('pe', lambda e, h=h, j=j, mt=mt, c0=c0, pov=pov, s=s: e.matmul(
                                            pov[:, 2 * h + j, :], Vm[:, s, mt, c0:c0 + 128], PTt[:, h * 2 + mt, :], start=(mt == 0), stop=(mt == 1)),
                                             reads=['Vm', 'PTt'], writes=pr)
                            P.op('dve', lambda e, tsl=tsl, pov=pov: e.tensor_copy(oT2[:, :, tsl], pov), reads=pr, writes=[f'oT2_{sub}'])
                        pxo = PS[:, 4:8, :].rearrange("p b (j n) -> p (b j) n", j=2)
                        for dj in range(8):
                            for k in range(8):
                                P.op('pe', lambda e, dj=dj, k=k: e.matmul(pxo[:, dj, :], W1024(WXB, k, dj * 128, (dj + 1) * 128), oT2[:, k, :],
                                                                          start=(k == 0), stop=(k == 7)),
                                     reads=['oT2_0', 'oT2_1'] + W1024_regs(WXB, k, dj * 128, (dj + 1) * 128), writes=psn(4 + dj // 2))
                        P.op('dve', lambda e, hv=hv: e.tensor_tensor(hv, pxo, hv, ALU.add), reads=psn(4, 5, 6, 7) + hr, writes=hr)
                phase_barrier('p2')
        def phase2_dump(half):
            dump('hT', hT[:], [f"h{s}_{c}" for s in range(2) for c in range(CPH)])

        def phase3(half):
            with ExitStack() as p3:
                hn3 = sbuf(p3, "hn3", [128, 8, 2, TPH], BF16)
                sq3 = sbuf(p3, "sq3", [128, 8, 128], F32)
                rstd3 = sbuf(p3, "rstd3", [128, 128], F32)
                rl = sbuf(p3, "rl", [128, 512], F32)
                aT = [sbuf(p3, f"aT{i}", [128, 4, 512], BF16) for i in range(2)]
                of = sbuf(p3, "of", [128, 8, 128], F32)
                osb = [sbuf(p3, f"osb{i}", [128, D], F32) for i in range(2)]

                def norm3(s, q0, n, gbase, outk, outregs, sqb, rsb):
                    hr = hreg(s, q0, q0 + n)
                    hv = hT[:, :, s, q0:q0 + n]
                    P.op('act', lambda e: e.activation(sqb[:, :, 0:n], hv, AF.Square), reads=hr, writes=['sq3'])
                    pst = bank(0)[:, 0:n]
                    for k in range(8):
                        P.op('pe', lambda e, k=k: e.matmul(pst, ones, sqb[:, k, 0:n], start=(k == 0), stop=(k == 7)),
                             reads=['sq3', 'cst'], writes=psn(0))
                    P.op('act', lambda e: e.activation(rsb[:, 0:n], pst, AF.Ln, bias=EPS, scale=1.0 / D), reads=psn(0), writes=['rstd3'])
                    P.op('act', lambda e: e.activation(rsb[:, 0:n], rsb[:, 0:n], AF.Exp, scale=-0.5), reads=['rstd3'], writes=['rstd3'])
                    for k in range(8):
                        P.op('dve', lambda e, k=k: e.scalar_tensor_tensor(outk(k), hv[:, k, :], prm[:, gbase + k:gbase + k + 1], rsb[:, 0:n],
                                                                          ALU.mult, ALU.mult),
                             reads=hr + ['prm', 'rstd3'], writes=outregs)

                for s in range(2):
                    for qi in range(TPH // 128):
                        q0 = qi * 128
                        norm3(s, q0, 128, P_GMLP, lambda k, s=s, q0=q0: hn3[:, k, s, q0:q0 + 128], [f'hn3_{s}_{qi // 4}'], sq3, rstd3)
                NFC = DFF // 512
                SL_UP = [0, 8192]
                SL_DN = [4096, 12288]
                wupv = wup_d.rearrange("(k p) n -> p k n", p=128)
                wdnv = wdn_d.rearrange("(j p) n -> p j n", p=128)

                def load_chunk(fc):
                    sl = fc % 2
                    for k in range(8):
                        lo = SL_UP[sl] + k * 512
                        P.dma('pool', lambda e, k=k, lo=lo, fc=fc: e.dma_start(out=WA[:, lo:lo + 512], in_=wupv[:, k, fc * 512:(fc + 1) * 512]),
                              writes=wa_regs(lo, lo + 512), group=f'g_up{sl}', tok=False)
                    for j in range(4):
                        lo = SL_DN[sl] + j * 1024
                        P.dma('pool', lambda e, j=j, lo=lo, fc=fc: e.dma_start(out=WA[:, lo:lo + 1024], in_=wdnv[:, fc * 4 + j, :]),
                              writes=wa_regs(lo, lo + 1024), group=f'g_dn{sl}', tok=False)

                load_chunk(0)
                nt = 0
                for fc in range(NFC):
                    if fc + 1 < NFC:
                        load_chunk(fc + 1)
                    sl = fc % 2
                    for s in range(2):
                        for qi in range(TPH // 512):
                            q0 = qi * 512
                            hr = hreg(s, q0, q0 + 512)
                            a = aT[nt % 2]
                            ar = f'aT{nt % 2}'
                            nt += 1
                            for fb in range(4):
                                for k in range(8):
                                    lo = SL_UP[sl] + k * 512 + fb * 128
                                    P.op('pe', lambda e, fb=fb, k=k, lo=lo, s=s, q0=q0: e.matmul(bank(fb), WA[:, lo:lo + 128], hn3[:, k, s, q0:q0 + 512],
                                                                                                 start=(k == 0), stop=(k == 7)),
                                         reads=[f'hn3_{s}_{qi}'] + wa_regs(lo, lo + 128), writes=psn(fb))
                                P.op('act', lambda e, fb=fb: e.activation(rl[:], bank(fb), AF.Relu), reads=psn(fb), writes=['rl'])
                                P.op('act', lambda e, fb=fb, a=a: e.activation(a[:, fb, :], rl[:], AF.Square), reads=['rl'], writes=[ar])
                            for dj in range(8):
                                b = 4 + dj % 4
                                for fb in range(4):
                                    lo = SL_DN[sl] + fb * 1024 + dj * 128
                                    P.op('pe', lambda e, fb=fb, dj=dj, lo=lo, a=a, b=b: e.matmul(bank(b), WA[:, lo:lo + 128], a[:, fb, :],
                                                                                                 start=(fb == 0), stop=(fb == 3)),
                                         reads=[ar] + wa_regs(lo, lo + 128), writes=psn(b))
                                hv = hT[:, dj, s, q0:q0 + 512]
                                P.op('dve', lambda e, hv=hv, b=b: e.tensor_tensor(hv, bank(b), hv, ALU.add), reads=psn(b) + hr, writes=hr)
                no = 0
                for s in range(2):
                    for ti in range(TPH // 128):
                        q0 = ti * 128
                        norm3(s, q0, 128, P_GFIN, lambda k: of[:, k, :], ['of'], sq3, rstd3)
                        pto = bank8(2)
                        for k in range(8):
                            P.op('pe', lambda e, k=k: e.transpose(pto[:, k, :], of[:, k, :], ident), reads=['of', 'cst'], writes=psn(2, 3))
                        ob = osb[no % 2]
                        obr = f'osb{no % 2}'
                        no += 1
                        P.op('act', lambda e, ob=ob: e.copy(ob[:].rearrange("p (k n) -> p k n", k=8), pto), reads=psn(2, 3), writes=[obr])
                        tg = half * TPH + q0
                        P.dma('sp', lambda e, ob=ob, s=s, tg=tg: e.dma_start(out=out_d[s, tg:tg + 128, :], in_=ob[:]),
                              reads=[obr], writes=['out'], group=f'g_{obr}')
                phase_barrier('p3')

        for half in range(NHALF):
            phase1(half)
            if stop_after is not None and stop_after[0] in ('p1tile', 'p1'):
                phase1_dump(half)
                break
            phase2(half)
            if stop_after is not None and stop_after[0] == 'p2':
                phase2_dump(half)
                break
            phase3(half)

        fg = [g for g in ('g_osb0', 'g_osb1') if stop_after is None]
        fg += [f'dbg{i + 1}' for i in range(dbg_n[0])]
        if max_ops is not None:
            P.ops = P.ops[:max_ops]
            fg = []
        st = P.finalize(final_groups=fg)
        build_nc.stats = st
    return nc


def make_consts():
    c = np.zeros((128, CW), np.float32)
    idx = np.arange(128)
    blk = idx // 64
    same = blk[:, None] == blk[None, :]
    c[:, C_ID:C_ID + 128] = np.eye(128)
    c[:, C_ONE:C_ONE + 128] = 1.0
    tri = same & (idx[:, None] <= idx[None, :])
    c[:, C_MTRI:C_MTRI + 128] = tri
    c[:, C_NMTRI:C_NMTRI + 128] = -tri.astype(np.float32)
    c[:, C_MBLK:C_MBLK + 128] = same
    c[:, C_MSN:C_MSN + 128] = -(same & (idx[None, :] > idx[:, None])).astype(np.float32)
    c[:, C_MI:C_MI + 128] = (same & (idx[None, :] >= idx[:, None]))
    for g, w in enumerate(WINDOWS):
        t = np.arange(16)
        c[:, C_RAT + g * 16:C_RAT + (g + 1) * 16] = (w / np.minimum(t + 1, w))[None, :]
    return c


def make_prm(inp):
    p = np.zeros((128, PW), np.float32)

    def colvec(v):
        return np.asarray(v, np.float32).reshape(8, 128).T

    p[:, P_GMIX:P_GMIX + 8] = colvec(inp['norm_mix_g'][0])
    p[:, P_GXAT:P_GXAT + 8] = colvec(inp['norm_xattn_g'][0])
    p[:, P_GMEM:P_GMEM + 8] = colvec(inp['mem_norm_g'][0])
    p[:, P_GMLP:P_GMLP + 8] = colvec(inp['norm_mlp_g'][0])
    p[:, P_GFIN:P_GFIN + 8] = colvec(inp['final_norm_g'])
    p[:, P_PSC:P_PSC + 4] = np.asarray(inp['pool_scale'][0], np.float32).reshape(4, 128).T
    cw = np.asarray(inp['conv_w'][0], np.float32)
    p[:, P_CONV:P_CONV + 48] = cw.reshape(4, 12, 128).transpose(2, 1, 0).reshape(128, 48)
    p[:, P_ALOG:P_ALOG + 4] = np.broadcast_to(np.asarray(inp['a_log'][0], np.float32)[None, :], (128, 4))
    p[:, P_DTB:P_DTB + 4] = np.broadcast_to(np.asarray(inp['dt_bias'][0], np.float32)[None, :], (128, 4))
    p[:, P_GO:P_GO + 128] = np.broadcast_to(np.asarray(inp['dn_out_norm_g'][0], np.float32)[None, :], (128, 128))
    return p


def make_in_maps(inp, n_cores=8):
    prm = make_prm(inp)
    cst = make_consts()
    shared = dict(
        w_in=np.ascontiguousarray(inp['w_in'][0], dtype=np.float32),
        w_pool=np.ascontiguousarray(inp['w_pool'][0], dtype=np.float32),
        w_out=np.ascontiguousarray(inp['w_out'][0], dtype=np.float32),
        w_xq=np.ascontiguousarray(inp['w_xq'][0], dtype=np.float32),
        w_xk=np.ascontiguousarray(inp['w_xk'][0], dtype=np.float32),
        w_xv=np.ascontiguousarray(inp['w_xv'][0], dtype=np.float32),
        w_xo=np.ascontiguousarray(inp['w_xo'][0], dtype=np.float32),
        w_up=np.ascontiguousarray(inp['w_up'][0], dtype=np.float32),
        w_down=np.ascontiguousarray(inp['w_down'][0], dtype=np.float32),
        prm=prm, cst=cst)
    x = np.asarray(inp['x'], np.float32)
    mem = np.asarray(inp['mem'], np.float32)
    maps = []
    for i in range(n_cores):
        m = dict(shared)
        m['x'] = np.ascontiguousarray(x[NSEQ * i:NSEQ * (i + 1)])
        m['mem'] = np.ascontiguousarray(mem[NSEQ * i:NSEQ * (i + 1)])
        maps.append(m)
    return maps


def kernel(**inputs):
    nc = build_nc()
    maps = make_in_maps(inputs, 8)
    res = run_bass_kernel_spmd(nc, maps, core_ids=list(range(8)))
    outs = [np.asarray(r['out'], np.float32) for r in res.results]
    return np.concatenate(outs, axis=0)
```

```python
import numpy as np
from contextlib import ExitStack
import concourse.bass as bass
import concourse.mybir as mybir
from concourse.bass_utils import run_bass_kernel_spmd

F32 = mybir.dt.float32
BF16 = mybir.dt.bfloat16
AF = mybir.ActivationFunctionType
ALU = mybir.AluOpType
AX = mybir.AxisListType

D = 1024
SEQ = 2048
NSEQ = 2
MEM = 256
INC = 2568
DFF = 4096
EPS = 1e-6
NHALF = 2
CPH = 16
TPH = CPH * 64
WINDOWS = (2, 4, 8, 16)

P_GMIX, P_GXAT, P_GMEM, P_GMLP, P_GFIN = 0, 8, 16, 24, 32
P_PSC = 40
P_CONV = 44
P_ALOG = 92
P_DTB = 96
P_GO = 100
PW = 228
C_ID, C_ONE, C_MTRI, C_NMTRI, C_MBLK, C_MSN, C_MI, C_RAT = 0, 128, 256, 384, 512, 640, 768, 896
CW = 960


class Prog:
    def __init__(self, nc, es):
        self.nc = nc
        self.es = es
        self.ops = []
        self.engs = {'pe': nc.tensor, 'act': nc.scalar, 'dve': nc.vector, 'pool': nc.gpsimd, 'sp': nc.sync}
        self.token = None

    def op(self, engine, fn, reads=(), writes=(), tok=True):
        r = tuple(reads)
        if tok and self.token is not None:
            r = r + (self.token,)
        self.ops.append(dict(e=engine, fn=fn, r=r, w=tuple(writes), dma=None))

    def dma(self, queue, fn, reads=(), writes=(), group=None, tok=True):
        r = tuple(reads)
        if tok and self.token is not None:
            r = r + (self.token,)
        self.ops.append(dict(e=queue, fn=fn, r=r, w=tuple(writes), dma=group))

    def barrier(self, fn):
        self.ops.append(dict(e='dve', fn=fn, r=(), w=(self.token,), dma=None))

    def finalize(self, final_groups=()):
        nc = self.nc
        ops = self.ops
        last_writer = {}
        readers = {}
        for i, o in enumerate(ops):
            deps = set()
            for r in o['r']:
                if r in last_writer:
                    deps.add(last_writer[r])
            for w in o['w']:
                if w in last_writer:
                    deps.add(last_writer[w])
                deps.update(readers.get(w, ()))
            deps.discard(i)
            best = {}
            for d in deps:
                od = ops[d]
                key = ('g', od['dma']) if od['dma'] is not None else ('e', od['e'])
                if key not in best or best[key] < d:
                    best[key] = d
            o['deps'] = set(best.values())
            for r in o['r']:
                readers.setdefault(r, []).append(i)
            for w in o['w']:
                last_writer[w] = i
                readers[w] = []

        def skip(od, o):
            return od['dma'] is None and od['e'] == 'pe' and o['e'] == 'pe' and o['dma'] is None

        need = [False] * len(ops)
        for i, o in enumerate(ops):
            for d in o['deps']:
                if not skip(ops[d], o):
                    need[d] = True
        sems = {e: self.es.enter_context(nc.semaphore('sem_' + e)) for e in self.engs}
        gsem = {}
        gcount = {}
        cnt = {e: 0 for e in self.engs}
        for i, o in enumerate(ops):
            if o['dma'] is not None:
                g = o['dma']
                if g not in gsem:
                    gsem[g] = self.es.enter_context(nc.semaphore('dg_' + g))
                    gcount[g] = 0
                gcount[g] += 16
                o['sig'] = (gsem[g], gcount[g])
            elif need[i]:
                cnt[o['e']] += 1
                o['sig'] = (sems[o['e']], cnt[o['e']])
            else:
                o['sig'] = None
            o['gsnap'] = None
        running = {}
        snaps = []
        for i, o in enumerate(ops):
            snaps.append(dict(running))
            if o['dma'] is not None:
                running[o['dma']] = o['sig'][1]
        waited = {e: {} for e in self.engs}
        nwaits = 0
        for i, o in enumerate(ops):
            e = o['e']
            eng = self.engs[e]
            wl = {}
            for d in o['deps']:
                od = ops[d]
                if skip(od, o):
                    continue
                s, v = od['sig']
                if od['dma'] is not None:
                    v = snaps[i][od['dma']]
                key = id(s)
                if key not in wl or wl[key][1] < v:
                    wl[key] = (s, v)
            for key, (s, v) in wl.items():
                if waited[e].get(key, 0) >= v:
                    continue
                waited[e][key] = v
                eng.wait_ge(s, v)
                nwaits += 1
            inst = o['fn'](eng)
            if o['dma'] is not None:
                inst.then_inc(o['sig'][0], 16)
            elif o['sig'] is not None:
                inst.then_inc(o['sig'][0], 1)
        for g in final_groups:
            nc.sync.wait_ge(gsem[g], gcount[g])
        self.stats = dict(n_ops=len(ops), n_waits=nwaits, sig=cnt, ngroups=len(gsem))
        return self.stats


def bcast(ap, pos, n):
    lst = [list(x) for x in ap.ap]
    lst.insert(pos, [0, n])
    return bass.AP(ap.tensor, ap.offset, lst)


def build_nc(stop_after=None, dbg=None, max_ops=None):
    nc = bass.Bass("TRN2", target_bir_lowering=False)

    def dr(name, shape, kind="ExternalInput", dt=F32):
        return nc.dram_tensor(name, list(shape), dt, kind=kind).ap()

    x_d = dr("x", [NSEQ, SEQ, D])
    mem_d = dr("mem", [NSEQ, MEM, D])
    win_d = dr("w_in", [D, INC])
    wpool_d = dr("w_pool", [4, 128, 128])
    wout_d = dr("w_out", [D, D])
    wxq_d = dr("w_xq", [D, D])
    wxk_d = dr("w_xk", [D, D])
    wxv_d = dr("w_xv", [D, D])
    wxo_d = dr("w_xo", [D, D])
    wup_d = dr("w_up", [D, DFF])
    wdn_d = dr("w_down", [DFF, D])
    prm_d = dr("prm", [128, PW])
    cst_d = dr("cst", [128, CW])
    out_d = dr("out", [NSEQ, SEQ, D], kind="ExternalOutput")
    dbg_d = {}
    if dbg:
        for k, (shp, dtn) in dbg.items():
            dbg_d[k] = dr("dbg_" + k, shp, kind="ExternalOutput", dt=(BF16 if dtn == 'bf16' else F32))

    es = ExitStack()
    with es:
        P = Prog(nc, es)

        uid = [0]

        def sbuf(stack, name, shape, dt):
            uid[0] += 1
            return stack.enter_context(nc.sbuf_tensor(f"sb{uid[0]}_{name}", list(shape), dt))

        hT = sbuf(es, "hT", [128, 8, NSEQ, TPH], F32)
        cst = sbuf(es, "cst", [128, CW], F32)
        prm = sbuf(es, "prm", [128, PW], F32)
        identb = sbuf(es, "identb", [128, 128], BF16)
        onesb = sbuf(es, "onesb", [128, 128], BF16)
        convd = sbuf(es, "convd", [128, 12, 4, 128], BF16)
        WA = sbuf(es, "warena", [128, 29312], BF16)
        S32 = sbuf(es, "S32", [128, 4, 256], F32)
        Sbf = sbuf(es, "Sbf", [128, 4, 256], BF16)
        qkvh = sbuf(es, "qkvh", [128, 12, 2, 3], BF16)
        uph = sbuf(es, "uph", [128, 4, 2, 16], F32)
        negA = sbuf(es, "negA", [128, 4], F32)
        dummy = sbuf(es, "dummy", [128, 2], F32)
        PS = es.enter_context(nc.psum_tensor("PS", [128, 8, 512], F32))

        ident = cst[:, C_ID:C_ID + 128]
        ones = cst[:, C_ONE:C_ONE + 128]
        mtri = cst[:, C_MTRI:C_MTRI + 128]
        nmtri = cst[:, C_NMTRI:C_NMTRI + 128]
        mblk = cst[:, C_MBLK:C_MBLK + 128]
        msn = cst[:, C_MSN:C_MSN + 128]
        mi = cst[:, C_MI:C_MI + 128]
        rat = cst[:, C_RAT:C_RAT + 64]

        def bank(b):
            return PS[:, b, :]

        def bank4(b):
            return PS[:, b, :].rearrange("p (j n) -> p j n", j=4)

        def bank8(b):
            return PS[:, b:b + 2, :].rearrange("p b (j n) -> p (b j) n", j=4)

        def bankb(b):
            return PS[:, b, :].bitcast(BF16).rearrange("p (j n) -> p j n", j=8)

        def psn(*bs):
            return [f"ps{b}" for b in bs]

        def wa_regs(lo, hi):
            return [f"wa{b}" for b in range(lo // 2048, (hi - 1) // 2048 + 1)]

        def hreg(s, t0, t1):
            return [f"h{s}_{c}" for c in range(t0 // 64, (t1 - 1) // 64 + 1)]

        dbg_n = [0]

        def dump(name, ap, reads):
            if name in dbg_d:
                dbg_n[0] += 1
                P.dma('sp', lambda e: e.dma_start(out=dbg_d[name], in_=ap), reads=reads, writes=['dbg_' + name],
                      group=f'dbg{dbg_n[0]}')

        P.dma('sp', lambda e: e.dma_start(out=cst[:], in_=cst_d[:, :]), writes=['cst'], group='c0')
        P.dma('sp', lambda e: e.dma_start(out=prm[:], in_=prm_d[:, :]), writes=['prm'], group='c1')
        P.op('dve', lambda e: e.tensor_copy(identb[:], ident), reads=['cst'], writes=['identb'])
        P.op('dve', lambda e: e.tensor_copy(onesb[:], ones), reads=['cst'], writes=['onesb'])
        for b in range(12):
            for k in range(4):
                col = P_CONV + b * 4 + k
                P.op('dve', lambda e, b=b, k=k, col=col: e.tensor_scalar_mul(convd[:, b, k, :], ident, prm[:, col:col + 1]),
                     reads=['cst', 'prm'], writes=['convd'])
        P.op('act', lambda e: e.activation(negA[:], prm[:, P_ALOG:P_ALOG + 4], AF.Exp), reads=['prm'], writes=['negA'])
        P.op('dve', lambda e: e.tensor_scalar_mul(negA[:], negA[:], -1.0), reads=['negA'], writes=['negA'])
        P.op('dve', lambda e: e.memset(S32[:], 0.0), writes=['S32'])
        P.op('dve', lambda e: e.memset(Sbf[:], 0.0), writes=['Sbf'])
        P.op('dve', lambda e: e.memset(qkvh[:], 0.0), writes=['qkvh'])
        P.op('dve', lambda e: e.memset(uph[:], 0.0), writes=['uph'])

        WIN0 = 0
        WOUT0 = 8 * INC
        WPOOL0 = WOUT0 + 8 * D
        winv = win_d.rearrange("(k p) n -> p k n", p=128)

        def load_w1024(dram, base, gname):
            v = dram.rearrange("(k p) n -> p k n", p=128)
            for k in range(8):
                lo = base + k * 1024
                P.dma('pool', lambda e, k=k, lo=lo: e.dma_start(out=WA[:, lo:lo + 1024], in_=v[:, k, :]),
                      writes=wa_regs(lo, lo + 1024), group=gname, tok=False)

        def load_p1_weights():
            for k in range(8):
                lo = WIN0 + k * INC
                P.dma('pool', lambda e, k=k, lo=lo: e.dma_start(out=WA[:, lo:lo + INC], in_=winv[:, k, :],
                                                                 max_dma_last_dim=4096),
                      writes=wa_regs(lo, lo + INC), group='g_win', tok=False)
            load_w1024(wout_d, WOUT0, 'g_wout')
            P.dma('pool', lambda e: e.dma_start(out=WA[:, WPOOL0:WPOOL0 + 512].rearrange("p (g d) -> p g d", g=4),
                                                in_=wpool_d.rearrange("g c d -> c g d")),
                  writes=wa_regs(WPOOL0, WPOOL0 + 512), group='g_wpool', tok=False)

        def Win(k, lo, hi):
            return WA[:, WIN0 + k * INC + lo:WIN0 + k * INC + hi]

        def Win_regs(k, lo, hi):
            return wa_regs(WIN0 + k * INC + lo, WIN0 + k * INC + hi)

        def W1024(base, k, lo, hi):
            return WA[:, base + k * 1024 + lo:base + k * 1024 + hi]

        def W1024_regs(base, k, lo, hi):
            return wa_regs(base + k * 1024 + lo, base + k * 1024 + hi)

        def phase_barrier(name):
            P.barrier(lambda e: e.memset(dummy[:], 0.0))

        P.token = 'tok'

        def rms_rows(stack_bufs, src_ps_or_sb, n, gcol, out_hn, reads, ps_stat_bank, sq, rstd, src_k):
            raise NotImplementedError

        def phase1(half):
            load_p1_weights()
            with ExitStack() as p1:
                xt = sbuf(p1, "xt", [128, D], F32)
                sq = sbuf(p1, "sq", [128, 8, 128], F32)
                rstd = sbuf(p1, "rstd", [128, 128], F32)
                hn = sbuf(p1, "hn", [128, 8, 128], BF16)
                qkvb = sbuf(p1, "qkvb", [128, 12, 2, 67], BF16)
                upb = sbuf(p1, "upb", [128, 4, 2, 80], F32)
                wsb = sbuf(p1, "wsb", [128, 4, 2, 80], F32)
                wsc = sbuf(p1, "wsc", [128, 4, 2, 80], F32)
                pooled = sbuf(p1, "pooled", [128, 4, 2, 64], BF16)
                catT = sbuf(p1, "catT", [128, 8, 128], BF16)
                qks = sbuf(p1, "qks", [128, 8, 128], F32)
                vT = sbuf(p1, "vT", [128, 4, 128], BF16)
                kq = sbuf(p1, "kq", [128, 4, 2, 128], BF16)
                kT = sbuf(p1, "kT", [128, 4, 128], BF16)
                sm = sbuf(p1, "sm", [128, 64], F32)
                gB = sbuf(p1, "gB", [128, 4, 128], F32)
                dexp = sbuf(p1, "dexp", [128, 4, 128], F32)
                E2 = xt[:].rearrange("p (h a n) -> p h a n", h=4, a=2)
                egrow = sbuf(p1, "egrow", [128, 4, 128], F32)
                qg = sbuf(p1, "qg", [128, 4, 128], BF16)
                LA = sbuf(p1, "LA", [128, 4, 2, 128], BF16)
                N0 = sbuf(p1, "N0", [128, 4, 128], BF16)
                NAa = sbuf(p1, "NAa", [128, 4, 2, 128], BF16)
                NAb = sbuf(p1, "NAb", [128, 4, 2, 128], BF16)
                PTa = sbuf(p1, "PTa", [128, 4, 128], BF16)
                PTb = sbuf(p1, "PTb", [128, 4, 128], BF16)
                negwT = sbuf(p1, "negwT", [128, 4, 128], BF16)
                vbblk = sbuf(p1, "vbblk", [128, 4, 256], BF16)
                kbg = sbuf(p1, "kbg", [128, 4, 128], BF16)
                kdec = sbuf(p1, "kdec", [128, 4, 128], BF16)
                vnblk = sbuf(p1, "vnblk", [128, 4, 256], BF16)
                zt = sbuf(p1, "zt", [128, 4, 128], BF16)
                ydn = sbuf(p1, "ydn", [128, 4, 128], BF16)
                otok = qks[:, 0:4, :]
                osq = dexp

                S_BA, S_BETA, S_G, S_T1, S_T2, S_GC, S_GL, S_SC1, S_SC2, S_OSS, S_ORS = 0, 8, 12, 16, 20, 24, 28, 32, 36, 40, 44

                P.op('dve', lambda e: e.memset(vbblk[:], 0.0), writes=['vbblk'])
                P.op('dve', lambda e: e.memset(sm[:], 0.0), writes=['sm', 'sm_o', 'sm_ba'])
                P.op('dve', lambda e: e.memset(vnblk[:], 0.0), writes=['vnblk'])

                tiles = []
                for c in range(CPH):
                    saved_ops = P.ops
                    P.ops = []
                    marks = {}
                    cg = half * CPH + c
                    t0 = c * 64
                    hregs = [f"h0_{c}", f"h1_{c}"]
                    hview = hT[:, :, :, t0:t0 + 64]
                    for s in range(NSEQ):
                        P.dma('sp', lambda e, s=s, cg=cg: e.dma_start(out=xt[s * 64:(s + 1) * 64, :],
                                                                      in_=x_d[s, cg * 64:(cg + 1) * 64, :]),
                              writes=['xt'], group='g_x')
                    pT = bank8(0)
                    for k in range(8):
                        P.op('pe', lambda e, k=k: e.transpose(pT[:, k, :], xt[:, k * 128:(k + 1) * 128], ident),
                             reads=['xt', 'cst'], writes=psn(0, 1))
                    P.op('act', lambda e, hview=hview: e.copy(hview, pT.rearrange("p k (s t) -> p k s t", s=2)),
                         reads=psn(0, 1), writes=hregs)
                    P.op('act', lambda e: e.activation(hn[:], pT, AF.Square), reads=psn(0, 1), writes=['hn'])
                    pst = bank(2)[:, 0:128]
                    for k in range(8):
                        P.op('pe', lambda e, k=k: e.matmul(pst, onesb[:], hn[:, k, :], start=(k == 0), stop=(k == 7)),
                             reads=['hn', 'onesb'], writes=psn(2))
                    P.op('act', lambda e: e.activation(rstd[:], pst, AF.Ln, bias=EPS, scale=1.0 / D),
                         reads=psn(2), writes=['rstd'])
                    P.op('act', lambda e: e.activation(rstd[:], rstd[:], AF.Exp, scale=-0.5), reads=['rstd'], writes=['rstd'])
                    for k in range(8):
                        P.op('dve', lambda e, k=k: e.scalar_tensor_tensor(hn[:, k, :], pT[:, k, :],
                                                                          prm[:, P_GMIX + k:P_GMIX + k + 1], rstd[:],
                                                                          ALU.mult, ALU.mult),
                             reads=psn(0, 1) + ['prm', 'rstd'], writes=['hn'])
                    marks['head_end'] = len(P.ops)
                    for grp in range(4):
                        pb = bank4(3 + grp)
                        for j in range(4):
                            col = (grp * 4 + j) * 128
                            for k in range(8):
                                P.op('pe', lambda e, pb=pb, j=j, k=k, col=col: e.matmul(
                                    pb[:, j, :], Win(k, col, col + 128), hn[:, k, :], start=(k == 0), stop=(k == 7)),
                                     reads=['hn'] + Win_regs(k, col, col + 128), writes=psn(3 + grp))
                    for k in range(8):
                        P.op('pe', lambda e, k=k: e.matmul(bank(7), hn[:, k, :], Win(k, 2048, 2560),
                                                           start=(k == 0), stop=(k == 7)),
                             reads=['hn'] + Win_regs(k, 2048, 2560), writes=psn(7))
                    pba = bank(2)[:, 128:136]
                    for k in range(8):
                        P.op('pe', lambda e, k=k: e.matmul(pba, hn[:, k, :], Win(k, 2560, 2568),
                                                           start=(k == 0), stop=(k == 7)),
                             reads=['hn'] + Win_regs(k, 2560, 2568), writes=psn(2))
                    P.op('pool', lambda e: e.tensor_copy(upb[:, :, :, 0:16], uph[:]), reads=['uph'], writes=['upb_h'])
                    P.op('pool', lambda e: e.tensor_copy(qkvb[:, :, :, 0:3], qkvh[:]), reads=['qkvh'], writes=['qkvb_h'])
                    P.op('act', lambda e: e.copy(upb[:, :, :, 16:80], bank4(3).rearrange("p g (s t) -> p g s t", s=2)),
                         reads=psn(3), writes=['upb_c'])
                    for j in range(3):
                        eng = 'dve' if j != 1 else 'act'
                        if eng == 'dve':
                            P.op('dve', lambda e, j=j: e.tensor_copy(qkvb[:, 4 * j:4 * j + 4, :, 3:67],
                                                                     bank4(4 + j).rearrange("p g (s t) -> p g s t", s=2)),
                                 reads=psn(4 + j), writes=[f'qkvb_c{j}'])
                        else:
                            P.op('act', lambda e, j=j: e.copy(qkvb[:, 4 * j:4 * j + 4, :, 3:67],
                                                              bank4(4 + j).rearrange("p g (s t) -> p g s t", s=2)),
                                 reads=psn(4 + j), writes=[f'qkvb_c{j}'])
                    P.op('act', lambda e: e.activation(zt[:], bank4(7), AF.Silu), reads=psn(7), writes=['zt'])
                    P.op('dve', lambda e: e.tensor_copy(sm[:, S_BA:S_BA + 8], pba), reads=psn(2), writes=['sm_ba'])
                    P.op('pool', lambda e: e.tensor_copy(uph[:], upb[:, :, :, 64:80]), reads=['upb_c'], writes=['uph'])
                    P.op('pool', lambda e: e.tensor_copy(qkvh[:], qkvb[:, :, :, 64:67]),
                         reads=['qkvb_c0', 'qkvb_c1', 'qkvb_c2'], writes=['qkvh'])

                    U = ['upb_h', 'upb_c']
                    P.op('pool', lambda e: e.tensor_tensor(wsb[:, :, :, 1:80], upb[:, :, :, 1:80], upb[:, :, :, 0:79], ALU.add),
                         reads=U, writes=['wsb'])
                    P.op('pool', lambda e: e.tensor_tensor(wsc[:, 1:4, :, 3:80], wsb[:, 1:4, :, 3:80], wsb[:, 1:4, :, 1:78], ALU.add),
                         reads=['wsb'], writes=['wsc'])
                    P.op('pool', lambda e: e.tensor_tensor(wsb[:, 2:4, :, 7:80], wsc[:, 2:4, :, 7:80], wsc[:, 2:4, :, 3:76], ALU.add),
                         reads=['wsc'], writes=['wsb'])
                    P.op('pool', lambda e: e.tensor_tensor(wsc[:, 3:4, :, 15:80], wsb[:, 3:4, :, 15:80], wsb[:, 3:4, :, 7:72], ALU.add),
                         reads=['wsb'], writes=['wsc'])
                    fin = [wsb, wsc, wsb, wsc]
                    for g in range(4):
                        P.op('pool', lambda e, g=g: e.tensor_scalar_mul(fin[g][:, g, :, 16:80], fin[g][:, g, :, 16:80],
                                                                        1.0 / WINDOWS[g]),
                             reads=['wsb', 'wsc'], writes=['wsb', 'wsc'])
                        if cg == 0:
                            P.op('pool', lambda e, g=g: e.tensor_tensor(fin[g][:, g, :, 16:32], fin[g][:, g, :, 16:32],
                                                                        bcast(rat[:, g * 16:(g + 1) * 16], 1, 2), ALU.mult),
                                 reads=['wsb', 'wsc', 'cst'], writes=['wsb', 'wsc'])
                        P.op('pool', lambda e, g=g: e.tensor_tensor(pooled[:, g, :, :], fin[g][:, g, :, 16:80],
                                                                    upb[:, g, :, 16:80], ALU.subtract),
                             reads=['wsb', 'wsc'] + U, writes=['pooled'])

                    for b in range(12):
                        pc = bank4(4 + b // 4)[:, b % 4, :].rearrange("p (s t) -> p s t", s=2)
                        for k in range(4):
                            P.op('pe', lambda e, b=b, k=k, pc=pc: e.matmul(pc, convd[:, b, k, :], qkvb[:, b, :, k:k + 64],
                                                                           start=(k == 0), stop=(k == 3)),
                                 reads=['convd', 'qkvb_h', f'qkvb_c{b // 4}'], writes=psn(4 + b // 4))
                    marks['silu'] = len(P.ops)
                    P.op('act', lambda e: e.activation(qks[:], bank8(4), AF.Silu), reads=psn(4, 5), writes=['qks'])
                    P.op('act', lambda e: e.activation(vT[:], bank4(6), AF.Silu), reads=psn(6), writes=['vT'])
                    P.op('dve', lambda e: e.tensor_tensor(hn[:], qks[:], qks[:], ALU.mult), reads=['qks'], writes=['hn'])
                    pl = bank8(0)
                    for j in range(8):
                        P.op('pe', lambda e, j=j: e.matmul(pl[:, j, :], onesb[:], hn[:, j, :], start=True, stop=True),
                             reads=['hn', 'onesb'], writes=psn(0, 1))
                    P.op('act', lambda e: e.activation(sq[:], pl, AF.Ln, bias=EPS), reads=psn(0, 1), writes=['sq'])
                    P.op('act', lambda e: e.activation(sq[:], sq[:], AF.Exp, scale=-0.5), reads=['sq'], writes=['sq'])
                    P.op('dve', lambda e: e.scalar_tensor_tensor(kq[:, :, 1, :], qks[:, 0:4, :], 128.0 ** -0.5, sq[:, 0:4, :],
                                                                 ALU.mult, ALU.mult), reads=['qks', 'sq'], writes=['kq_q'])
                    P.op('dve', lambda e: e.tensor_tensor(kT[:], qks[:, 4:8, :], sq[:, 4:8, :], ALU.mult),
                         reads=['qks', 'sq'], writes=['kT'])

                    marks['sa0'] = len(P.ops)
                    b_ = sm[:, S_BA:S_BA + 4]
                    a_ = sm[:, S_BA + 4:S_BA + 8]
                    beta = sm[:, S_BETA:S_BETA + 4]
                    g_ = sm[:, S_G:S_G + 4]
                    t1 = sm[:, S_T1:S_T1 + 4]
                    t2 = sm[:, S_T2:S_T2 + 4]
                    SM = ['sm']
                    P.op('act', lambda e: e.activation(t1, b_, AF.Exp, scale=-1.0), reads=['sm_ba'], writes=SM)
                    P.op('dve', lambda e: e.tensor_scalar_add(t1, t1, 1.0), reads=SM, writes=SM)
                    P.op('dve', lambda e: e.reciprocal(beta, t1), reads=SM, writes=SM)
                    P.op('dve', lambda e: e.tensor_tensor(t1, a_, prm[:, P_DTB:P_DTB + 4], ALU.add), reads=['sm_ba', 'prm'] + SM, writes=SM)
                    P.op('dve', lambda e: e.tensor_scalar_mul(t2, t1, -1.0), reads=SM, writes=SM)
                    P.op('dve', lambda e: e.tensor_tensor(t2, t2, t1, ALU.max), reads=SM, writes=SM)
                    P.op('act', lambda e: e.activation(t2, t2, AF.Exp, scale=-1.0), reads=SM, writes=SM)
                    P.op('act', lambda e: e.activation(t2, t2, AF.Ln, bias=1.0), reads=SM, writes=SM)
                    P.op('dve', lambda e: e.tensor_scalar_max(t1, t1, 0.0), reads=SM, writes=SM)
                    P.op('dve', lambda e: e.tensor_tensor(t1, t1, t2, ALU.add), reads=SM, writes=SM)
                    P.op('dve', lambda e: e.tensor_tensor(g_, t1, negA[:], ALU.mult), reads=SM + ['negA'], writes=SM)
                    marks['sa1'] = len(P.ops)
                    pg = bank(2)[:, 136:144]
                    P.op('pe', lambda e: e.matmul(pg[:, 0:4], mtri, g_, start=True, stop=True), reads=SM + ['cst'], writes=psn(2))
                    P.op('pe', lambda e: e.matmul(pg[:, 4:8], mblk, g_, start=True, stop=True), reads=SM + ['cst'], writes=psn(2))
                    gc = sm[:, S_GC:S_GC + 4]
                    gl = sm[:, S_GL:S_GL + 4]
                    sc1 = sm[:, S_SC1:S_SC1 + 4]
                    sc2 = sm[:, S_SC2:S_SC2 + 4]
                    P.op('dve', lambda e: e.tensor_copy(sm[:, S_GC:S_GC + 8], pg), reads=psn(2), writes=SM)
                    P.op('act', lambda e: e.activation(sc1, gc, AF.Exp), reads=SM, writes=SM)
                    P.op('dve', lambda e: e.tensor_tensor(sc1, sc1, beta, ALU.mult), reads=SM, writes=SM)
                    P.op('dve', lambda e: e.tensor_tensor(sc2, gl, gc, ALU.subtract), reads=SM, writes=SM)
                    P.op('act', lambda e: e.activation(sc2, sc2, AF.Exp), reads=SM, writes=SM)
                    for h in range(4):
                        P.op('dve', lambda e, h=h: e.tensor_scalar_mul(gB[:, h, :], ones, sm[:, S_G + h:S_G + h + 1]),
                             reads=SM + ['cst'], writes=['gB'])
                    pdf = bank4(3)
                    pgr = bank4(7)
                    for h in range(4):
                        P.op('pe', lambda e, h=h: e.matmul(pdf[:, h, :], gB[:, h, :], mtri, start=True, stop=False),
                             reads=['gB', 'cst'], writes=psn(3))
                        P.op('pe', lambda e, h=h: e.matmul(pdf[:, h, :], nmtri, gB[:, h, :], start=False, stop=True),
                             reads=['gB', 'cst'], writes=psn(3))
                    for h in range(4):
                        P.op('pe', lambda e, h=h: e.matmul(pgr[:, h, :], gB[:, h, :], mtri, start=True, stop=True),
                             reads=['gB', 'cst'], writes=psn(7))
                    P.op('dve', lambda e: e.tensor_scalar_min(dexp[:], pdf, 0.0), reads=psn(3), writes=['dexp'])
                    P.op('act', lambda e: e.activation(dexp[:], dexp[:], AF.Exp), reads=['dexp'], writes=['dexp'])
                    P.op('act', lambda e: e.activation(egrow[:], pgr, AF.Exp), reads=psn(7), writes=['egrow'])
                    P.op('pool', lambda e: e.tensor_tensor(E2[:, :, 0, :], dexp[:], bcast(msn, 1, 4), ALU.mult),
                         reads=['dexp', 'cst'], writes=['xt'])
                    P.op('pool', lambda e: e.tensor_tensor(E2[:, :, 1, :], dexp[:], bcast(mi, 1, 4), ALU.mult),
                         reads=['dexp', 'cst'], writes=['xt'])
                    marks['brow'] = len(P.ops)
                    for h in range(4):
                        P.op('dve', lambda e, h=h: e.tensor_scalar_mul(gB[:, h, :], ones, sm[:, S_BETA + h:S_BETA + h + 1]),
                             reads=SM + ['cst'], writes=['gB'])
                    pbr = bank4(3)
                    for h in range(4):
                        P.op('pe', lambda e, h=h: e.matmul(pbr[:, h, :], gB[:, h, :], ident, start=True, stop=True),
                             reads=['gB', 'cst'], writes=psn(3))
                    P.op('dve', lambda e: e.tensor_tensor(kq[:, :, 0, :], pbr, kT[:], ALU.mult), reads=psn(3) + ['kT'], writes=['kq_k'])
                    P.op('pool', lambda e: e.tensor_tensor(qg[:], kq[:, :, 1, :], egrow[:], ALU.mult), reads=['kq_q', 'egrow'], writes=['qg'])
                    ptb = bankb(0)
                    for h in range(4):
                        P.op('pe', lambda e, h=h: e.transpose(ptb[:, h, :], kT[:, h, :], identb[:]), reads=['kT', 'identb'], writes=psn(0))
                    for h in range(4):
                        P.op('pe', lambda e, h=h: e.transpose(ptb[:, 4 + h, :], vT[:, h, :], identb[:]), reads=['vT', 'identb'], writes=psn(0))
                    for h in range(4):
                        P.op('dve', lambda e, h=h: e.tensor_scalar_mul(kbg[:, h, :], ptb[:, h, :], sm[:, S_SC1 + h:S_SC1 + h + 1]),
                             reads=psn(0) + SM, writes=['kbg'])
                        P.op('dve', lambda e, h=h: e.tensor_scalar_mul(kdec[:, h, :], ptb[:, h, :], sm[:, S_SC2 + h:S_SC2 + h + 1]),
                             reads=psn(0) + SM, writes=['kdec'])
                        for s in range(2):
                            rows = slice(s * 64, (s + 1) * 64)
                            P.op('dve', lambda e, h=h, s=s, rows=rows: e.tensor_scalar_mul(
                                vbblk[rows, h, s * 128:(s + 1) * 128], ptb[rows, 4 + h, :], sm[rows, S_BETA + h:S_BETA + h + 1]),
                                 reads=psn(0) + SM, writes=['vbblk'])
                    pla = PS[:, 4:6, :].rearrange("p b (h n) -> p (b h) n", h=2)
                    for h in range(4):
                        P.op('pe', lambda e, h=h: e.matmul(pla[:, h, :], kT[:, h, :], kq[:, h, :, :].rearrange("p a n -> p (a n)"),
                                                           start=True, stop=True),
                             reads=['kT', 'kq_k', 'kq_q'], writes=psn(4, 5))
                    P.op('dve', lambda e: e.tensor_tensor(LA[:].rearrange("p h a n -> p h (a n)"), pla,
                                                          E2[:].rearrange("p h a n -> p h (a n)"), ALU.mult),
                         reads=psn(4, 5) + ['xt'], writes=['LA'])
                    ptn = bankb(1)
                    for h in range(4):
                        P.op('pe', lambda e, h=h: e.transpose(ptn[:, h, :], LA[:, h, 0, :], identb[:]), reads=['LA', 'identb'], writes=psn(1))
                    P.op('act', lambda e: e.copy(N0[:], ptn[:, 0:4, :]), reads=psn(1), writes=['N0'])
                    P.op('dve', lambda e: e.tensor_tensor(PTa[:], LA[:, :, 0, :], bcast(identb[:], 1, 4), ALU.add),
                         reads=['LA', 'identb'], writes=['PTa'])
                    Nprev = lambda h: N0[:, h, :]
                    NTprev = lambda h: LA[:, h, 0, :]
                    prevreg = ['N0', 'LA']
                    PTcur, PTnext = PTa, PTb
                    PTr = {id(PTa): 'PTa', id(PTb): 'PTb'}
                    NAs = [NAa, NAb]
                    NAr = ['NAa', 'NAb']
                    for l in range(1, 6):
                        psq = PS[:, 6:8, :].rearrange("p b (h n) -> p (b h) n", h=2)
                        for h in range(4):
                            P.op('pe', lambda e, h=h, Np=Nprev, NTp=NTprev: e.matmul(psq[:, h, 0:128], NTp(h), Np(h), start=True, stop=True),
                                 reads=prevreg, writes=psn(6, 7))
                            if l < 5:
                                P.op('pe', lambda e, h=h, Np=Nprev, NTp=NTprev: e.matmul(psq[:, h, 128:256], Np(h), NTp(h), start=True, stop=True),
                                     reads=prevreg, writes=psn(6, 7))
                        NAn = NAs[l % 2]
                        NAnr = NAr[l % 2]
                        if l < 5:
                            P.op('act', lambda e, NAn=NAn: e.copy(NAn[:].rearrange("p h a n -> p h (a n)"), psq), reads=psn(6, 7), writes=[NAnr])
                        else:
                            P.op('act', lambda e, NAn=NAn: e.copy(NAn[:, :, 0, :], psq[:, :, 0:128]), reads=psn(6, 7), writes=[NAnr])
                        Nprev = (lambda NAn: (lambda h: NAn[:, h, 0, :]))(NAn)
                        NTprev = (lambda NAn: (lambda h: NAn[:, h, 1, :]))(NAn)
                        prevreg = [NAnr]
                        ppt = bank4(3)
                        for h in range(4):
                            P.op('pe', lambda e, h=h, PTc=PTcur: e.matmul(ppt[:, h, :], identb[:], PTc[:, h, :], start=True, stop=False),
                                 reads=[PTr[id(PTcur)], 'identb'], writes=psn(3))
                            P.op('pe', lambda e, h=h, PTc=PTcur, Np=Nprev: e.matmul(ppt[:, h, :], Np(h), PTc[:, h, :], start=False, stop=True),
                                 reads=[PTr[id(PTcur)], NAnr], writes=psn(3))
                        P.op('dve', lambda e, PTn=PTnext: e.tensor_copy(PTn[:], ppt), reads=psn(3), writes=[PTr[id(PTnext)]])
                        PTcur, PTnext = PTnext, PTcur
                    TT = PTcur
                    TTr = PTr[id(TT)]
                    pw = bank4(0)
                    for h in range(4):
                        P.op('pe', lambda e, h=h, TT=TT: e.matmul(pw[:, h, :], kbg[:, h, :], TT[:, h, :], start=True, stop=True),
                             reads=['kbg', TTr], writes=psn(0))
                    P.op('act', lambda e: e.mul(negwT[:], pw, -1.0), reads=psn(0), writes=['negwT'])
                    pvn = PS[:, 4:6, :].rearrange("p b (h n) -> p (b h) n", h=2)
                    for h in range(4):
                        P.op('pe', lambda e, h=h, TT=TT: e.matmul(pvn[:, h, :], TT[:, h, :], vbblk[:, h, :], start=True, stop=False),
                             reads=['vbblk', TTr], writes=psn(4, 5))
                        P.op('pe', lambda e, h=h: e.matmul(pvn[:, h, :], negwT[:, h, :], Sbf[:, h, :], start=False, stop=True),
                             reads=['negwT', 'Sbf'], writes=psn(4, 5))
                    P.op('dve', lambda e: e.tensor_copy(vnblk[0:64, :, 0:128], pvn[0:64, :, 0:128]), reads=psn(4, 5), writes=['vnblk'])
                    P.op('act', lambda e: e.copy(vnblk[64:128, :, 128:256], pvn[64:128, :, 128:256]), reads=psn(4, 5), writes=['vnblk'])
                    po = PS[:, 6:8, :].rearrange("p b (h n) -> p (b h) n", h=2)
                    for h in range(4):
                        P.op('pe', lambda e, h=h: e.matmul(po[:, h, :], qg[:, h, :], Sbf[:, h, :], start=True, stop=False),
                             reads=['qg', 'Sbf'], writes=psn(6, 7))
                        P.op('pe', lambda e, h=h: e.matmul(po[:, h, :], LA[:, h, 1, :], vnblk[:, h, :], start=False, stop=True),
                             reads=['LA', 'vnblk'], writes=psn(6, 7))
                    psu = PS[:, 0:2, :].rearrange("p b (h n) -> p (b h) n", h=2)
                    for h in range(4):
                        P.op('pe', lambda e, h=h: e.matmul(psu[:, h, :], kdec[:, h, :], vnblk[:, h, :], start=True, stop=True),
                             reads=['kdec', 'vnblk'], writes=psn(0, 1))
                    for h in range(4):
                        for s in range(2):
                            col = s * 64 + 63
                            P.op('dve', lambda e, h=h, s=s, col=col: e.scalar_tensor_tensor(
                                S32[:, h, s * 128:(s + 1) * 128], S32[:, h, s * 128:(s + 1) * 128], egrow[:, h, col:col + 1],
                                psu[:, h, s * 128:(s + 1) * 128], ALU.mult, ALU.add),
                                 reads=['S32', 'egrow'] + psn(0, 1), writes=['S32'])
                    P.op('act', lambda e: e.copy(Sbf[:], S32[:]), reads=['S32'], writes=['Sbf'])
                    P.op('dve', lambda e: e.tensor_copy(otok[0:64, :, :], po[0:64, :, 0:128]), reads=psn(6, 7), writes=['qks'])
                    P.op('act', lambda e: e.copy(otok[64:128, :, :], po[64:128, :, 128:256]), reads=psn(6, 7), writes=['qks'])
                    marks['tail'] = len(P.ops)
                    oss = sm[:, S_OSS:S_OSS + 4]
                    ors = sm[:, S_ORS:S_ORS + 4]
                    P.op('pool', lambda e: e.tensor_tensor(osq[:], otok[:], otok[:], ALU.mult), reads=['qks'], writes=['dexp'])
                    P.op('dve', lambda e: e.tensor_reduce(oss, osq[:], AX.X, ALU.add), reads=['dexp'], writes=['sm_o'])
                    P.op('act', lambda e: e.activation(ors, oss, AF.Ln, bias=EPS, scale=1.0 / 128), reads=['sm_o'], writes=['sm_o'])
                    P.op('act', lambda e: e.activation(ors, ors, AF.Exp, scale=-0.5), reads=['sm_o'], writes=['sm_o'])
                    P.op('pool', lambda e: e.tensor_tensor(osq[:], zt[:], bcast(prm[:, P_GO:P_GO + 128], 1, 4), ALU.mult),
                         reads=['zt', 'prm', 'sm_o'], writes=['dexp'])
                    for h in range(4):
                        P.op('dve', lambda e, h=h: e.scalar_tensor_tensor(ydn[:, h, :], otok[:, h, :], sm[:, S_ORS + h:S_ORS + h + 1],
                                                                          osq[:, h, :], ALU.mult, ALU.mult),
                             reads=['qks', 'sm_o', 'dexp'], writes=['ydn'])
                    pyt = bankb(2)
                    for h in range(4):
                        P.op('pe', lambda e, h=h: e.transpose(pyt[:, h, :], ydn[:, h, :], identb[:]), reads=['ydn', 'identb'], writes=psn(2))
                    P.op('act', lambda e: e.copy(catT[:, 4:8, :], pyt[:, 0:4, :]), reads=psn(2), writes=['catT_d'])
                    pp = bank4(3)
                    for g in range(4):
                        P.op('pe', lambda e, g=g: e.matmul(pp[:, g, :], WA[:, WPOOL0 + g * 128:WPOOL0 + (g + 1) * 128],
                                                           pooled[:, g, :, :].rearrange("p s t -> p (s t)"), start=True, stop=True),
                             reads=['pooled'] + wa_regs(WPOOL0, WPOOL0 + 512), writes=psn(3))
                    for g in range(4):
                        P.op('dve', lambda e, g=g: e.tensor_scalar_mul(catT[:, g, :], pp[:, g, :], prm[:, P_PSC + g:P_PSC + g + 1]),
                             reads=psn(3) + ['prm'], writes=['catT_p'])
                    pout = bank8(4)
                    for j in range(8):
                        for k in range(8):
                            P.op('pe', lambda e, j=j, k=k: e.matmul(pout[:, j, :], W1024(WOUT0, k, j * 128, (j + 1) * 128), catT[:, k, :],
                                                                    start=(k == 0), stop=(k == 7)),
                                 reads=['catT_d', 'catT_p'] + W1024_regs(WOUT0, k, j * 128, (j + 1) * 128), writes=psn(4, 5))
                    P.op('dve', lambda e, hview=hview: e.tensor_tensor(hview, pout.rearrange("p k (s t) -> p k s t", s=2), hview, ALU.add),
                         reads=psn(4, 5) + hregs, writes=hregs)
                    tiles.append((P.ops, marks))
                    P.ops = saved_ops
                    if stop_after == ('p1tile', cg):
                        break
                for ti, (L, mk) in enumerate(tiles):
                    A = L[mk['sa0']:mk['sa1']]
                    if ti == 0:
                        P.ops.extend(L[:mk['head_end']])
                    P.ops.extend(L[mk['head_end']:mk['silu']])
                    X = L[mk['silu']:mk['sa0']]
                    Y = A + L[mk['sa1']:mk['brow']]
                    nx, ny = len(X), len(Y)
                    i = j = 0
                    while i < nx or j < ny:
                        if j < ny and (i >= nx or j * nx <= i * ny):
                            P.ops.append(Y[j])
                            j += 1
                        else:
                            P.ops.append(X[i])
                            i += 1
                    P.ops.extend(L[mk['brow']:mk['tail']])
                    if ti + 1 < len(tiles):
                        L2, mk2 = tiles[ti + 1]
                        P.ops.extend(L2[:mk2['head_end']])
                    P.ops.extend(L[mk['tail']:])
                if stop_after is not None and stop_after[0] == 'p1tile':
                    dump('hT', hT[:, :, :, 0:(stop_after[1] % CPH + 1) * 64], [f"h{s}_{c}" for s in range(2) for c in range(CPH)])
                    dump('S32', S32[:], ['S32'])
                    dump('catT', catT[:], ['catT_d', 'catT_p'])
                    dump('otok', otok, ['qks'])
                    dump('kq', kq[:], ['kq_k', 'kq_q'])
                    dump('LA', LA[:], ['LA'])
                    dump('TT', TT[:], [TTr])
                    dump('sm', sm[:], ['sm', 'sm_o', 'sm_ba'])
                    dump('kT', kT[:], ['kT'])
                    dump('vT', vT[:], ['vT'])
                    dump('vnblk', vnblk[:], ['vnblk'])
                    dump('pooled', pooled[:], ['pooled'])
                phase_barrier('p1')
        def phase1_dump(half):
            if stop_after[0] == 'p1':
                dump('hT', hT[:], [f"h{s}_{c}" for s in range(2) for c in range(CPH)])

        def phase2(half):
            WXA, WXB = 0, 8192
            load_w1024(wxk_d, WXA, 'g_wxa')
            load_w1024(wxv_d, WXB, 'g_wxb')
            with ExitStack() as p2:
                KT = sbuf(p2, "KT", [128, 2, 8, 256], BF16)
                Vm = sbuf(p2, "Vm", [128, 2, 2, 1024], BF16)
                with ExitStack() as p2a:
                    memt = sbuf(p2a, "memt", [128, D], F32)
                    mjunk = sbuf(p2a, "mjunk", [128, D], F32)
                    mn = sbuf(p2a, "mn", [128, D], BF16)
                    mT = sbuf(p2a, "mT", [128, 8, 2, 256], BF16)
                    msm = sbuf(p2a, "msm", [128, 4], F32)
                    for s in range(2):
                        for mt in range(2):
                            P.dma('sp', lambda e, s=s, mt=mt: e.dma_start(out=memt[:], in_=mem_d[s, mt * 128:(mt + 1) * 128, :]),
                                  writes=['memt'], group='g_mem')
                            P.op('act', lambda e: e.activation(mjunk[:], memt[:], AF.Square), reads=['memt'], writes=['mjunk'])
                            P.op('dve', lambda e: e.tensor_reduce(msm[:, 0:1], mjunk[:], AX.X, ALU.add), reads=['mjunk'], writes=['msm'])
                            P.op('act', lambda e: e.activation(msm[:, 1:2], msm[:, 0:1], AF.Ln, bias=EPS, scale=1.0 / D), reads=['msm'], writes=['msm'])
                            P.op('act', lambda e: e.activation(msm[:, 1:2], msm[:, 1:2], AF.Exp, scale=-0.5), reads=['msm'], writes=['msm'])
                            P.op('dve', lambda e: e.tensor_scalar_mul(mn[:], memt[:], msm[:, 1:2]), reads=['memt', 'msm'], writes=['mn'])
                            pmt = bankb(0)
                            for k in range(8):
                                P.op('pe', lambda e, k=k: e.transpose(pmt[:, k, :], mn[:, k * 128:(k + 1) * 128], identb[:]),
                                     reads=['mn', 'identb'], writes=psn(0))
                            for k in range(8):
                                P.op('dve', lambda e, k=k, s=s, mt=mt: e.tensor_scalar_mul(mT[:, k, s, mt * 128:(mt + 1) * 128], pmt[:, k, :],
                                                                                           prm[:, P_GMEM + k:P_GMEM + k + 1]),
                                     reads=psn(0) + ['prm'], writes=['mT'])
                    nb = 0
                    for s in range(2):
                        for dj in range(8):
                            b = 1 + (nb % 4)
                            nb += 1
                            pk = bank(b)[:, 0:256]
                            for k in range(8):
                                P.op('pe', lambda e, k=k, s=s, dj=dj, pk=pk: e.matmul(pk, W1024(WXA, k, dj * 128, (dj + 1) * 128), mT[:, k, s, :],
                                                                                      start=(k == 0), stop=(k == 7)),
                                     reads=['mT'] + W1024_regs(WXA, k, dj * 128, (dj + 1) * 128), writes=psn(b))
                            if nb % 2:
                                P.op('act', lambda e, s=s, dj=dj, pk=pk: e.copy(KT[:, s, dj, :], pk), reads=psn(b), writes=['KT'])
                            else:
                                P.op('dve', lambda e, s=s, dj=dj, pk=pk: e.tensor_copy(KT[:, s, dj, :], pk), reads=psn(b), writes=['KT'])
                    for s in range(2):
                        for mt in range(2):
                            for hf in range(2):
                                b = 1 + (nb % 4)
                                nb += 1
                                pv = bank(b)
                                for k in range(8):
                                    P.op('pe', lambda e, k=k, s=s, mt=mt, hf=hf, pv=pv: e.matmul(
                                        pv, mT[:, k, s, mt * 128:(mt + 1) * 128], W1024(WXB, k, hf * 512, (hf + 1) * 512),
                                        start=(k == 0), stop=(k == 7)),
                                         reads=['mT'] + W1024_regs(WXB, k, hf * 512, (hf + 1) * 512), writes=psn(b))
                                if nb % 2:
                                    P.op('act', lambda e, s=s, mt=mt, hf=hf, pv=pv: e.copy(Vm[:, s, mt, hf * 512:(hf + 1) * 512], pv),
                                         reads=psn(b), writes=['Vm'])
                                else:
                                    P.op('dve', lambda e, s=s, mt=mt, hf=hf, pv=pv: e.tensor_copy(Vm[:, s, mt, hf * 512:(hf + 1) * 512], pv),
                                         reads=psn(b), writes=['Vm'])
                    phase_barrier('p2a')
                load_w1024(wxq_d, WXA, 'g_wxa')
                load_w1024(wxo_d, WXB, 'g_wxb')
                TQ = 256
                sq2 = sbuf(p2, "sq2", [128, 8, TQ], BF16)
                rstd2 = sbuf(p2, "rstd2", [128, TQ], F32)
                hn2 = sbuf(p2, "hn2", [128, 8, TQ], BF16)
                qT2 = sbuf(p2, "qT2", [128, 8, TQ], BF16)
                oT2 = sbuf(p2, "oT2", [128, 8, TQ], BF16)
                Eb = sbuf(p2, "Eb", [128, 4, 256], F32)
                Pm = sbuf(p2, "Pm", [128, 4, 256], BF16)
                PTt = sbuf(p2, "PTt", [128, 8, 128], BF16)
                ssm = sbuf(p2, "ssm", [128, 16], F32)
                for s in range(2):
                    for qi in range(TPH // TQ):
                        q0 = qi * TQ
                        hr = hreg(s, q0, q0 + TQ)
                        hv = hT[:, :, s, q0:q0 + TQ]
                        P.op('act', lambda e, hv=hv: e.activation(sq2[:], hv, AF.Square), reads=hr, writes=['sq2'])
                        pst = bank(0)[:, 0:TQ]
                        for k in range(8):
                            P.op('pe', lambda e, k=k, pst=pst: e.matmul(pst, onesb[:], sq2[:, k, :], start=(k == 0), stop=(k == 7)),
                                 reads=['sq2', 'onesb'], writes=psn(0))
                        P.op('act', lambda e, pst=pst: e.activation(rstd2[:], pst, AF.Ln, bias=EPS, scale=1.0 / D), reads=psn(0), writes=['rstd2'])
                        P.op('act', lambda e: e.activation(rstd2[:], rstd2[:], AF.Exp, scale=-0.5), reads=['rstd2'], writes=['rstd2'])
                        for k in range(8):
                            P.op('dve', lambda e, k=k, hv=hv: e.scalar_tensor_tensor(hn2[:, k, :], hv[:, k, :], prm[:, P_GXAT + k:P_GXAT + k + 1],
                                                                                     rstd2[:], ALU.mult, ALU.mult),
                                 reads=hr + ['prm', 'rstd2'], writes=['hn2'])
                        pq = PS[:, 0:4, :].rearrange("p b (j n) -> p (b j) n", j=2)
                        for dj in range(8):
                            for k in range(8):
                                P.op('pe', lambda e, dj=dj, k=k: e.matmul(pq[:, dj, :], W1024(WXA, k, dj * 128, (dj + 1) * 128), hn2[:, k, :],
                                                                          start=(k == 0), stop=(k == 7)),
                                     reads=['hn2'] + W1024_regs(WXA, k, dj * 128, (dj + 1) * 128), writes=psn(dj // 2))
                        P.op('act', lambda e: e.mul(qT2[:, 0:4, :], pq[:, 0:4, :], 1.0 / 16), reads=psn(0, 1), writes=['qT2a'])
                        P.op('dve', lambda e: e.tensor_scalar_mul(qT2[:, 4:8, :], pq[:, 4:8, :], 1.0 / 16), reads=psn(2, 3), writes=['qT2b'])
                        for sub in range(TQ // 128):
                            tsl = slice(sub * 128, (sub + 1) * 128)
                            psc = PS[:, 4:6, :].rearrange("p b (h n) -> p (b h) n", h=2)
                            for h in range(4):
                                for j in range(2):
                                    P.op('pe', lambda e, h=h, j=j, tsl=tsl, s=s: e.matmul(psc[:, h, :], qT2[:, 2 * h + j, tsl], KT[:, s, 2 * h + j, :],
                                                                                     start=(j == 0), stop=(j == 1)),
                                         reads=['qT2a', 'qT2b', 'KT'], writes=psn(4, 5))
                            P.op('dve', lambda e: e.tensor_reduce(ssm[:, 0:4], psc, AX.X, ALU.max, negate=True), reads=psn(4, 5), writes=['ssm'])
                            P.op('dve', lambda e: e.tensor_tensor(Eb[:], psc, bcast(ssm[:, 0:4], 2, 256), ALU.add), reads=psn(4, 5) + ['ssm'], writes=['Eb'])
                            P.op('act', lambda e: e.activation(Eb[:], Eb[:], AF.Exp), reads=['Eb'], writes=['Eb'])
                            P.op('dve', lambda e: e.tensor_reduce(ssm[:, 4:8], Eb[:], AX.X, ALU.add), reads=['Eb'], writes=['ssm2'])
                            P.op('dve', lambda e: e.reciprocal(ssm[:, 8:12], ssm[:, 4:8]), reads=['ssm2'], writes=['ssm3'])
                            P.op('dve', lambda e: e.tensor_tensor(Pm[:], Eb[:], bcast(ssm[:, 8:12], 2, 256), ALU.mult), reads=['Eb', 'ssm3'], writes=['Pm'])
                            ptp = bankb(6)
                            for h in range(4):
                                for mt in range(2):
                                    P.op('pe', lambda e, h=h, mt=mt: e.transpose(ptp[:, h * 2 + mt, :], Pm[:, h, mt * 128:(mt + 1) * 128], identb[:]),
                                         reads=['Pm', 'identb'], writes=psn(6))
                            P.op('act', lambda e: e.copy(PTt[:], ptp), reads=psn(6), writes=['PTt'])
                            pov = bank8(0) if sub == 0 else bank8(2)
                            pr = psn(0, 1) if sub == 0 else psn(2, 3)
                            for h in range(4):
                                for j in range(2):
                                    for mt in range(2):
                                        c0 = h * 256 + j * 128
                                        P.op('pe', lambda e, h=h, j=j, mt=mt, c0=c0, pov=pov, s=s: e.matmul(
                                            pov[:, 2 * h + j, :], Vm[:, s, mt, c0:c0 + 128], PTt[:, h * 2 + mt, :], start=(mt == 0), stop=(mt == 1)),
                                             reads=['Vm', 'PTt'], writes=pr)
                            P.op('dve', lambda e, tsl=tsl, pov=pov: e.tensor_copy(oT2[:, :, tsl], pov), reads=pr, writes=[f'oT2_{sub}'])
                        pxo = PS[:, 4:8, :].rearrange("p b (j n) -> p (b j) n", j=2)
                        for dj in range(8):
                            for k in range(8):
                                P.op('pe', lambda e, dj=dj, k=k: e.matmul(pxo[:, dj, :], W1024(WXB, k, dj * 128, (dj + 1) * 128), oT2[:, k, :],
                                                                          start=(k == 0), stop=(k == 7)),
                                     reads=['oT2_0', 'oT2_1'] + W1024_regs(WXB, k, dj * 128, (dj + 1) * 128), writes=psn(4 + dj // 2))
                        P.op('dve', lambda e, hv=hv: e.tensor_tensor(hv, pxo, hv, ALU.add), reads=psn(4, 5, 6, 7) + hr, writes=hr)
                phase_barrier('p2')
        def phase2_dump(half):
            dump('hT', hT[:], [f"h{s}_{c}" for s in range(2) for c in range(CPH)])

        def phase3(half):
            with ExitStack() as p3:
                hn3 = sbuf(p3, "hn3", [128, 8, 2, TPH], BF16)
                sq3 = sbuf(p3, "sq3", [128, 8, 128], BF16)
                rstd3 = sbuf(p3, "rstd3", [128, 128], F32)
                rl = sbuf(p3, "rl", [128, 512], F32)
                aT = [sbuf(p3, f"aT{i}", [128, 4, 512], BF16) for i in range(2)]
                of = sbuf(p3, "of", [128, 8, 128], F32)
                osb = [sbuf(p3, f"osb{i}", [128, D], F32) for i in range(2)]

                def norm3(s, q0, n, gbase, outk, outregs, sqb, rsb):
                    hr = hreg(s, q0, q0 + n)
                    hv = hT[:, :, s, q0:q0 + n]
                    P.op('act', lambda e: e.activation(sqb[:, :, 0:n], hv, AF.Square), reads=hr, writes=['sq3'])
                    pst = bank(0)[:, 0:n]
                    for k in range(8):
                        P.op('pe', lambda e, k=k: e.matmul(pst, onesb[:], sqb[:, k, 0:n], start=(k == 0), stop=(k == 7)),
                             reads=['sq3', 'onesb'], writes=psn(0))
                    P.op('act', lambda e: e.activation(rsb[:, 0:n], pst, AF.Ln, bias=EPS, scale=1.0 / D), reads=psn(0), writes=['rstd3'])
                    P.op('act', lambda e: e.activation(rsb[:, 0:n], rsb[:, 0:n], AF.Exp, scale=-0.5), reads=['rstd3'], writes=['rstd3'])
                    for k in range(8):
                        P.op('dve', lambda e, k=k: e.scalar_tensor_tensor(outk(k), hv[:, k, :], prm[:, gbase + k:gbase + k + 1], rsb[:, 0:n],
                                                                          ALU.mult, ALU.mult),
                             reads=hr + ['prm', 'rstd3'], writes=outregs)

                for s in range(2):
                    for qi in range(TPH // 128):
                        q0 = qi * 128
                        norm3(s, q0, 128, P_GMLP, lambda k, s=s, q0=q0: hn3[:, k, s, q0:q0 + 128], [f'hn3_{s}_{qi // 4}'], sq3, rstd3)
                NFC = DFF // 512
                SL_UP = [0, 8192]
                SL_DN = [4096, 12288]
                wupv = wup_d.rearrange("(k p) n -> p k n", p=128)
                wdnv = wdn_d.rearrange("(j p) n -> p j n", p=128)

                def load_chunk(fc):
                    sl = fc % 2
                    for k in range(8):
                        lo = SL_UP[sl] + k * 512
                        P.dma('pool', lambda e, k=k, lo=lo, fc=fc: e.dma_start(out=WA[:, lo:lo + 512], in_=wupv[:, k, fc * 512:(fc + 1) * 512]),
                              writes=wa_regs(lo, lo + 512), group=f'g_up{sl}', tok=False)
                    for j in range(4):
                        lo = SL_DN[sl] + j * 1024
                        P.dma('pool', lambda e, j=j, lo=lo, fc=fc: e.dma_start(out=WA[:, lo:lo + 1024], in_=wdnv[:, fc * 4 + j, :]),
                              writes=wa_regs(lo, lo + 1024), group=f'g_dn{sl}', tok=False)

                load_chunk(0)
                nt = 0
                for fc in range(NFC):
                    if fc + 1 < NFC:
                        load_chunk(fc + 1)
                    sl = fc % 2
                    for s in range(2):
                        for qi in range(TPH // 512):
                            q0 = qi * 512
                            hr = hreg(s, q0, q0 + 512)
                            a = aT[nt % 2]
                            ar = f'aT{nt % 2}'
                            nt += 1
                            for fb in range(4):
                                for k in range(8):
                                    lo = SL_UP[sl] + k * 512 + fb * 128
                                    P.op('pe', lambda e, fb=fb, k=k, lo=lo, s=s, q0=q0: e.matmul(bank(fb), WA[:, lo:lo + 128], hn3[:, k, s, q0:q0 + 512],
                                                                                                 start=(k == 0), stop=(k == 7)),
                                         reads=[f'hn3_{s}_{qi}'] + wa_regs(lo, lo + 128), writes=psn(fb))
                                P.op('act', lambda e, fb=fb: e.activation(rl[:], bank(fb), AF.Relu), reads=psn(fb), writes=['rl'])
                                P.op('act', lambda e, fb=fb, a=a: e.activation(a[:, fb, :], rl[:], AF.Square), reads=['rl'], writes=[ar])
                            for dj in range(8):
                                b = 4 + dj % 4
                                for fb in range(4):
                                    lo = SL_DN[sl] + fb * 1024 + dj * 128
                                    P.op('pe', lambda e, fb=fb, dj=dj, lo=lo, a=a, b=b: e.matmul(bank(b), WA[:, lo:lo + 128], a[:, fb, :],
                                                                                                 start=(fb == 0), stop=(fb == 3)),
                                         reads=[ar] + wa_regs(lo, lo + 128), writes=psn(b))
                                hv = hT[:, dj, s, q0:q0 + 512]
                                P.op('dve', lambda e, hv=hv, b=b: e.tensor_tensor(hv, bank(b), hv, ALU.add), reads=psn(b) + hr, writes=hr)
                no = 0
                for s in range(2):
                    for ti in range(TPH // 128):
                        q0 = ti * 128
                        norm3(s, q0, 128, P_GFIN, lambda k: of[:, k, :], ['of'], sq3, rstd3)
                        pto = bank8(2)
                        for k in range(8):
                            P.op('pe', lambda e, k=k: e.transpose(pto[:, k, :], of[:, k, :], ident), reads=['of', 'cst'], writes=psn(2, 3))
                        ob = osb[no % 2]
                        obr = f'osb{no % 2}'
                        no += 1
                        P.op('act', lambda e, ob=ob: e.copy(ob[:].rearrange("p (k n) -> p k n", k=8), pto), reads=psn(2, 3), writes=[obr])
                        tg = half * TPH + q0
                        P.dma('sp', lambda e, ob=ob, s=s, tg=tg: e.dma_start(out=out_d[s, tg:tg + 128, :], in_=ob[:]),
                              reads=[obr], writes=['out'], group=f'g_{obr}')
                phase_barrier('p3')

        for half in range(NHALF):
            phase1(half)
            if stop_after is not None and stop_after[0] in ('p1tile', 'p1'):
                phase1_dump(half)
                break
            phase2(half)
            if stop_after is not None and stop_after[0] == 'p2':
                phase2_dump(half)
                break
            phase3(half)

        fg = [g for g in ('g_osb0', 'g_osb1') if stop_after is None]
        fg += [f'dbg{i + 1}' for i in range(dbg_n[0])]
        if max_ops is not None:
            P.ops = P.ops[:max_ops]
            fg = []
        st = P.finalize(final_groups=fg)
        build_nc.stats = st
    return nc


def make_consts():
    c = np.zeros((128, CW), np.float32)
    idx = np.arange(128)
    blk = idx // 64
    same = blk[:, None] == blk[None, :]
    c[:, C_ID:C_ID + 128] = np.eye(128)
    c[:, C_ONE:C_ONE + 128] = 1.0
    tri = same & (idx[:, None] <= idx[None, :])
    c[:, C_MTRI:C_MTRI + 128] = tri
    c[:, C_NMTRI:C_NMTRI + 128] = -tri.astype(np.float32)
    c[:, C_MBLK:C_MBLK + 128] = same
    c[:, C_MSN:C_MSN + 128] = -(same & (idx[None, :] > idx[:, None])).astype(np.float32)
    c[:, C_MI:C_MI + 128] = (same & (idx[None, :] >= idx[:, None]))
    for g, w in enumerate(WINDOWS):
        t = np.arange(16)
        c[:, C_RAT + g * 16:C_RAT + (g + 1) * 16] = (w / np.minimum(t + 1, w))[None, :]
    return c


def make_prm(inp):
    p = np.zeros((128, PW), np.float32)

    def colvec(v):
        return np.asarray(v, np.float32).reshape(8, 128).T

    p[:, P_GMIX:P_GMIX + 8] = colvec(inp['norm_mix_g'][0])
    p[:, P_GXAT:P_GXAT + 8] = colvec(inp['norm_xattn_g'][0])
    p[:, P_GMEM:P_GMEM + 8] = colvec(inp['mem_norm_g'][0])
    p[:, P_GMLP:P_GMLP + 8] = colvec(inp['norm_mlp_g'][0])
    p[:, P_GFIN:P_GFIN + 8] = colvec(inp['final_norm_g'])
    p[:, P_PSC:P_PSC + 4] = np.asarray(inp['pool_scale'][0], np.float32).reshape(4, 128).T
    cw = np.asarray(inp['conv_w'][0], np.float32)
    p[:, P_CONV:P_CONV + 48] = cw.reshape(4, 12, 128).transpose(2, 1, 0).reshape(128, 48)
    p[:, P_ALOG:P_ALOG + 4] = np.broadcast_to(np.asarray(inp['a_log'][0], np.float32)[None, :], (128, 4))
    p[:, P_DTB:P_DTB + 4] = np.broadcast_to(np.asarray(inp['dt_bias'][0], np.float32)[None, :], (128, 4))
    p[:, P_GO:P_GO + 128] = np.broadcast_to(np.asarray(inp['dn_out_norm_g'][0], np.float32)[None, :], (128, 128))
    return p


def make_in_maps(inp, n_cores=8):
    prm = make_prm(inp)
    cst = make_consts()
    shared = dict(
        w_in=np.ascontiguousarray(inp['w_in'][0], dtype=np.float32),
        w_pool=np.ascontiguousarray(inp['w_pool'][0], dtype=np.float32),
        w_out=np.ascontiguousarray(inp['w_out'][0], dtype=np.float32),
        w_xq=np.ascontiguousarray(inp['w_xq'][0], dtype=np.float32),
        w_xk=np.ascontiguousarray(inp['w_xk'][0], dtype=np.float32),
        w_xv=np.ascontiguousarray(inp['w_xv'][0], dtype=np.float32),
        w_xo=np.ascontiguousarray(inp['w_xo'][0], dtype=np.float32),
        w_up=np.ascontiguousarray(inp['w_up'][0], dtype=np.float32),
        w_down=np.ascontiguousarray(inp['w_down'][0], dtype=np.float32),
        prm=prm, cst=cst)
    x = np.asarray(inp['x'], np.float32)
    mem = np.asarray(inp['mem'], np.float32)
    maps = []
    for i in range(n_cores):
        m = dict(shared)
        m['x'] = np.ascontiguousarray(x[NSEQ * i:NSEQ * (i + 1)])
        m['mem'] = np.ascontiguousarray(mem[NSEQ * i:NSEQ * (i + 1)])
        maps.append(m)
    return maps


def kernel(**inputs):
    nc = build_nc()
    maps = make_in_maps(inputs, 8)
    res = run_bass_kernel_spmd(nc, maps, core_ids=list(range(8)))
    outs = [np.asarray(r['out'], np.float32) for r in res.results]
    return np.concatenate(outs, axis=0)
```

```python
import numpy as np
from contextlib import ExitStack
import concourse.bass as bass
import concourse.mybir as mybir
from concourse.bass_utils import run_bass_kernel_spmd

F32 = mybir.dt.float32
BF16 = mybir.dt.bfloat16
AF = mybir.ActivationFunctionType
ALU = mybir.AluOpType
AX = mybir.AxisListType

D = 1024
SEQ = 2048
NSEQ = 2
MEM = 256
INC = 2568
DFF = 4096
EPS = 1e-6
NHALF = 2
CPH = 16
TPH = CPH * 64
WINDOWS = (2, 4, 8, 16)

P_GMIX, P_GXAT, P_GMEM, P_GMLP, P_GFIN = 0, 8, 16, 24, 32
P_PSC = 40
P_CONV = 44
P_ALOG = 92
P_DTB = 96
P_GO = 100
PW = 228
C_ID, C_ONE, C_MTRI, C_NMTRI, C_MBLK, C_MSN, C_MI, C_RAT = 0, 128, 256, 384, 512, 640, 768, 896
CW = 960


class Prog:
    def __init__(self, nc, es):
        self.nc = nc
        self.es = es
        self.ops = []
        self.engs = {'pe': nc.tensor, 'act': nc.scalar, 'dve': nc.vector, 'pool': nc.gpsimd, 'sp': nc.sync}
        self.token = None

    def op(self, engine, fn, reads=(), writes=(), tok=True):
        r = tuple(reads)
        if tok and self.token is not None:
            r = r + (self.token,)
        self.ops.append(dict(e=engine, fn=fn, r=r, w=tuple(writes), dma=None))

    def dma(self, queue, fn, reads=(), writes=(), group=None, tok=True):
        r = tuple(reads)
        if tok and self.token is not None:
            r = r + (self.token,)
        self.ops.append(dict(e=queue, fn=fn, r=r, w=tuple(writes), dma=group))

    def barrier(self, fn):
        self.ops.append(dict(e='dve', fn=fn, r=(), w=(self.token,), dma=None))

    def finalize(self, final_groups=()):
        nc = self.nc
        ops = self.ops
        last_writer = {}
        readers = {}
        for i, o in enumerate(ops):
            deps = set()
            for r in o['r']:
                if r in last_writer:
                    deps.add(last_writer[r])
            for w in o['w']:
                if w in last_writer:
                    deps.add(last_writer[w])
                deps.update(readers.get(w, ()))
            deps.discard(i)
            best = {}
            for d in deps:
                od = ops[d]
                key = ('g', od['dma']) if od['dma'] is not None else ('e', od['e'])
                if key not in best or best[key] < d:
                    best[key] = d
            o['deps'] = set(best.values())
            for r in o['r']:
                readers.setdefault(r, []).append(i)
            for w in o['w']:
                last_writer[w] = i
                readers[w] = []

        def skip(od, o):
            return od['dma'] is None and od['e'] == 'pe' and o['e'] == 'pe' and o['dma'] is None

        need = [False] * len(ops)
        for i, o in enumerate(ops):
            for d in o['deps']:
                if not skip(ops[d], o):
                    need[d] = True
        sems = {e: self.es.enter_context(nc.semaphore('sem_' + e)) for e in self.engs}
        gsem = {}
        gcount = {}
        cnt = {e: 0 for e in self.engs}
        for i, o in enumerate(ops):
            if o['dma'] is not None:
                g = o['dma']
                if g not in gsem:
                    gsem[g] = self.es.enter_context(nc.semaphore('dg_' + g))
                    gcount[g] = 0
                gcount[g] += 16
                o['sig'] = (gsem[g], gcount[g])
            elif need[i]:
                cnt[o['e']] += 1
                o['sig'] = (sems[o['e']], cnt[o['e']])
            else:
                o['sig'] = None
            o['gsnap'] = None
        running = {}
        snaps = []
        for i, o in enumerate(ops):
            snaps.append(dict(running))
            if o['dma'] is not None:
                running[o['dma']] = o['sig'][1]
        waited = {e: {} for e in self.engs}
        nwaits = 0
        for i, o in enumerate(ops):
            e = o['e']
            eng = self.engs[e]
            wl = {}
            for d in o['deps']:
                od = ops[d]
                if skip(od, o):
                    continue
                s, v = od['sig']
                if od['dma'] is not None:
                    v = snaps[i][od['dma']]
                key = id(s)
                if key not in wl or wl[key][1] < v:
                    wl[key] = (s, v)
            for key, (s, v) in wl.items():
                if waited[e].get(key, 0) >= v:
                    continue
                waited[e][key] = v
                eng.wait_ge(s, v)
                nwaits += 1
            inst = o['fn'](eng)
            if o['dma'] is not None:
                inst.then_inc(o['sig'][0], 16)
            elif o['sig'] is not None:
                inst.then_inc(o['sig'][0], 1)
        for g in final_groups:
            nc.sync.wait_ge(gsem[g], gcount[g])
        self.stats = dict(n_ops=len(ops), n_waits=nwaits, sig=cnt, ngroups=len(gsem))
        return self.stats


def bcast(ap, pos, n):
    lst = [list(x) for x in ap.ap]
    lst.insert(pos, [0, n])
    return bass.AP(ap.tensor, ap.offset, lst)


def build_nc(stop_after=None, dbg=None, max_ops=None):
    nc = bass.Bass("TRN2", target_bir_lowering=False)

    def dr(name, shape, kind="ExternalInput", dt=F32):
        return nc.dram_tensor(name, list(shape), dt, kind=kind).ap()

    x_d = dr("x", [NSEQ, SEQ, D])
    mem_d = dr("mem", [NSEQ, MEM, D])
    win_d = dr("w_in", [D, INC])
    wpool_d = dr("w_pool", [4, 128, 128])
    wout_d = dr("w_out", [D, D])
    wxq_d = dr("w_xq", [D, D])
    wxk_d = dr("w_xk", [D, D])
    wxv_d = dr("w_xv", [D, D])
    wxo_d = dr("w_xo", [D, D])
    wup_d = dr("w_up", [D, DFF])
    wdn_d = dr("w_down", [DFF, D])
    prm_d = dr("prm", [128, PW])
    cst_d = dr("cst", [128, CW])
    out_d = dr("out", [NSEQ, SEQ, D], kind="ExternalOutput")
    dbg_d = {}
    if dbg:
        for k, (shp, dtn) in dbg.items():
            dbg_d[k] = dr("dbg_" + k, shp, kind="ExternalOutput", dt=(BF16 if dtn == 'bf16' else F32))

    es = ExitStack()
    with es:
        P = Prog(nc, es)

        uid = [0]

        def sbuf(stack, name, shape, dt):
            uid[0] += 1
            return stack.enter_context(nc.sbuf_tensor(f"sb{uid[0]}_{name}", list(shape), dt))

        hT = sbuf(es, "hT", [128, 8, NSEQ, TPH], F32)
        cst = sbuf(es, "cst", [128, CW], F32)
        prm = sbuf(es, "prm", [128, PW], F32)
        identb = sbuf(es, "identb", [128, 128], BF16)
        onesb = sbuf(es, "onesb", [128, 128], BF16)
        convd = sbuf(es, "convd", [128, 12, 4, 128], BF16)
        WA = sbuf(es, "warena", [128, 29312], BF16)
        S32 = sbuf(es, "S32", [128, 4, 256], F32)
        Sbf = sbuf(es, "Sbf", [128, 4, 256], BF16)
        qkvh = sbuf(es, "qkvh", [128, 12, 2, 3], BF16)
        uph = sbuf(es, "uph", [128, 4, 2, 16], F32)
        negA = sbuf(es, "negA", [128, 4], F32)
        dummy = sbuf(es, "dummy", [128, 2], F32)
        PS = es.enter_context(nc.psum_tensor("PS", [128, 8, 512], F32))

        ident = cst[:, C_ID:C_ID + 128]
        ones = cst[:, C_ONE:C_ONE + 128]
        mtri = cst[:, C_MTRI:C_MTRI + 128]
        nmtri = cst[:, C_NMTRI:C_NMTRI + 128]
        mblk = cst[:, C_MBLK:C_MBLK + 128]
        msn = cst[:, C_MSN:C_MSN + 128]
        mi = cst[:, C_MI:C_MI + 128]
        rat = cst[:, C_RAT:C_RAT + 64]

        def bank(b):
            return PS[:, b, :]

        def bank4(b):
            return PS[:, b, :].rearrange("p (j n) -> p j n", j=4)

        def bank8(b):
            return PS[:, b:b + 2, :].rearrange("p b (j n) -> p (b j) n", j=4)

        def bankb(b):
            return PS[:, b, :].bitcast(BF16).rearrange("p (j n) -> p j n", j=8)

        def psn(*bs):
            return [f"ps{b}" for b in bs]

        def wa_regs(lo, hi):
            return [f"wa{b}" for b in range(lo // 2048, (hi - 1) // 2048 + 1)]

        def hreg(s, t0, t1):
            return [f"h{s}_{c}" for c in range(t0 // 64, (t1 - 1) // 64 + 1)]

        dbg_n = [0]

        def dump(name, ap, reads):
            if name in dbg_d:
                dbg_n[0] += 1
                P.dma('sp', lambda e: e.dma_start(out=dbg_d[name], in_=ap), reads=reads, writes=['dbg_' + name],
                      group=f'dbg{dbg_n[0]}')

        P.dma('sp', lambda e: e.dma_start(out=cst[:], in_=cst_d[:, :]), writes=['cst'], group='c0')
        P.dma('sp', lambda e: e.dma_start(out=prm[:], in_=prm_d[:, :]), writes=['prm'], group='c1')
        P.op('dve', lambda e: e.tensor_copy(identb[:], ident), reads=['cst'], writes=['identb'])
        P.op('dve', lambda e: e.tensor_copy(onesb[:], ones), reads=['cst'], writes=['onesb'])
        for b in range(12):
            for k in range(4):
                col = P_CONV + b * 4 + k
                P.op('dve', lambda e, b=b, k=k, col=col: e.tensor_scalar_mul(convd[:, b, k, :], ident, prm[:, col:col + 1]),
                     reads=['cst', 'prm'], writes=['convd'])
        P.op('act', lambda e: e.activation(negA[:], prm[:, P_ALOG:P_ALOG + 4], AF.Exp), reads=['prm'], writes=['negA'])
        P.op('dve', lambda e: e.tensor_scalar_mul(negA[:], negA[:], -1.0), reads=['negA'], writes=['negA'])
        P.op('dve', lambda e: e.memset(S32[:], 0.0), writes=['S32'])
        P.op('dve', lambda e: e.memset(Sbf[:], 0.0), writes=['Sbf'])
        P.op('dve', lambda e: e.memset(qkvh[:], 0.0), writes=['qkvh'])
        P.op('dve', lambda e: e.memset(uph[:], 0.0), writes=['uph'])

        WIN0 = 0
        WOUT0 = 8 * INC
        WPOOL0 = WOUT0 + 8 * D
        winv = win_d.rearrange("(k p) n -> p k n", p=128)

        def load_w1024(dram, base, gname):
            v = dram.rearrange("(k p) n -> p k n", p=128)
            for k in range(8):
                lo = base + k * 1024
                P.dma('pool', lambda e, k=k, lo=lo: e.dma_start(out=WA[:, lo:lo + 1024], in_=v[:, k, :]),
                      writes=wa_regs(lo, lo + 1024), group=gname, tok=False)

        def load_p1_weights():
            for k in range(8):
                lo = WIN0 + k * INC
                P.dma('pool', lambda e, k=k, lo=lo: e.dma_start(out=WA[:, lo:lo + INC], in_=winv[:, k, :],
                                                                 max_dma_last_dim=4096),
                      writes=wa_regs(lo, lo + INC), group='g_win', tok=False)
            load_w1024(wout_d, WOUT0, 'g_wout')
            P.dma('pool', lambda e: e.dma_start(out=WA[:, WPOOL0:WPOOL0 + 512].rearrange("p (g d) -> p g d", g=4),
                                                in_=wpool_d.rearrange("g c d -> c g d")),
                  writes=wa_regs(WPOOL0, WPOOL0 + 512), group='g_wpool', tok=False)

        def Win(k, lo, hi):
            return WA[:, WIN0 + k * INC + lo:WIN0 + k * INC + hi]

        def Win_regs(k, lo, hi):
            return wa_regs(WIN0 + k * INC + lo, WIN0 + k * INC + hi)

        def W1024(base, k, lo, hi):
            return WA[:, base + k * 1024 + lo:base + k * 1024 + hi]

        def W1024_regs(base, k, lo, hi):
            return wa_regs(base + k * 1024 + lo, base + k * 1024 + hi)

        def phase_barrier(name):
            P.barrier(lambda e: e.memset(dummy[:], 0.0))

        P.token = 'tok'

        def rms_rows(stack_bufs, src_ps_or_sb, n, gcol, out_hn, reads, ps_stat_bank, sq, rstd, src_k):
            raise NotImplementedError

        def phase1(half):
            load_p1_weights()
            with ExitStack() as p1:
                xt = sbuf(p1, "xt", [128, D], F32)
                sq = sbuf(p1, "sq", [128, 8, 128], F32)
                rstd = sbuf(p1, "rstd", [128, 128], F32)
                hn = sbuf(p1, "hn", [128, 8, 128], BF16)
                qkvb = sbuf(p1, "qkvb", [128, 12, 2, 67], BF16)
                upb = sbuf(p1, "upb", [128, 4, 2, 80], F32)
                wsb = sbuf(p1, "wsb", [128, 4, 2, 80], F32)
                wsc = sbuf(p1, "wsc", [128, 4, 2, 80], F32)
                pooled = sbuf(p1, "pooled", [128, 4, 2, 64], BF16)
                catT = sbuf(p1, "catT", [128, 8, 128], BF16)
                qks = sbuf(p1, "qks", [128, 8, 128], F32)
                vT = sbuf(p1, "vT", [128, 4, 128], BF16)
                kq = sbuf(p1, "kq", [128, 4, 2, 128], BF16)
                kT = sbuf(p1, "kT", [128, 4, 128], BF16)
                sm = sbuf(p1, "sm", [128, 64], F32)
                gB = sbuf(p1, "gB", [128, 4, 128], F32)
                dexp = sbuf(p1, "dexp", [128, 4, 128], F32)
                E2 = xt[:].rearrange("p (h a n) -> p h a n", h=4, a=2)
                egrow = sbuf(p1, "egrow", [128, 4, 128], F32)
                qg = sbuf(p1, "qg", [128, 4, 128], BF16)
                LA = sbuf(p1, "LA", [128, 4, 2, 128], BF16)
                N0 = sbuf(p1, "N0", [128, 4, 128], BF16)
                NAa = sbuf(p1, "NAa", [128, 4, 2, 128], BF16)
                NAb = sbuf(p1, "NAb", [128, 4, 2, 128], BF16)
                PTa = sbuf(p1, "PTa", [128, 4, 128], BF16)
                PTb = sbuf(p1, "PTb", [128, 4, 128], BF16)
                negwT = sbuf(p1, "negwT", [128, 4, 128], BF16)
                vbblk = sbuf(p1, "vbblk", [128, 4, 256], BF16)
                kbg = sbuf(p1, "kbg", [128, 4, 128], BF16)
                kdec = sbuf(p1, "kdec", [128, 4, 128], BF16)
                vnblk = sbuf(p1, "vnblk", [128, 4, 256], BF16)
                zt = sbuf(p1, "zt", [128, 4, 128], BF16)
                ydn = sbuf(p1, "ydn", [128, 4, 128], BF16)
                otok = qks[:, 0:4, :]
                osq = dexp

                S_BA, S_BETA, S_G, S_T1, S_T2, S_GC, S_GL, S_SC1, S_SC2, S_OSS, S_ORS = 0, 8, 12, 16, 20, 24, 28, 32, 36, 40, 44

                P.op('dve', lambda e: e.memset(vbblk[:], 0.0), writes=['vbblk'])
                P.op('dve', lambda e: e.memset(sm[:], 0.0), writes=['sm', 'sm_o', 'sm_ba'])
                P.op('dve', lambda e: e.memset(vnblk[:], 0.0), writes=['vnblk'])

                tiles = []
                for c in range(CPH):
                    saved_ops = P.ops
                    P.ops = []
                    marks = {}
                    cg = half * CPH + c
                    t0 = c * 64
                    hregs = [f"h0_{c}", f"h1_{c}"]
                    hview = hT[:, :, :, t0:t0 + 64]
                    for s in range(NSEQ):
                        P.dma('sp', lambda e, s=s, cg=cg: e.dma_start(out=xt[s * 64:(s + 1) * 64, :],
                                                                      in_=x_d[s, cg * 64:(cg + 1) * 64, :]),
                              writes=['xt'], group='g_x')
                    pT = bank8(0)
                    for k in range(8):
                        P.op('pe', lambda e, k=k: e.transpose(pT[:, k, :], xt[:, k * 128:(k + 1) * 128], ident),
                             reads=['xt', 'cst'], writes=psn(0, 1))
                    P.op('act', lambda e, hview=hview: e.copy(hview, pT.rearrange("p k (s t) -> p k s t", s=2)),
                         reads=psn(0, 1), writes=hregs)
                    P.op('act', lambda e: e.activation(hn[:], pT, AF.Square), reads=psn(0, 1), writes=['hn'])
                    pst = bank(2)[:, 0:128]
                    for k in range(8):
                        P.op('pe', lambda e, k=k: e.matmul(pst, onesb[:], hn[:, k, :], start=(k == 0), stop=(k == 7)),
                             reads=['hn', 'onesb'], writes=psn(2))
                    P.op('act', lambda e: e.activation(rstd[:], pst, AF.Ln, bias=EPS, scale=1.0 / D),
                         reads=psn(2), writes=['rstd'])
                    P.op('act', lambda e: e.activation(rstd[:], rstd[:], AF.Exp, scale=-0.5), reads=['rstd'], writes=['rstd'])
                    for k in range(8):
                        P.op('dve', lambda e, k=k: e.scalar_tensor_tensor(hn[:, k, :], pT[:, k, :],
                                                                          prm[:, P_GMIX + k:P_GMIX + k + 1], rstd[:],
                                                                          ALU.mult, ALU.mult),
                             reads=psn(0, 1) + ['prm', 'rstd'], writes=['hn'])
                    marks['head_end'] = len(P.ops)
                    for grp in range(4):
                        pb = bank4(3 + grp)
                        for j in range(4):
                            col = (grp * 4 + j) * 128
                            for k in range(8):
                                P.op('pe', lambda e, pb=pb, j=j, k=k, col=col: e.matmul(
                                    pb[:, j, :], Win(k, col, col + 128), hn[:, k, :], start=(k == 0), stop=(k == 7)),
                                     reads=['hn'] + Win_regs(k, col, col + 128), writes=psn(3 + grp))
                    for k in range(8):
                        P.op('pe', lambda e, k=k: e.matmul(bank(7), hn[:, k, :], Win(k, 2048, 2560),
                                                           start=(k == 0), stop=(k == 7)),
                             reads=['hn'] + Win_regs(k, 2048, 2560), writes=psn(7))
                    pba = bank(2)[:, 128:136]
                    for k in range(8):
                        P.op('pe', lambda e, k=k: e.matmul(pba, hn[:, k, :], Win(k, 2560, 2568),
                                                           start=(k == 0), stop=(k == 7)),
                             reads=['hn'] + Win_regs(k, 2560, 2568), writes=psn(2))
                    P.op('pool', lambda e: e.tensor_copy(upb[:, :, :, 0:16], uph[:]), reads=['uph'], writes=['upb_h'])
                    P.op('pool', lambda e: e.tensor_copy(qkvb[:, :, :, 0:3], qkvh[:]), reads=['qkvh'], writes=['qkvb_h'])
                    P.op('act', lambda e: e.copy(upb[:, :, :, 16:80], bank4(3).rearrange("p g (s t) -> p g s t", s=2)),
                         reads=psn(3), writes=['upb_c'])
                    for j in range(3):
                        eng = 'dve' if j != 1 else 'act'
                        if eng == 'dve':
                            P.op('dve', lambda e, j=j: e.tensor_copy(qkvb[:, 4 * j:4 * j + 4, :, 3:67],
                                                                     bank4(4 + j).rearrange("p g (s t) -> p g s t", s=2)),
                                 reads=psn(4 + j), writes=[f'qkvb_c{j}'])
                        else:
                            P.op('act', lambda e, j=j: e.copy(qkvb[:, 4 * j:4 * j + 4, :, 3:67],
                                                              bank4(4 + j).rearrange("p g (s t) -> p g s t", s=2)),
                                 reads=psn(4 + j), writes=[f'qkvb_c{j}'])
                    P.op('act', lambda e: e.activation(zt[:], bank4(7), AF.Silu), reads=psn(7), writes=['zt'])
                    P.op('dve', lambda e: e.tensor_copy(sm[:, S_BA:S_BA + 8], pba), reads=psn(2), writes=['sm_ba'])
                    P.op('pool', lambda e: e.tensor_copy(uph[:], upb[:, :, :, 64:80]), reads=['upb_c'], writes=['uph'])
                    P.op('pool', lambda e: e.tensor_copy(qkvh[:], qkvb[:, :, :, 64:67]),
                         reads=['qkvb_c0', 'qkvb_c1', 'qkvb_c2'], writes=['qkvh'])

                    U = ['upb_h', 'upb_c']
                    P.op('pool', lambda e: e.tensor_tensor(wsb[:, :, :, 1:80], upb[:, :, :, 1:80], upb[:, :, :, 0:79], ALU.add),
                         reads=U, writes=['wsb'])
                    P.op('pool', lambda e: e.tensor_tensor(wsc[:, 1:4, :, 3:80], wsb[:, 1:4, :, 3:80], wsb[:, 1:4, :, 1:78], ALU.add),
                         reads=['wsb'], writes=['wsc'])
                    P.op('pool', lambda e: e.tensor_tensor(wsb[:, 2:4, :, 7:80], wsc[:, 2:4, :, 7:80], wsc[:, 2:4, :, 3:76], ALU.add),
                         reads=['wsc'], writes=['wsb'])
                    P.op('pool', lambda e: e.tensor_tensor(wsc[:, 3:4, :, 15:80], wsb[:, 3:4, :, 15:80], wsb[:, 3:4, :, 7:72], ALU.add),
                         reads=['wsb'], writes=['wsc'])
                    fin = [wsb, wsc, wsb, wsc]
                    for g in range(4):
                        P.op('pool', lambda e, g=g: e.tensor_scalar_mul(fin[g][:, g, :, 16:80], fin[g][:, g, :, 16:80],
                                                                        1.0 / WINDOWS[g]),
                             reads=['wsb', 'wsc'], writes=['wsb', 'wsc'])
                        if cg == 0:
                            P.op('pool', lambda e, g=g: e.tensor_tensor(fin[g][:, g, :, 16:32], fin[g][:, g, :, 16:32],
                                                                        bcast(rat[:, g * 16:(g + 1) * 16], 1, 2), ALU.mult),
                                 reads=['wsb', 'wsc', 'cst'], writes=['wsb', 'wsc'])
                        P.op('pool', lambda e, g=g: e.tensor_tensor(pooled[:, g, :, :], fin[g][:, g, :, 16:80],
                                                                    upb[:, g, :, 16:80], ALU.subtract),
                             reads=['wsb', 'wsc'] + U, writes=['pooled'])

                    for b in range(12):
                        pc = bank4(4 + b // 4)[:, b % 4, :].rearrange("p (s t) -> p s t", s=2)
                        for k in range(4):
                            P.op('pe', lambda e, b=b, k=k, pc=pc: e.matmul(pc, convd[:, b, k, :], qkvb[:, b, :, k:k + 64],
                                                                           start=(k == 0), stop=(k == 3)),
                                 reads=['convd', 'qkvb_h', f'qkvb_c{b // 4}'], writes=psn(4 + b // 4))
                    marks['silu'] = len(P.ops)
                    P.op('act', lambda e: e.activation(qks[:], bank8(4), AF.Silu), reads=psn(4, 5), writes=['qks'])
                    P.op('act', lambda e: e.activation(vT[:], bank4(6), AF.Silu), reads=psn(6), writes=['vT'])
                    P.op('dve', lambda e: e.tensor_tensor(hn[:], qks[:], qks[:], ALU.mult), reads=['qks'], writes=['hn'])
                    pl = bank8(0)
                    for j in range(8):
                        P.op('pe', lambda e, j=j: e.matmul(pl[:, j, :], onesb[:], hn[:, j, :], start=True, stop=True),
                             reads=['hn', 'onesb'], writes=psn(0, 1))
                    P.op('act', lambda e: e.activation(sq[:], pl, AF.Ln, bias=EPS), reads=psn(0, 1), writes=['sq'])
                    P.op('act', lambda e: e.activation(sq[:], sq[:], AF.Exp, scale=-0.5), reads=['sq'], writes=['sq'])
                    P.op('dve', lambda e: e.scalar_tensor_tensor(kq[:, :, 1, :], qks[:, 0:4, :], 128.0 ** -0.5, sq[:, 0:4, :],
                                                                 ALU.mult, ALU.mult), reads=['qks', 'sq'], writes=['kq_q'])
                    P.op('dve', lambda e: e.tensor_tensor(kT[:], qks[:, 4:8, :], sq[:, 4:8, :], ALU.mult),
                         reads=['qks', 'sq'], writes=['kT'])

                    marks['sa0'] = len(P.ops)
                    b_ = sm[:, S_BA:S_BA + 4]
                    a_ = sm[:, S_BA + 4:S_BA + 8]
                    beta = sm[:, S_BETA:S_BETA + 4]
                    g_ = sm[:, S_G:S_G + 4]
                    t1 = sm[:, S_T1:S_T1 + 4]
                    t2 = sm[:, S_T2:S_T2 + 4]
                    SM = ['sm']
                    P.op('act', lambda e: e.activation(t1, b_, AF.Exp, scale=-1.0), reads=['sm_ba'], writes=SM)
                    P.op('dve', lambda e: e.tensor_scalar_add(t1, t1, 1.0), reads=SM, writes=SM)
                    P.op('dve', lambda e: e.reciprocal(beta, t1), reads=SM, writes=SM)
                    P.op('dve', lambda e: e.tensor_tensor(t1, a_, prm[:, P_DTB:P_DTB + 4], ALU.add), reads=['sm_ba', 'prm'] + SM, writes=SM)
                    P.op('dve', lambda e: e.tensor_scalar_mul(t2, t1, -1.0), reads=SM, writes=SM)
                    P.op('dve', lambda e: e.tensor_tensor(t2, t2, t1, ALU.max), reads=SM, writes=SM)
                    P.op('act', lambda e: e.activation(t2, t2, AF.Exp, scale=-1.0), reads=SM, writes=SM)
                    P.op('act', lambda e: e.activation(t2, t2, AF.Ln, bias=1.0), reads=SM, writes=SM)
                    P.op('dve', lambda e: e.tensor_scalar_max(t1, t1, 0.0), reads=SM, writes=SM)
                    P.op('dve', lambda e: e.tensor_tensor(t1, t1, t2, ALU.add), reads=SM, writes=SM)
                    P.op('dve', lambda e: e.tensor_tensor(g_, t1, negA[:], ALU.mult), reads=SM + ['negA'], writes=SM)
                    marks['sa1'] = len(P.ops)
                    pg = bank(2)[:, 136:144]
                    P.op('pe', lambda e: e.matmul(pg[:, 0:4], mtri, g_, start=True, stop=True), reads=SM + ['cst'], writes=psn(2))
                    P.op('pe', lambda e: e.matmul(pg[:, 4:8], mblk, g_, start=True, stop=True), reads=SM + ['cst'], writes=psn(2))
                    gc = sm[:, S_GC:S_GC + 4]
                    gl = sm[:, S_GL:S_GL + 4]
                    sc1 = sm[:, S_SC1:S_SC1 + 4]
                    sc2 = sm[:, S_SC2:S_SC2 + 4]
                    P.op('dve', lambda e: e.tensor_copy(sm[:, S_GC:S_GC + 8], pg), reads=psn(2), writes=SM)
                    P.op('act', lambda e: e.activation(sc1, gc, AF.Exp), reads=SM, writes=SM)
                    P.op('dve', lambda e: e.tensor_tensor(sc1, sc1, beta, ALU.mult), reads=SM, writes=SM)
                    P.op('dve', lambda e: e.tensor_tensor(sc2, gl, gc, ALU.subtract), reads=SM, writes=SM)
                    P.op('act', lambda e: e.activation(sc2, sc2, AF.Exp), reads=SM, writes=SM)
                    for h in range(4):
                        P.op('dve', lambda e, h=h: e.tensor_scalar_mul(gB[:, h, :], ones, sm[:, S_G + h:S_G + h + 1]),
                             reads=SM + ['cst'], writes=['gB'])
                    pdf = bank4(3)
                    pgr = bank4(7)
                    for h in range(4):
                        P.op('pe', lambda e, h=h: e.matmul(pdf[:, h, :], gB[:, h, :], mtri, start=True, stop=False),
                             reads=['gB', 'cst'], writes=psn(3))
                        P.op('pe', lambda e, h=h: e.matmul(pdf[:, h, :], nmtri, gB[:, h, :], start=False, stop=True),
                             reads=['gB', 'cst'], writes=psn(3))
                    for h in range(4):
                        P.op('pe', lambda e, h=h: e.matmul(pgr[:, h, :], gB[:, h, :], mtri, start=True, stop=True),
                             reads=['gB', 'cst'], writes=psn(7))
                    P.op('dve', lambda e: e.tensor_scalar_min(dexp[:], pdf, 0.0), reads=psn(3), writes=['dexp'])
                    P.op('act', lambda e: e.activation(dexp[:], dexp[:], AF.Exp), reads=['dexp'], writes=['dexp'])
                    P.op('act', lambda e: e.activation(egrow[:], pgr, AF.Exp), reads=psn(7), writes=['egrow'])
                    P.op('pool', lambda e: e.tensor_tensor(E2[:, :, 0, :], dexp[:], bcast(msn, 1, 4), ALU.mult),
                         reads=['dexp', 'cst'], writes=['xt'])
                    P.op('pool', lambda e: e.tensor_tensor(E2[:, :, 1, :], dexp[:], bcast(mi, 1, 4), ALU.mult),
                         reads=['dexp', 'cst'], writes=['xt'])
                    marks['brow'] = len(P.ops)
                    for h in range(4):
                        P.op('dve', lambda e, h=h: e.tensor_scalar_mul(gB[:, h, :], ones, sm[:, S_BETA + h:S_BETA + h + 1]),
                             reads=SM + ['cst'], writes=['gB'])
                    pbr = bank4(3)
                    for h in range(4):
                        P.op('pe', lambda e, h=h: e.matmul(pbr[:, h, :], gB[:, h, :], ident, start=True, stop=True),
                             reads=['gB', 'cst'], writes=psn(3))
                    P.op('dve', lambda e: e.tensor_tensor(kq[:, :, 0, :], pbr, kT[:], ALU.mult), reads=psn(3) + ['kT'], writes=['kq_k'])
                    P.op('pool', lambda e: e.tensor_tensor(qg[:], kq[:, :, 1, :], egrow[:], ALU.mult), reads=['kq_q', 'egrow'], writes=['qg'])
                    ptb = bankb(0)
                    for h in range(4):
                        P.op('pe', lambda e, h=h: e.transpose(ptb[:, h, :], kT[:, h, :], identb[:]), reads=['kT', 'identb'], writes=psn(0))
                    for h in range(4):
                        P.op('pe', lambda e, h=h: e.transpose(ptb[:, 4 + h, :], vT[:, h, :], identb[:]), reads=['vT', 'identb'], writes=psn(0))
                    for h in range(4):
                        P.op('dve', lambda e, h=h: e.tensor_scalar_mul(kbg[:, h, :], ptb[:, h, :], sm[:, S_SC1 + h:S_SC1 + h + 1]),
                             reads=psn(0) + SM, writes=['kbg'])
                        P.op('dve', lambda e, h=h: e.tensor_scalar_mul(kdec[:, h, :], ptb[:, h, :], sm[:, S_SC2 + h:S_SC2 + h + 1]),
                             reads=psn(0) + SM, writes=['kdec'])
                        for s in range(2):
                            rows = slice(s * 64, (s + 1) * 64)
                            P.op('dve', lambda e, h=h, s=s, rows=rows: e.tensor_scalar_mul(
                                vbblk[rows, h, s * 128:(s + 1) * 128], ptb[rows, 4 + h, :], sm[rows, S_BETA + h:S_BETA + h + 1]),
                                 reads=psn(0) + SM, writes=['vbblk'])
                    pla = PS[:, 4:6, :].rearrange("p b (h n) -> p (b h) n", h=2)
                    for h in range(4):
                        P.op('pe', lambda e, h=h: e.matmul(pla[:, h, :], kT[:, h, :], kq[:, h, :, :].rearrange("p a n -> p (a n)"),
                                                           start=True, stop=True),
                             reads=['kT', 'kq_k', 'kq_q'], writes=psn(4, 5))
                    P.op('dve', lambda e: e.tensor_tensor(LA[:].rearrange("p h a n -> p h (a n)"), pla,
                                                          E2[:].rearrange("p h a n -> p h (a n)"), ALU.mult),
                         reads=psn(4, 5) + ['xt'], writes=['LA'])
                    ptn = bankb(1)
                    for h in range(4):
                        P.op('pe', lambda e, h=h: e.transpose(ptn[:, h, :], LA[:, h, 0, :], identb[:]), reads=['LA', 'identb'], writes=psn(1))
                    P.op('act', lambda e: e.copy(N0[:], ptn[:, 0:4, :]), reads=psn(1), writes=['N0'])
                    P.op('dve', lambda e: e.tensor_tensor(PTa[:], LA[:, :, 0, :], bcast(identb[:], 1, 4), ALU.add),
                         reads=['LA', 'identb'], writes=['PTa'])
                    Nprev = lambda h: N0[:, h, :]
                    NTprev = lambda h: LA[:, h, 0, :]
                    prevreg = ['N0', 'LA']
                    PTcur, PTnext = PTa, PTb
                    PTr = {id(PTa): 'PTa', id(PTb): 'PTb'}
                    NAs = [NAa, NAb]
                    NAr = ['NAa', 'NAb']
                    for l in range(1, 6):
                        psq = PS[:, 6:8, :].rearrange("p b (h n) -> p (b h) n", h=2)
                        for h in range(4):
                            P.op('pe', lambda e, h=h, Np=Nprev, NTp=NTprev: e.matmul(psq[:, h, 0:128], NTp(h), Np(h), start=True, stop=True),
                                 reads=prevreg, writes=psn(6, 7))
                            if l < 5:
                                P.op('pe', lambda e, h=h, Np=Nprev, NTp=NTprev: e.matmul(psq[:, h, 128:256], Np(h), NTp(h), start=True, stop=True),
                                     reads=prevreg, writes=psn(6, 7))
                        NAn = NAs[l % 2]
                        NAnr = NAr[l % 2]
                        if l < 5:
                            P.op('act', lambda e, NAn=NAn: e.copy(NAn[:].rearrange("p h a n -> p h (a n)"), psq), reads=psn(6, 7), writes=[NAnr])
                        else:
                            P.op('act', lambda e, NAn=NAn: e.copy(NAn[:, :, 0, :], psq[:, :, 0:128]), reads=psn(6, 7), writes=[NAnr])
                        Nprev = (lambda NAn: (lambda h: NAn[:, h, 0, :]))(NAn)
                        NTprev = (lambda NAn: (lambda h: NAn[:, h, 1, :]))(NAn)
                        prevreg = [NAnr]
                        ppt = bank4(3)
                        for h in range(4):
                            P.op('pe', lambda e, h=h, PTc=PTcur: e.matmul(ppt[:, h, :], identb[:], PTc[:, h, :], start=True, stop=False),
                                 reads=[PTr[id(PTcur)], 'identb'], writes=psn(3))
                            P.op('pe', lambda e, h=h, PTc=PTcur, Np=Nprev: e.matmul(ppt[:, h, :], Np(h), PTc[:, h, :], start=False, stop=True),
                                 reads=[PTr[id(PTcur)], NAnr], writes=psn(3))
                        P.op('dve', lambda e, PTn=PTnext: e.tensor_copy(PTn[:], ppt), reads=psn(3), writes=[PTr[id(PTnext)]])
                        PTcur, PTnext = PTnext, PTcur
                    TT = PTcur
                    TTr = PTr[id(TT)]
                    pw = bank4(0)
                    for h in range(4):
                        P.op('pe', lambda e, h=h, TT=TT: e.matmul(pw[:, h, :], kbg[:, h, :], TT[:, h, :], start=True, stop=True),
                             reads=['kbg', TTr], writes=psn(0))
                    P.op('act', lambda e: e.mul(negwT[:], pw, -1.0), reads=psn(0), writes=['negwT'])
                    pvn = PS[:, 4:6, :].rearrange("p b (h n) -> p (b h) n", h=2)
                    for h in range(4):
                        P.op('pe', lambda e, h=h, TT=TT: e.matmul(pvn[:, h, :], TT[:, h, :], vbblk[:, h, :], start=True, stop=False),
                             reads=['vbblk', TTr], writes=psn(4, 5))
                        P.op('pe', lambda e, h=h: e.matmul(pvn[:, h, :], negwT[:, h, :], Sbf[:, h, :], start=False, stop=True),
                             reads=['negwT', 'Sbf'], writes=psn(4, 5))
                    P.op('dve', lambda e: e.tensor_copy(vnblk[0:64, :, 0:128], pvn[0:64, :, 0:128]), reads=psn(4, 5), writes=['vnblk'])
                    P.op('act', lambda e: e.copy(vnblk[64:128, :, 128:256], pvn[64:128, :, 128:256]), reads=psn(4, 5), writes=['vnblk'])
                    po = PS[:, 6:8, :].rearrange("p b (h n) -> p (b h) n", h=2)
                    for h in range(4):
                        P.op('pe', lambda e, h=h: e.matmul(po[:, h, :], qg[:, h, :], Sbf[:, h, :], start=True, stop=False),
                             reads=['qg', 'Sbf'], writes=psn(6, 7))
                        P.op('pe', lambda e, h=h: e.matmul(po[:, h, :], LA[:, h, 1, :], vnblk[:, h, :], start=False, stop=True),
                             reads=['LA', 'vnblk'], writes=psn(6, 7))
                    psu = PS[:, 0:2, :].rearrange("p b (h n) -> p (b h) n", h=2)
                    for h in range(4):
                        P.op('pe', lambda e, h=h: e.matmul(psu[:, h, :], kdec[:, h, :], vnblk[:, h, :], start=True, stop=True),
                             reads=['kdec', 'vnblk'], writes=psn(0, 1))
                    for h in range(4):
                        for s in range(2):
                            col = s * 64 + 63
                            P.op('dve', lambda e, h=h, s=s, col=col: e.scalar_tensor_tensor(
                                S32[:, h, s * 128:(s + 1) * 128], S32[:, h, s * 128:(s + 1) * 128], egrow[:, h, col:col + 1],
                                psu[:, h, s * 128:(s + 1) * 128], ALU.mult, ALU.add),
                                 reads=['S32', 'egrow'] + psn(0, 1), writes=['S32'])
                    P.op('act', lambda e: e.copy(Sbf[:], S32[:]), reads=['S32'], writes=['Sbf'])
                    P.op('dve', lambda e: e.tensor_copy(otok[0:64, :, :], po[0:64, :, 0:128]), reads=psn(6, 7), writes=['qks'])
                    P.op('act', lambda e: e.copy(otok[64:128, :, :], po[64:128, :, 128:256]), reads=psn(6, 7), writes=['qks'])
                    marks['tail'] = len(P.ops)
                    oss = sm[:, S_OSS:S_OSS + 4]
                    ors = sm[:, S_ORS:S_ORS + 4]
                    P.op('pool', lambda e: e.tensor_tensor(osq[:], otok[:], otok[:], ALU.mult), reads=['qks'], writes=['dexp'])
                    P.op('dve', lambda e: e.tensor_reduce(oss, osq[:], AX.X, ALU.add), reads=['dexp'], writes=['sm_o'])
                    P.op('act', lambda e: e.activation(ors, oss, AF.Ln, bias=EPS, scale=1.0 / 128), reads=['sm_o'], writes=['sm_o'])
                    P.op('act', lambda e: e.activation(ors, ors, AF.Exp, scale=-0.5), reads=['sm_o'], writes=['sm_o'])
                    P.op('pool', lambda e: e.tensor_tensor(osq[:], zt[:], bcast(prm[:, P_GO:P_GO + 128], 1, 4), ALU.mult),
                         reads=['zt', 'prm', 'sm_o'], writes=['dexp'])
                    for h in range(4):
                        P.op('dve', lambda e, h=h: e.scalar_tensor_tensor(ydn[:, h, :], otok[:, h, :], sm[:, S_ORS + h:S_ORS + h + 1],
                                                                          osq[:, h, :], ALU.mult, ALU.mult),
                             reads=['qks', 'sm_o', 'dexp'], writes=['ydn'])
                    pyt = bankb(2)
                    for h in range(4):
                        P.op('pe', lambda e, h=h: e.transpose(pyt[:, h, :], ydn[:, h, :], identb[:]), reads=['ydn', 'identb'], writes=psn(2))
                    P.op('act', lambda e: e.copy(catT[:, 4:8, :], pyt[:, 0:4, :]), reads=psn(2), writes=['catT_d'])
                    pp = bank4(3)
                    for g in range(4):
                        P.op('pe', lambda e, g=g: e.matmul(pp[:, g, :], WA[:, WPOOL0 + g * 128:WPOOL0 + (g + 1) * 128],
                                                           pooled[:, g, :, :].rearrange("p s t -> p (s t)"), start=True, stop=True),
                             reads=['pooled'] + wa_regs(WPOOL0, WPOOL0 + 512), writes=psn(3))
                    for g in range(4):
                        P.op('dve', lambda e, g=g: e.tensor_scalar_mul(catT[:, g, :], pp[:, g, :], prm[:, P_PSC + g:P_PSC + g + 1]),
                             reads=psn(3) + ['prm'], writes=['catT_p'])
                    pout = bank8(4)
                    for j in range(8):
                        for k in range(8):
                            P.op('pe', lambda e, j=j, k=k: e.matmul(pout[:, j, :], W1024(WOUT0, k, j * 128, (j + 1) * 128), catT[:, k, :],
                                                                    start=(k == 0), stop=(k == 7)),
                                 reads=['catT_d', 'catT_p'] + W1024_regs(WOUT0, k, j * 128, (j + 1) * 128), writes=psn(4, 5))
                    P.op('dve', lambda e, hview=hview: e.tensor_tensor(hview, pout.rearrange("p k (s t) -> p k s t", s=2), hview, ALU.add),
                         reads=psn(4, 5) + hregs, writes=hregs)
                    tiles.append((P.ops, marks))
                    P.ops = saved_ops
                    if stop_after == ('p1tile', cg):
                        break
                for ti, (L, mk) in enumerate(tiles):
                    A = L[mk['sa0']:mk['sa1']]
                    if ti == 0:
                        P.ops.extend(L[:mk['head_end']])
                    P.ops.extend(L[mk['head_end']:mk['silu']])
                    X = L[mk['silu']:mk['sa0']]
                    Y = A + L[mk['sa1']:mk['brow']]
                    nx, ny = len(X), len(Y)
                    i = j = 0
                    while i < nx or j < ny:
                        if j < ny and (i >= nx or j * nx <= i * ny):
                            P.ops.append(Y[j])
                            j += 1
                        else:
                            P.ops.append(X[i])
                            i += 1
                    P.ops.extend(L[mk['brow']:mk['tail']])
                    if ti + 1 < len(tiles):
                        L2, mk2 = tiles[ti + 1]
                        P.ops.extend(L2[:mk2['head_end']])
                    P.ops.extend(L[mk['tail']:])
                if stop_after is not None and stop_after[0] == 'p1tile':
                    dump('hT', hT[:, :, :, 0:(stop_after[1] % CPH + 1) * 64], [f"h{s}_{c}" for s in range(2) for c in range(CPH)])
                    dump('S32', S32[:], ['S32'])
                    dump('catT', catT[:], ['catT_d', 'catT_p'])
                    dump('otok', otok, ['qks'])
                    dump('kq', kq[:], ['kq_k', 'kq_q'])
                    dump('LA', LA[:], ['LA'])
                    dump('TT', TT[:], [TTr])
                    dump('sm', sm[:], ['sm', 'sm_o', 'sm_ba'])
                    dump('kT', kT[:], ['kT'])
                    dump('vT', vT[:], ['vT'])
                    dump('vnblk', vnblk[:], ['vnblk'])
                    dump('pooled', pooled[:], ['pooled'])
                phase_barrier('p1')
        def phase1_dump(half):
            if stop_after[0] == 'p1':
                dump('hT', hT[:], [f"h{s}_{c}" for s in range(2) for c in range(CPH)])

        def phase2(half):
            WXA, WXB = 0, 8192
            load_w1024(wxk_d, WXA, 'g_wxa')
            load_w1024(wxv_d, WXB, 'g_wxb')
            with ExitStack() as p2:
                KT = sbuf(p2, "KT", [128, 2, 8, 256], BF16)
                Vm = sbuf(p2, "Vm", [128, 2, 2, 1024], BF16)
                with ExitStack() as p2a:
                    memt = sbuf(p2a, "memt", [128, D], F32)
                    mjunk = sbuf(p2a, "mjunk", [128, D], F32)
                    mn = sbuf(p2a, "mn", [128, D], BF16)
                    mT = sbuf(p2a, "mT", [128, 8, 2, 256], BF16)
                    msm = sbuf(p2a, "msm", [128, 4], F32)
                    for s in range(2):
                        for mt in range(2):
                            P.dma('sp', lambda e, s=s, mt=mt: e.dma_start(out=memt[:], in_=mem_d[s, mt * 128:(mt + 1) * 128, :]),
                                  writes=['memt'], group='g_mem')
                            P.op('act', lambda e: e.activation(mjunk[:], memt[:], AF.Square), reads=['memt'], writes=['mjunk'])
                            P.op('dve', lambda e: e.tensor_reduce(msm[:, 0:1], mjunk[:], AX.X, ALU.add), reads=['mjunk'], writes=['msm'])
                            P.op('act', lambda e: e.activation(msm[:, 1:2], msm[:, 0:1], AF.Ln, bias=EPS, scale=1.0 / D), reads=['msm'], writes=['msm'])
                            P.op('act', lambda e: e.activation(msm[:, 1:2], msm[:, 1:2], AF.Exp, scale=-0.5), reads=['msm'], writes=['msm'])
                            P.op('dve', lambda e: e.tensor_scalar_mul(mn[:], memt[:], msm[:, 1:2]), reads=['memt', 'msm'], writes=['mn'])
                            pmt = bankb(0)
                            for k in range(8):
                                P.op('pe', lambda e, k=k: e.transpose(pmt[:, k, :], mn[:, k * 128:(k + 1) * 128], identb[:]),
                                     reads=['mn', 'identb'], writes=psn(0))
                            for k in range(8):
                                P.op('dve', lambda e, k=k, s=s, mt=mt: e.tensor_scalar_mul(mT[:, k, s, mt * 128:(mt + 1) * 128], pmt[:, k, :],
                                                                                           prm[:, P_GMEM + k:P_GMEM + k + 1]),
                                     reads=psn(0) + ['prm'], writes=['mT'])
                    nb = 0
                    for s in range(2):
                        for dj in range(8):
                            b = 1 + (nb % 4)
                            nb += 1
                            pk = bank(b)[:, 0:256]
                            for k in range(8):
                                P.op('pe', lambda e, k=k, s=s, dj=dj, pk=pk: e.matmul(pk, W1024(WXA, k, dj * 128, (dj + 1) * 128), mT[:, k, s, :],
                                                                                      start=(k == 0), stop=(k == 7)),
                                     reads=['mT'] + W1024_regs(WXA, k, dj * 128, (dj + 1) * 128), writes=psn(b))
                            if nb % 2:
                                P.op('act', lambda e, s=s, dj=dj, pk=pk: e.copy(KT[:, s, dj, :], pk), reads=psn(b), writes=['KT'])
                            else:
                                P.op('dve', lambda e, s=s, dj=dj, pk=pk: e.tensor_copy(KT[:, s, dj, :], pk), reads=psn(b), writes=['KT'])
                    for s in range(2):
                        for mt in range(2):
                            for hf in range(2):
                                b = 1 + (nb % 4)
                                nb += 1
                                pv = bank(b)
                                for k in range(8):
                                    P.op('pe', lambda e, k=k, s=s, mt=mt, hf=hf, pv=pv: e.matmul(
                                        pv, mT[:, k, s, mt * 128:(mt + 1) * 128], W1024(WXB, k, hf * 512, (hf + 1) * 512),
                                        start=(k == 0), stop=(k == 7)),
                                         reads=['mT'] + W1024_regs(WXB, k, hf * 512, (hf + 1) * 512), writes=psn(b))
                                if nb % 2:
                                    P.op('act', lambda e, s=s, mt=mt, hf=hf, pv=pv: e.copy(Vm[:, s, mt, hf * 512:(hf + 1) * 512], pv),
                                         reads=psn(b), writes=['Vm'])
                                else:
                                    P.op('dve', lambda e, s=s, mt=mt, hf=hf, pv=pv: e.tensor_copy(Vm[:, s, mt, hf * 512:(hf + 1) * 512], pv),
                                         reads=psn(b), writes=['Vm'])
                    phase_barrier('p2a')
                load_w1024(wxq_d, WXA, 'g_wxa')
                load_w1024(wxo_d, WXB, 'g_wxb')
                TQ = 256
                sq2 = sbuf(p2, "sq2", [128, 8, TQ], BF16)
                rstd2 = sbuf(p2, "rstd2", [128, TQ], F32)
                hn2 = sbuf(p2, "hn2", [128, 8, TQ], BF16)
                qT2 = sbuf(p2, "qT2", [128, 8, TQ], BF16)
                oT2 = sbuf(p2, "oT2", [128, 8, TQ], BF16)
                Eb = sbuf(p2, "Eb", [128, 4, 256], F32)
                Pm = sbuf(p2, "Pm", [128, 4, 256], BF16)
                PTt = sbuf(p2, "PTt", [128, 8, 128], BF16)
                ssm = sbuf(p2, "ssm", [128, 16], F32)
                for s in range(2):
                    for qi in range(TPH // TQ):
                        q0 = qi * TQ
                        hr = hreg(s, q0, q0 + TQ)
                        hv = hT[:, :, s, q0:q0 + TQ]
                        P.op('act', lambda e, hv=hv: e.activation(sq2[:], hv, AF.Square), reads=hr, writes=['sq2'])
                        pst = bank(0)[:, 0:TQ]
                        for k in range(8):
                            P.op('pe', lambda e, k=k, pst=pst: e.matmul(pst, onesb[:], sq2[:, k, :], start=(k == 0), stop=(k == 7)),
                                 reads=['sq2', 'onesb'], writes=psn(0))
                        P.op('act', lambda e, pst=pst: e.activation(rstd2[:], pst, AF.Ln, bias=EPS, scale=1.0 / D), reads=psn(0), writes=['rstd2'])
                        P.op('act', lambda e: e.activation(rstd2[:], rstd2[:], AF.Exp, scale=-0.5), reads=['rstd2'], writes=['rstd2'])
                        for k in range(8):
                            P.op('dve', lambda e, k=k, hv=hv: e.scalar_tensor_tensor(hn2[:, k, :], hv[:, k, :], prm[:, P_GXAT + k:P_GXAT + k + 1],
                                                                                     rstd2[:], ALU.mult, ALU.mult),
                                 reads=hr + ['prm', 'rstd2'], writes=['hn2'])
                        pq = PS[:, 0:4, :].rearrange("p b (j n) -> p (b j) n", j=2)
                        for dj in range(8):
                            for k in range(8):
                                P.op('pe', lambda e, dj=dj, k=k: e.matmul(pq[:, dj, :], W1024(WXA, k, dj * 128, (dj + 1) * 128), hn2[:, k, :],
                                                                          start=(k == 0), stop=(k == 7)),
                                     reads=['hn2'] + W1024_regs(WXA, k, dj * 128, (dj + 1) * 128), writes=psn(dj // 2))
                        P.op('act', lambda e: e.mul(qT2[:, 0:4, :], pq[:, 0:4, :], 1.0 / 16), reads=psn(0, 1), writes=['qT2a'])
                        P.op('dve', lambda e: e.tensor_scalar_mul(qT2[:, 4:8, :], pq[:, 4:8, :], 1.0 / 16), reads=psn(2, 3), writes=['qT2b'])
                        for sub in range(TQ // 128):
                            tsl = slice(sub * 128, (sub + 1) * 128)
                            psc = PS[:, 4:6, :].rearrange("p b (h n) -> p (b h) n", h=2)
                            for h in range(4):
                                for j in range(2):
                                    P.op('pe', lambda e, h=h, j=j, tsl=tsl, s=s: e.matmul(psc[:, h, :], qT2[:, 2 * h + j, tsl], KT[:, s, 2 * h + j, :],
                                                                                     start=(j == 0), stop=(j == 1)),
                                         reads=['qT2a', 'qT2b', 'KT'], writes=psn(4, 5))
                            P.op('dve', lambda e: e.tensor_reduce(ssm[:, 0:4], psc, AX.X, ALU.max, negate=True), reads=psn(4, 5), writes=['ssm'])
                            P.op('dve', lambda e: e.tensor_tensor(Eb[:], psc, bcast(ssm[:, 0:4], 2, 256), ALU.add), reads=psn(4, 5) + ['ssm'], writes=['Eb'])
                            P.op('act', lambda e: e.activation(Eb[:], Eb[:], AF.Exp), reads=['Eb'], writes=['Eb'])
                            P.op('dve', lambda e: e.tensor_reduce(ssm[:, 4:8], Eb[:], AX.X, ALU.add), reads=['Eb'], writes=['ssm2'])
                            P.op('dve', lambda e: e.reciprocal(ssm[:, 8:12], ssm[:, 4:8]), reads=['ssm2'], writes=['ssm3'])
                            P.op('dve', lambda e: e.tensor_tensor(Pm[:], Eb[:], bcast(ssm[:, 8:12], 2, 256), ALU.mult), reads=['Eb', 'ssm3'], writes=['Pm'])
                            ptp = bankb(6)
                            for h in range(4):
                                for mt in range(2):
                                    P.op('pe', lambda e, h=h, mt=mt: e.transpose(ptp[:, h * 2 + mt, :], Pm[:, h, mt * 128:(mt + 1) * 128], identb[:]),
                                         reads=['Pm', 'identb'], writes=psn(6))
                            P.op('act', lambda e: e.copy(PTt[:], ptp), reads=psn(6), writes=['PTt'])
                            pov = bank8(0) if sub == 0 else bank8(2)
                            pr = psn(0, 1) if sub == 0 else psn(2, 3)
                            for h in range(4):
                                for j in range(2):
                                    for mt in range(2):
                                        c0 = h * 256 + j * 128
                                        P.op('pe', lambda e, h=h, j=j, mt=mt, c0=c0, pov=pov, s=s: e.matmul(
                                            pov[:, 2 * h + j, :], Vm[:, s, mt, c0:c0 + 128], PTt[:, h * 2 + mt, :], start=(mt == 0), stop=(mt == 1)),
                                             reads=['Vm', 'PTt'], writes=pr)
                            P.op('dve', lambda e, tsl=tsl, pov=pov: e.tensor_copy(oT2[:, :, tsl], pov), reads=pr, writes=[f'oT2_{sub}'])
                        pxo = PS[:, 4:8, :].rearrange("p b (j n) -> p (b j) n", j=2)
                        for dj in range(8):
                            for k in range(8):
                                P.op('pe', lambda e, dj=dj, k=k: e.matmul(pxo[:, dj, :], W1024(WXB, k, dj * 128, (dj + 1) * 128), oT2[:, k, :],
                                                                          start=(k == 0), stop=(k == 7)),
                                     reads=['oT2_0', 'oT2_1'] + W1024_regs(WXB, k, dj * 128, (dj + 1) * 128), writes=psn(4 + dj // 2))
                        P.op('dve', lambda e, hv=hv: e.tensor_tensor(hv, pxo, hv, ALU.add), reads=psn(4, 5, 6, 7) + hr, writes=hr)
                phase_barrier('p2')
        def phase2_dump(half):
            dump('hT', hT[:], [f"h{s}_{c}" for s in range(2) for c in range(CPH)])

        def phase3(half):
            with ExitStack() as p3:
                hn3 = sbuf(p3, "hn3", [128, 8, 2, TPH], BF16)
                sq3 = sbuf(p3, "sq3", [128, 8, 256], BF16)
                rstd3 = sbuf(p3, "rstd3", [128, 256], F32)
                rl = sbuf(p3, "rl", [128, 512], F32)
                aT = [sbuf(p3, f"aT{i}", [128, 4, 512], BF16) for i in range(2)]
                of = sbuf(p3, "of", [128, 8, 128], F32)
                osb = [sbuf(p3, f"osb{i}", [128, D], F32) for i in range(2)]

                def norm3(s, q0, n, gbase, outk, outregs, sqb, rsb):
                    hr = hreg(s, q0, q0 + n)
                    hv = hT[:, :, s, q0:q0 + n]
                    P.op('act', lambda e: e.activation(sqb[:, :, 0:n], hv, AF.Square), reads=hr, writes=['sq3'])
                    pst = bank(0)[:, 0:n]
                    for k in range(8):
                        P.op('pe', lambda e, k=k: e.matmul(pst, onesb[:], sqb[:, k, 0:n], start=(k == 0), stop=(k == 7)),
                             reads=['sq3', 'onesb'], writes=psn(0))
                    P.op('act', lambda e: e.activation(rsb[:, 0:n], pst, AF.Ln, bias=EPS, scale=1.0 / D), reads=psn(0), writes=['rstd3'])
                    P.op('act', lambda e: e.activation(rsb[:, 0:n], rsb[:, 0:n], AF.Exp, scale=-0.5), reads=['rstd3'], writes=['rstd3'])
                    for k in range(8):
                        P.op('dve', lambda e, k=k: e.scalar_tensor_tensor(outk(k), hv[:, k, :], prm[:, gbase + k:gbase + k + 1], rsb[:, 0:n],
                                                                          ALU.mult, ALU.mult),
                             reads=hr + ['prm', 'rstd3'], writes=outregs)

                for s in range(2):
                    for qi in range(TPH // 256):
                        q0 = qi * 256
                        norm3(s, q0, 256, P_GMLP, lambda k, s=s, q0=q0: hn3[:, k, s, q0:q0 + 256], [f'hn3_{s}_{qi // 2}'], sq3, rstd3)
                NFC = DFF // 512
                SL_UP = [0, 8192]
                SL_DN = [4096, 12288]
                wupv = wup_d.rearrange("(k p) n -> p k n", p=128)
                wdnv = wdn_d.rearrange("(j p) n -> p j n", p=128)

                def load_chunk(fc):
                    sl = fc % 2
                    for k in range(8):
                        lo = SL_UP[sl] + k * 512
                        P.dma('pool', lambda e, k=k, lo=lo, fc=fc: e.dma_start(out=WA[:, lo:lo + 512], in_=wupv[:, k, fc * 512:(fc + 1) * 512]),
                              writes=wa_regs(lo, lo + 512), group=f'g_up{sl}', tok=False)
                    for j in range(4):
                        lo = SL_DN[sl] + j * 1024
                        P.dma('pool', lambda e, j=j, lo=lo, fc=fc: e.dma_start(out=WA[:, lo:lo + 1024], in_=wdnv[:, fc * 4 + j, :]),
                              writes=wa_regs(lo, lo + 1024), group=f'g_dn{sl}', tok=False)

                load_chunk(0)
                nt = 0
                for fc in range(NFC):
                    if fc + 1 < NFC:
                        load_chunk(fc + 1)
                    sl = fc % 2
                    for s in range(2):
                        for qi in range(TPH // 512):
                            q0 = qi * 512
                            hr = hreg(s, q0, q0 + 512)
                            a = aT[nt % 2]
                            ar = f'aT{nt % 2}'
                            nt += 1
                            for fb in range(4):
                                for k in range(8):
                                    lo = SL_UP[sl] + k * 512 + fb * 128
                                    P.op('pe', lambda e, fb=fb, k=k, lo=lo, s=s, q0=q0: e.matmul(bank(fb), WA[:, lo:lo + 128], hn3[:, k, s, q0:q0 + 512],
                                                                                                 start=(k == 0), stop=(k == 7)),
                                         reads=[f'hn3_{s}_{qi}'] + wa_regs(lo, lo + 128), writes=psn(fb))
                                P.op('act', lambda e, fb=fb: e.activation(rl[:], bank(fb), AF.Relu), reads=psn(fb), writes=['rl'])
                                P.op('act', lambda e, fb=fb, a=a: e.activation(a[:, fb, :], rl[:], AF.Square), reads=['rl'], writes=[ar])
                            for dj in range(8):
                                b = 4 + dj % 4
                                for fb in range(4):
                                    lo = SL_DN[sl] + fb * 1024 + dj * 128
                                    P.op('pe', lambda e, fb=fb, dj=dj, lo=lo, a=a, b=b: e.matmul(bank(b), WA[:, lo:lo + 128], a[:, fb, :],
                                                                                                 start=(fb == 0), stop=(fb == 3)),
                                         reads=[ar] + wa_regs(lo, lo + 128), writes=psn(b))
                                hv = hT[:, dj, s, q0:q0 + 512]
                                P.op('dve', lambda e, hv=hv, b=b: e.tensor_tensor(hv, bank(b), hv, ALU.add), reads=psn(b) + hr, writes=hr)
                no = 0
                for s in range(2):
                    for ti in range(TPH // 128):
                        q0 = ti * 128
                        norm3(s, q0, 128, P_GFIN, lambda k: of[:, k, :], ['of'], sq3, rstd3)
                        pto = bank8(2)
                        for k in range(8):
                            P.op('pe', lambda e, k=k: e.transpose(pto[:, k, :], of[:, k, :], ident), reads=['of', 'cst'], writes=psn(2, 3))
                        ob = osb[no % 2]
                        obr = f'osb{no % 2}'
                        no += 1
                        P.op('act', lambda e, ob=ob: e.copy(ob[:].rearrange("p (k n) -> p k n", k=8), pto), reads=psn(2, 3), writes=[obr])
                        tg = half * TPH + q0
                        P.dma('sp', lambda e, ob=ob, s=s, tg=tg: e.dma_start(out=out_d[s, tg:tg + 128, :], in_=ob[:]),
                              reads=[obr], writes=['out'], group=f'g_{obr}')
                phase_barrier('p3')

        for half in range(NHALF):
            phase1(half)
            if stop_after is not None and stop_after[0] in ('p1tile', 'p1'):
                phase1_dump(half)
                break
            phase2(half)
            if stop_after is not None and stop_after[0] == 'p2':
                phase2_dump(half)
                break
            phase3(half)

        fg = [g for g in ('g_osb0', 'g_osb1') if stop_after is None]
        fg += [f'dbg{i + 1}' for i in range(dbg_n[0])]
        if max_ops is not None:
            P.ops = P.ops[:max_ops]
            fg = []
        st = P.finalize(final_groups=fg)
        build_nc.stats = st
    return nc


def make_consts():
    c = np.zeros((128, CW), np.float32)
    idx = np.arange(128)
    blk = idx // 64
    same = blk[:, None] == blk[None, :]
    c[:, C_ID:C_ID + 128] = np.eye(128)
    c[:, C_ONE:C_ONE + 128] = 1.0
    tri = same & (idx[:, None] <= idx[None, :])
    c[:, C_MTRI:C_MTRI + 128] = tri
    c[:, C_NMTRI:C_NMTRI + 128] = -tri.astype(np.float32)
    c[:, C_MBLK:C_MBLK + 128] = same
    c[:, C_MSN:C_MSN + 128] = -(same & (idx[None, :] > idx[:, None])).astype(np.float32)
    c[:, C_MI:C_MI + 128] = (same & (idx[None, :] >= idx[:, None]))
    for g, w in enumerate(WINDOWS):
        t = np.arange(16)
        c[:, C_RAT + g * 16:C_RAT + (g + 1) * 16] = (w / np.minimum(t + 1, w))[None, :]
    return c


def make_prm(inp):
    p = np.zeros((128, PW), np.float32)

    def colvec(v):
        return np.asarray(v, np.float32).reshape(8, 128).T

    p[:, P_GMIX:P_GMIX + 8] = colvec(inp['norm_mix_g'][0])
    p[:, P_GXAT:P_GXAT + 8] = colvec(inp['norm_xattn_g'][0])
    p[:, P_GMEM:P_GMEM + 8] = colvec(inp['mem_norm_g'][0])
    p[:, P_GMLP:P_GMLP + 8] = colvec(inp['norm_mlp_g'][0])
    p[:, P_GFIN:P_GFIN + 8] = colvec(inp['final_norm_g'])
    p[:, P_PSC:P_PSC + 4] = np.asarray(inp['pool_scale'][0], np.float32).reshape(4, 128).T
    cw = np.asarray(inp['conv_w'][0], np.float32)
    p[:, P_CONV:P_CONV + 48] = cw.reshape(4, 12, 128).transpose(2, 1, 0).reshape(128, 48)
    p[:, P_ALOG:P_ALOG + 4] = np.broadcast_to(np.asarray(inp['a_log'][0], np.float32)[None, :], (128, 4))
    p[:, P_DTB:P_DTB + 4] = np.broadcast_to(np.asarray(inp['dt_bias'][0], np.float32)[None, :], (128, 4))
    p[:, P_GO:P_GO + 128] = np.broadcast_to(np.asarray(inp['dn_out_norm_g'][0], np.float32)[None, :], (128, 128))
    return p


def make_in_maps(inp, n_cores=8):
    prm = make_prm(inp)
    cst = make_consts()
    shared = dict(
        w_in=np.ascontiguousarray(inp['w_in'][0], dtype=np.float32),
        w_pool=np.ascontiguousarray(inp['w_pool'][0], dtype=np.float32),
        w_out=np.ascontiguousarray(inp['w_out'][0], dtype=np.float32),
        w_xq=np.ascontiguousarray(inp['w_xq'][0], dtype=np.float32),
        w_xk=np.ascontiguousarray(inp['w_xk'][0], dtype=np.float32),
        w_xv=np.ascontiguousarray(inp['w_xv'][0], dtype=np.float32),
        w_xo=np.ascontiguousarray(inp['w_xo'][0], dtype=np.float32),
        w_up=np.ascontiguousarray(inp['w_up'][0], dtype=np.float32),
        w_down=np.ascontiguousarray(inp['w_down'][0], dtype=np.float32),
        prm=prm, cst=cst)
    x = np.asarray(inp['x'], np.float32)
    mem = np.asarray(inp['mem'], np.float32)
    maps = []
    for i in range(n_cores):
        m = dict(shared)
        m['x'] = np.ascontiguousarray(x[NSEQ * i:NSEQ * (i + 1)])
        m['mem'] = np.ascontiguousarray(mem[NSEQ * i:NSEQ * (i + 1)])
        maps.append(m)
    return maps


def kernel(**inputs):
    nc = build_nc()
    maps = make_in_maps(inputs, 8)
    res = run_bass_kernel_spmd(nc, maps, core_ids=list(range(8)))
    outs = [np.asarray(r['out'], np.float32) for r in res.results]
    return np.concatenate(outs, axis=0)
```
